# Optimizing a Trainium2 kernel written in Bass

```python
import math
import jax
import jax.numpy as jnp
from jax import lax
import numpy as np

D_MODEL = 2048
BATCH = 2
SEQ = 4096
DEPTH = 4

N_A_LAYERS = DEPTH // 2
N_B_LAYERS = DEPTH - N_A_LAYERS

SSM_WIDTH = D_MODEL // 2
SSM_GROUP = 16
SSM_GROUPS = SSM_WIDTH // SSM_GROUP
SSM_STATE = 64
DT_MIN = 0.001
DT_MAX = 0.1

N_HEADS = 16
HEAD_DIM = 128
N_KV_HEADS = 4
HEADS_PER_KV = N_HEADS // N_KV_HEADS
ATTN_WIDTH = N_HEADS * HEAD_DIM
KV_WIDTH = N_KV_HEADS * HEAD_DIM
N_BRANCH = 3
CMP_BLOCK = 32
CMP_STRIDE = 16
SEL_BLOCK = 64
N_SELECT = 16
WINDOW = 512
Q_BLOCK = 64
B_IN_WIDTH = ATTN_WIDTH * (1 + N_BRANCH) + N_BRANCH * N_HEADS

LN_EPS = 1e-5
DEEPNORM_ALPHA = (2 * DEPTH) ** 0.25
DEEPNORM_BETA = (8 * DEPTH) ** -0.25
MASK_VALUE = -1e30
FORCE_SCORE = 1e6

kernel_name = "yoco_s5_nsa_deepnorm_trunk"


def layer_norm(x, g, b):
    x32 = x.astype(jnp.float32)
    mu = jnp.mean(x32, axis=-1, keepdims=True)
    var = jnp.mean(jnp.square(x32 - mu), axis=-1, keepdims=True)
    return ((x32 - mu) * lax.rsqrt(var + LN_EPS) * g.astype(jnp.float32) + b.astype(jnp.float32)).astype(x.dtype)


def _complex_linear_combine(e1, e2):
    a1r, a1i, b1r, b1i = e1
    a2r, a2i, b2r, b2i = e2
    return (a2r * a1r - a2i * a1i,
            a2r * a1i + a2i * a1r,
            a2r * b1r - a2i * b1i + b2r,
            a2r * b1i + a2i * b1r + b2i)


def s5_mixer(x, w_in, lam_re, lam_im, log_dt, b_re, b_im, c_re, c_im, d_skip, w_glu, b_glu, w_out):
    bsz, seq, _ = x.shape
    f32 = jnp.float32
    u, z = jnp.split(x @ w_in, 2, axis=-1)
    lr = lam_re.astype(f32)
    li = lam_im.astype(f32)
    dt = jnp.exp(log_dt.astype(f32))[:, None]
    mag = jnp.exp(lr * dt)
    ar = mag * jnp.cos(li * dt)
    ai = mag * jnp.sin(li * dt)
    inv_abs2 = 1.0 / (lr * lr + li * li)
    cr = ((ar - 1.0) * lr + ai * li) * inv_abs2
    ci = (ai * lr - (ar - 1.0) * li) * inv_abs2
    br = b_re.astype(f32)
    bi = b_im.astype(f32)
    bbar_r = cr[..., None] * br - ci[..., None] * bi
    bbar_i = cr[..., None] * bi + ci[..., None] * br
    ug = jnp.swapaxes(u, 0, 1).astype(f32).reshape(seq, bsz, SSM_GROUPS, SSM_GROUP)
    bu_r = jnp.einsum('lbgc,gpc->lbgp', ug, bbar_r)
    bu_i = jnp.einsum('lbgc,gpc->lbgp', ug, bbar_i)
    a_shape = (seq, 1, SSM_GROUPS, SSM_STATE)
    _, _, h_r, h_i = lax.associative_scan(
        _complex_linear_combine,
        (jnp.broadcast_to(ar, a_shape), jnp.broadcast_to(ai, a_shape), bu_r, bu_i),
        axis=0)
    y = (jnp.einsum('gcp,lbgp->lbgc', c_re.astype(f32), h_r)
         - jnp.einsum('gcp,lbgp->lbgc', c_im.astype(f32), h_i))
    y = jnp.swapaxes(y.reshape(seq, bsz, SSM_WIDTH), 0, 1).astype(x.dtype) + d_skip * u
    g = jax.nn.gelu(y)
    y = g * jax.nn.sigmoid(g @ w_glu + b_glu)
    return (y * jax.nn.silu(z)) @ w_out


def shared_kv(h, kv_w, pos_k, w1_k, w2_k, pos_v, w1_v, w2_v):
    bsz, seq, _ = h.shape
    kv = (h @ kv_w).reshape(bsz, seq, 2 * N_BRANCH, N_KV_HEADS, HEAD_DIM)
    k_c, v_c, k_s, v_s, k_w, v_w = [kv[:, :, i] for i in range(2 * N_BRANCH)]
    n_cmp = (seq - CMP_BLOCK) // CMP_STRIDE + 1
    idx = jnp.arange(n_cmp)[:, None] * CMP_STRIDE + jnp.arange(CMP_BLOCK)[None, :]

    def compress(t, pos, w1, w2):
        blk = t[:, idx] + pos[None, None, :, None, :]
        blk = jnp.moveaxis(blk, 3, 2).reshape(bsz, n_cmp, N_KV_HEADS, CMP_BLOCK * HEAD_DIM)
        return jax.nn.gelu(blk @ w1) @ w2

    return (compress(k_c, pos_k, w1_k, w2_k), compress(v_c, pos_v, w1_v, w2_v), k_s, v_s, k_w, v_w)


def masked_softmax(s, valid):
    s = jnp.where(valid, s, MASK_VALUE)
    m = jnp.max(s, axis=-1, keepdims=True)
    e = jnp.where(valid, jnp.exp(s - m), 0.0)
    return e / jnp.maximum(jnp.sum(e, axis=-1, keepdims=True), 1e-30)


def nsa_attention(q, k_cmp, v_cmp, k_slc, v_slc, k_win, v_win):
    bsz, seq = q.shape[:2]
    dtype = q.dtype
    f32 = jnp.float32
    n_q = seq // Q_BLOCK
    n_cmp = k_cmp.shape[1]
    n_sblk = seq // SEL_BLOCK
    n_sel = min(N_SELECT, n_sblk)
    scale = HEAD_DIM ** -0.5
    cmp_start = jnp.arange(n_cmp) * CMP_STRIDE
    cmp_end = cmp_start + CMP_BLOCK - 1
    sel_start = jnp.arange(n_sblk) * SEL_BLOCK
    overlap = ((cmp_start[:, None] < sel_start[None, :] + SEL_BLOCK)
               & (cmp_end[:, None] >= sel_start[None, :])).astype(f32)
    kb_s = k_slc.reshape(bsz, n_sblk, SEL_BLOCK, N_KV_HEADS, HEAD_DIM).transpose(0, 3, 1, 2, 4)
    vb_s = v_slc.reshape(bsz, n_sblk, SEL_BLOCK, N_KV_HEADS, HEAD_DIM).transpose(0, 3, 1, 2, 4)
    k_w_pad = jnp.pad(k_win, ((0, 0), (WINDOW, 0), (0, 0), (0, 0)))
    v_w_pad = jnp.pad(v_win, ((0, 0), (WINDOW, 0), (0, 0), (0, 0)))
    qg = q.reshape(bsz, n_q, Q_BLOCK, N_KV_HEADS, HEADS_PER_KV, HEAD_DIM).transpose(1, 0, 2, 3, 4, 5)
    gather_blocks = jax.vmap(jax.vmap(lambda blocks, ids: blocks[ids]))
    blk_ids = jnp.arange(n_sblk)

    def block_fn(args):
        i, qb = args
        t = i * Q_BLOCK + jnp.arange(Q_BLOCK)
        qb = qb * scale
        s = jnp.einsum('bqghd,bngd->bghqn', qb, k_cmp).astype(f32)
        p_cmp = masked_softmax(s, cmp_end[None, :] <= t[:, None])
        o_cmp = jnp.einsum('bghqn,bngd->bqghd', p_cmp.astype(dtype), v_cmp)
        imp = jnp.einsum('bghqn,nj->bgqj', p_cmp, overlap)
        cur = t // SEL_BLOCK
        future = blk_ids[None, :] > cur[:, None]
        forced = ((blk_ids[None, :] == 0) | (blk_ids[None, :] == cur[:, None])
                  | (blk_ids[None, :] == cur[:, None] - 1))
        imp = jnp.where(future, MASK_VALUE, jnp.where(forced, FORCE_SCORE, imp))
        _, sel = lax.top_k(imp, n_sel)
        ks = gather_blocks(kb_s, sel).reshape(bsz, N_KV_HEADS, Q_BLOCK, n_sel * SEL_BLOCK, HEAD_DIM)
        vs = gather_blocks(vb_s, sel).reshape(bsz, N_KV_HEADS, Q_BLOCK, n_sel * SEL_BLOCK, HEAD_DIM)
        pos = (sel[..., None] * SEL_BLOCK + jnp.arange(SEL_BLOCK)).reshape(bsz, N_KV_HEADS, Q_BLOCK, n_sel * SEL_BLOCK)
        valid = (pos <= t[None, None, :, None])[:, :, None]
        s = jnp.einsum('bqghd,bgqkd->bghqk', qb, ks).astype(f32)
        p = masked_softmax(s, valid)
        o_slc = jnp.einsum('bghqk,bgqkd->bqghd', p.astype(dtype), vs)
        kw = lax.dynamic_slice_in_dim(k_w_pad, i * Q_BLOCK, WINDOW + Q_BLOCK, axis=1)
        vw = lax.dynamic_slice_in_dim(v_w_pad, i * Q_BLOCK, WINDOW + Q_BLOCK, axis=1)
        kpos = i * Q_BLOCK - WINDOW + jnp.arange(WINDOW + Q_BLOCK)
        valid = ((kpos[None, :] <= t[:, None]) & (kpos[None, :] > t[:, None] - WINDOW)
                 & (kpos[None, :] >= 0))
        s = jnp.einsum('bqghd,bkgd->bghqk', qb, kw).astype(f32)
        p = masked_softmax(s, valid)
        o_win = jnp.einsum('bghqk,bkgd->bqghd', p.astype(dtype), vw)
        return o_cmp, o_slc, o_win

    o_cmp, o_slc, o_win = lax.map(block_fn, (jnp.arange(n_q), qg))

    def unblock(o):
        return o.transpose(1, 0, 2, 3, 4, 5).reshape(bsz, seq, N_HEADS, HEAD_DIM)

    return (unblock(o_cmp), unblock(o_slc), unblock(o_win))


def nsa_mixer(x, w_in, w_out, shared):
    bsz, seq, _ = x.shape
    proj = x @ w_in
    q = proj[..., :ATTN_WIDTH].reshape(bsz, seq, N_HEADS, HEAD_DIM)
    z = proj[..., ATTN_WIDTH:ATTN_WIDTH * (1 + N_BRANCH)].reshape(bsz, seq, N_BRANCH, N_HEADS, HEAD_DIM)
    gate = jax.nn.sigmoid(proj[..., ATTN_WIDTH * (1 + N_BRANCH):].reshape(bsz, seq, N_BRANCH, N_HEADS))
    o = jnp.stack(nsa_attention(q, *shared), axis=2)
    y = jnp.sum(gate[..., None] * o * jax.nn.silu(z), axis=2).reshape(bsz, seq, ATTN_WIDTH)
    return y @ w_out


def setup_inputs(seed: int = 0) -> dict:
    key = jax.random.key(seed)
    ks = jax.random.split(key, 24)
    nrm = jax.random.normal
    f32 = jnp.float32
    na, nb = N_A_LAYERS, N_B_LAYERS
    g, p, c, e = SSM_GROUPS, SSM_STATE, SSM_GROUP, SSM_WIDTH
    lam_im0 = math.pi * jnp.arange(p, dtype=f32)
    return {
        "x": nrm(ks[0], (BATCH, SEQ, D_MODEL), f32),
        "a_w_in": nrm(ks[1], (na, D_MODEL, 2 * e), f32) * D_MODEL ** -0.5,
        "a_lam_re": -0.5 + 0.01 * nrm(ks[2], (na, g, p), f32),
        "a_lam_im": lam_im0 + 0.01 * nrm(ks[3], (na, g, p), f32),
        "a_log_dt": jax.random.uniform(ks[4], (na, g), f32, math.log(DT_MIN), math.log(DT_MAX)),
        "a_b_re": nrm(ks[5], (na, g, p, c), f32) * (2 * c) ** -0.5,
        "a_b_im": nrm(ks[6], (na, g, p, c), f32) * (2 * c) ** -0.5,
        "a_c_re": nrm(ks[7], (na, g, c, p), f32) * p ** -0.5,
        "a_c_im": nrm(ks[8], (na, g, c, p), f32) * p ** -0.5,
        "a_d": nrm(ks[9], (na, e), f32),
        "a_w_glu": nrm(ks[10], (na, e, e), f32) * e ** -0.5,
        "a_b_glu": 0.01 * nrm(ks[11], (na, e), f32),
        "a_w_out": nrm(ks[12], (na, e, D_MODEL), f32) * e ** -0.5 * DEEPNORM_BETA,
        "kv_w": nrm(ks[13], (D_MODEL, 2 * N_BRANCH * KV_WIDTH), f32) * D_MODEL ** -0.5,
        "cmp_pos_k": 0.02 * nrm(ks[14], (CMP_BLOCK, HEAD_DIM), f32),
        "cmp_w1_k": nrm(ks[15], (CMP_BLOCK * HEAD_DIM, HEAD_DIM), f32) * (CMP_BLOCK * HEAD_DIM) ** -0.5,
        "cmp_w2_k": nrm(ks[16], (HEAD_DIM, HEAD_DIM), f32) * HEAD_DIM ** -0.5,
        "cmp_pos_v": 0.02 * nrm(ks[17], (CMP_BLOCK, HEAD_DIM), f32),
        "cmp_w1_v": nrm(ks[18], (CMP_BLOCK * HEAD_DIM, HEAD_DIM), f32) * (CMP_BLOCK * HEAD_DIM) ** -0.5,
        "cmp_w2_v": nrm(ks[19], (HEAD_DIM, HEAD_DIM), f32) * HEAD_DIM ** -0.5,
        "b_w_in": nrm(ks[20], (nb, D_MODEL, B_IN_WIDTH), f32) * D_MODEL ** -0.5,
        "b_w_out": nrm(ks[21], (nb, ATTN_WIDTH, D_MODEL), f32) * ATTN_WIDTH ** -0.5 * DEEPNORM_BETA,
        "ln_g": 1.0 + 0.01 * nrm(ks[22], (DEPTH, D_MODEL), f32),
        "ln_b": 0.01 * nrm(ks[23], (DEPTH, D_MODEL), f32),
    }


def reference(x, a_w_in, a_lam_re, a_lam_im, a_log_dt, a_b_re, a_b_im, a_c_re, a_c_im, a_d,
              a_w_glu, a_b_glu, a_w_out, kv_w, cmp_pos_k, cmp_w1_k, cmp_w2_k, cmp_pos_v,
              cmp_w1_v, cmp_w2_v, b_w_in, b_w_out, ln_g, ln_b):
    shared = None
    for layer in range(DEPTH):
        if layer < N_A_LAYERS:
            i = layer
            y = s5_mixer(x, a_w_in[i], a_lam_re[i], a_lam_im[i], a_log_dt[i], a_b_re[i], a_b_im[i],
                         a_c_re[i], a_c_im[i], a_d[i], a_w_glu[i], a_b_glu[i], a_w_out[i])
        else:
            i = layer - N_A_LAYERS
            y = nsa_mixer(x, b_w_in[i], b_w_out[i], shared)
        x = layer_norm(DEEPNORM_ALPHA * x + y, ln_g[layer], ln_b[layer])
        if layer == N_A_LAYERS - 1:
            shared = shared_kv(x, kv_w, cmp_pos_k, cmp_w1_k, cmp_w2_k, cmp_pos_v, cmp_w1_v, cmp_w2_v)
    return x
```

```python
import contextlib
import numpy as np
import ml_dtypes
import concourse.bass as bass
import concourse.mybir as mybir
from concourse.bass_utils import run_bass_kernel_spmd

F32 = mybir.dt.float32
BF16 = mybir.dt.bfloat16
AF = mybir.ActivationFunctionType
ALU = mybir.AluOpType
NPBF = ml_dtypes.bfloat16

D_MODEL = 2048
BATCH = 2
SEQ = 4096
NCORES = 8
LN_EPS = 1e-5
ALPHA = 8 ** 0.25
SCALE = 128 ** -0.5
NEG = -30000.0


class Prog:
    COMPUTE = ["pe", "act", "dve", "pool"]
    QUEUES = {"sp": "sp", "qa": "act", "qp": "pool"}
    STREAMS = ["pe", "act", "dve", "pool", "sp"]
    NSLOT = 6

    def __init__(self, nc, same_sync=True):
        self.nc = nc
        self.owners = list(self.COMPUTE) + [f"{q}{j}" for q in self.QUEUES for j in range(self.NSLOT)]
        self.stream = {e: e for e in self.COMPUTE}
        for q, st in self.QUEUES.items():
            for j in range(self.NSLOT):
                self.stream[f"{q}{j}"] = st
        self.inc = {e: (1 if e in self.COMPUTE else 16) for e in self.owners}
        self.q = {e: [] for e in self.STREAMS}
        self.cnt = {e: 0 for e in self.owners}
        self.seen = {st: {o: 0 for o in self.owners} for st in self.STREAMS}
        self.qn = {q: 0 for q in self.QUEUES}
        self.lastw = {}
        self.reads = {}
        self.same_sync = same_sync
        self.seq = 0
        self.cap = None

    def _need(self, eng, other, idx):
        st = self.stream[eng]
        if other == eng and eng == "pe":
            return
        if other == eng and not self.same_sync and eng in self.COMPUTE:
            return
        if self.seen[st][other] >= idx:
            return
        self.seen[st][other] = idx
        self.seq += 1
        self.q[st].append(("wait", other, idx, self.seq))

    def barrier(self):
        rep = {"pe": "pe", "act": "act", "dve": "dve", "pool": "pool", "sp": "sp0"}
        for st in self.STREAMS:
            for o in self.owners:
                if self.cnt[o] > 0:
                    self._need(rep[st], o, self.cnt[o])

    def capture(self):
        self.cap = []
        return self.cap

    def end_capture(self):
        c, self.cap = self.cap, None
        return c

    def replay(self, *streams):
        pos = [0] * len(streams)
        tot = [max(1, len(x)) for x in streams]
        n = sum(len(x) for x in streams)
        for _ in range(n):
            best, bi = None, None
            for i, x in enumerate(streams):
                if pos[i] < len(x):
                    f = pos[i] / tot[i]
                    if best is None or f < best:
                        best, bi = f, i
            a = streams[bi][pos[bi]]
            pos[bi] += 1
            self.op(*a)

    def op(self, eng, fn, reads=(), writes=()):
        if self.cap is not None:
            self.cap.append((eng, fn, tuple(reads), tuple(writes)))
            return None
        if eng in self.QUEUES:
            j = self.qn[eng] % self.NSLOT
            self.qn[eng] += 1
            eng = f"{eng}{j}"
            if self.cnt[eng] > 0:
                self._need(eng, eng, self.cnt[eng])
        for k in reads:
            w = self.lastw.get(k)
            if w is not None:
                self._need(eng, *w)
        for k in writes:
            w = self.lastw.get(k)
            if w is not None:
                self._need(eng, *w)
            for (o, i) in self.reads.get(k, {}).items():
                self._need(eng, o, i)
        self.cnt[eng] += 1
        idx = self.cnt[eng]
        self.seq += 1
        self.q[self.stream[eng]].append(("op", fn, idx, self.seq, eng))
        for k in reads:
            self.reads.setdefault(k, {})[eng] = idx
        for k in writes:
            self.lastw[k] = (eng, idx)
            self.reads[k] = {}
        return idx

    def emit(self):
        nc = self.nc
        with contextlib.ExitStack() as st:
            sems = {e: st.enter_context(nc.semaphore("sem_" + e)) for e in self.owners}
            block = st.enter_context(nc.Block())
            engobj = {"pe": block.tensor, "act": block.scalar, "dve": block.vector,
                      "pool": block.gpsimd, "sp": block.sync}
            for e in self.STREAMS:
                items = self.q[e]

                def body(engine, items=items, e=e):
                    for it in items:
                        if it[0] == "wait":
                            _, o, idx, _s = it
                            engine.wait_ge(sems[o], idx * self.inc[o])
                        else:
                            _, fn, idx, _s, owner = it
                            ins = fn(engine)
                            ins.then_inc(sems[owner], self.inc[owner])
                    if e == "sp":
                        for o in self.owners:
                            if self.cnt[o] > 0:
                                engine.wait_ge(sems[o], self.cnt[o] * self.inc[o])
                engobj[e](body)


class KB:
    def __init__(self, name="k"):
        self.nc = bass.Bass("TRN2", target_bir_lowering=False)
        self.st = contextlib.ExitStack()
        self.P = Prog(self.nc)
        self.nps = 0
        self.uid = 0

    def dram_in(self, name, shape, dt=F32):
        return self.nc.dram_tensor(name, list(shape), dt, kind="ExternalInput").ap()

    def dram_out(self, name, shape, dt=F32):
        return self.nc.dram_tensor(name, list(shape), dt, kind="ExternalOutput").ap()

    def sb(self, name, shape, dt=F32):
        return self.st.enter_context(self.nc.sbuf_tensor(name, list(shape), dt))

    def ps(self, name, shape, dt=F32):
        return self.st.enter_context(self.nc.psum_tensor(name, list(shape), dt))

    def finish(self):
        self.P.emit()
        self.st.close()
        return self.nc


def run(nc, in_maps):
    res = run_bass_kernel_spmd(nc, in_maps, core_ids=list(range(len(in_maps))))
    return res.results


def act_apply(e, out, in_, act):
    if act == "id":
        return e.copy(out=out, in_=in_)
    if act == "scale":
        return e.mul(out=out, in_=in_, mul=SCALE)
    if act == "silu":
        return e.activation(out=out, in_=in_, func=AF.Silu)
    if act == "sigmoid":
        return e.activation(out=out, in_=in_, func=AF.Sigmoid)
    raise ValueError(act)


def proj_blocks(tiles, WB=512):
    blocks, cur, curw = [], [], 0
    for t in tiles:
        if curw + t[0] > WB:
            blocks.append(cur)
            cur, curw = [], 0
        cur.append(t)
        curw += t[0]
    if cur:
        blocks.append(cur)
    return blocks


def proj_host_w(w, tiles, WB=512):
    K_, N = w.shape
    KC = K_ // 128
    blocks = proj_blocks(tiles, WB)
    out = np.zeros((len(blocks), 128, KC, WB), np.float32)
    c0 = 0
    for bi, tl in enumerate(blocks):
        bw = sum(t[0] for t in tl)
        out[bi, :, :, :bw] = w[:, c0:c0 + bw].reshape(KC, 128, bw).transpose(1, 0, 2)
        c0 += bw
    return out


def host_fm(aT):
    K_, T = aT.shape
    return np.ascontiguousarray(aT.reshape(K_ // 128, 128, T).transpose(1, 0, 2))


def build_proj(KC, T, tiles):
    kb = KB()
    P = kb.P
    WB = 512
    blocks = proj_blocks(tiles, WB)
    n32 = sum(t[0] for t in tiles if t[2] == "o32")
    n16 = sum(t[0] for t in tiles if t[2] == "o16")
    xT = kb.dram_in("xT", [128, KC, T])
    w = kb.dram_in("w", [len(blocks), 128, KC, WB])
    outs = {}
    if n32:
        outs["o32"] = kb.dram_out("o32", [n32, T], F32)
    if n16:
        outs["o16"] = kb.dram_out("o16", [n16, T], BF16)
    xb = kb.sb("xb", [128, KC, T], BF16)
    KG = 4
    LQ = ["sp", "qa", "qp"]
    xst = [kb.sb(f"xst{i}", [128, KG, T], F32) for i in range(2)]
    wst = [kb.sb(f"wst{i}", [128, KC, WB], F32) for i in range(2)]
    wb = [kb.sb(f"wb{i}", [128, KC, WB], BF16) for i in range(2)]
    o32s = [kb.sb(f"o32s{i}", [128, 512], F32) for i in range(2)]
    o16s = [kb.sb(f"o16s{i}", [128, 512], BF16) for i in range(2)]
    pss = [kb.ps(f"ps{i}", [128, 512], F32) for i in range(8)]
    H = KC // 2

    def load_w(bi):
        bw = sum(t[0] for t in blocks[bi])
        ws, wbb = wst[bi % 2], wb[bi % 2]
        P.op(LQ[(2 * bi) % 3], lambda e: e.dma_start(out=ws[:, 0:H, 0:bw], in_=w[bi, :, 0:H, 0:bw]), writes=[f"wst{bi % 2}a"])
        P.op(LQ[(2 * bi + 1) % 3], lambda e: e.dma_start(out=ws[:, H:KC, 0:bw], in_=w[bi, :, H:KC, 0:bw]), writes=[f"wst{bi % 2}b"])
        P.op("dve", lambda e: e.tensor_copy(out=wbb[:, 0:H, 0:bw], in_=ws[:, 0:H, 0:bw]), reads=[f"wst{bi % 2}a"], writes=[f"wb{bi % 2}"])
        P.op("act", lambda e: e.copy(out=wbb[:, H:KC, 0:bw], in_=ws[:, H:KC, 0:bw]), reads=[f"wst{bi % 2}b"], writes=[f"wb{bi % 2}"])

    for i in range(KC // KG):
        s_ = xst[i % 2]
        P.op(LQ[i % 3], lambda e, s_=s_, i=i: e.dma_start(out=s_[:], in_=xT[:, i * KG:(i + 1) * KG, :]), writes=[f"xst{i % 2}"])
        P.op("dve" if i % 2 == 0 else "pool", lambda e, s_=s_, i=i: e.tensor_copy(out=xb[:, i * KG:(i + 1) * KG, :], in_=s_[:]),
             reads=[f"xst{i % 2}"], writes=["xb"])
    load_w(0)
    rowpos = {"o32": 0, "o16": 0}
    psi = 0
    oi = {"o32": 0, "o16": 0}
    for bi, tl in enumerate(blocks):
        if bi + 1 < len(blocks):
            load_w(bi + 1)
        wbb = wb[bi % 2]
        off = 0
        for (nc_, act, oname) in tl:
            for n in range(T // 512):
                ps = pss[psi % 8]
                pk = f"ps{psi % 8}"
                psi += 1
                for k in range(KC):
                    P.op("pe", lambda e, ps=ps, wbb=wbb, off=off, nc_=nc_, k=k, n=n: e.matmul(
                        ps[0:nc_, :], wbb[:, k, off:off + nc_], xb[:, k, n * 512:(n + 1) * 512],
                        start=(k == 0), stop=(k == KC - 1)),
                        reads=[f"wb{bi % 2}", "xb"], writes=[pk])
                j = oi[oname] % 2
                oi[oname] += 1
                osb = (o32s if oname == "o32" else o16s)[j]
                ok = f"{oname}s{j}"
                P.op("act", lambda e, osb=osb, ps=ps, nc_=nc_, act=act: act_apply(e, osb[0:nc_, :], ps[0:nc_, :], act),
                     reads=[pk], writes=[ok])
                r0 = rowpos[oname]
                od = outs[oname]
                P.op("sp", lambda e, od=od, osb=osb, r0=r0, nc_=nc_, n=n: e.dma_start(
                    out=od[r0:r0 + nc_, n * 512:(n + 1) * 512], in_=osb[0:nc_, :]),
                    reads=[ok])
            rowpos[oname] += nc_
            off += nc_
    return kb.finish()


def build_tail(KC, T, glu, ydt=F32):
    kb = KB()
    P = kb.P
    x = kb.dram_in("x", [T, D_MODEL])
    w_v = kb.dram_in("w", [128, KC, D_MODEL])
    lng = kb.dram_in("lng", [128, D_MODEL])
    lnb = kb.dram_in("lnb", [128, D_MODEL])
    xo = kb.dram_out("xo", [T, D_MODEL])
    wb = kb.sb("wb", [128, KC, D_MODEL], BF16)
    yb = kb.sb("yb", [128, KC, T], BF16)
    STF = KC * 256
    stg = [kb.sb(f"stg{i}", [128, STF], F32) for i in range(2)]
    lng_s = kb.sb("lng_s", [128, D_MODEL], F32)
    lnb_s = kb.sb("lnb_s", [128, D_MODEL], F32)
    xt = [kb.sb(f"xt{i}", [128, D_MODEL], F32) for i in range(2)]
    stats = kb.sb("stats", [128, 4, 6], F32)
    mv = kb.sb("mv", [128, 2], F32)
    rstd = kb.sb("rstd", [128, 1], F32)
    epsc = kb.sb("epsc", [128, 1], F32)
    pss = [kb.ps(f"ps{i}", [128, 512], F32) for i in range(8)]
    psi = [0]

    def nextps():
        i = psi[0] % 8
        psi[0] += 1
        return pss[i], f"ps{i}"

    P.op("sp", lambda e: e.dma_start(out=lng_s[:], in_=lng[:, :]), writes=["lng"])
    P.op("sp", lambda e: e.dma_start(out=lnb_s[:], in_=lnb[:, :]), writes=["lnb"])
    P.op("pool", lambda e: e.memset(epsc[:], LN_EPS), writes=["epsc"])
    si = [0]

    def load_cast(dst, dkey, src_v, ncols, engs=("dve", "act")):
        kc = src_v.shape[1]
        kg = max(1, STF // ncols)
        for k0 in range(0, kc, kg):
            kn = min(kg, kc - k0)
            j = si[0] % 2
            q = ("sp", "qa", "qp")[si[0] % 3]
            si[0] += 1
            sv = stg[j][:, 0:kn * ncols].rearrange("p (k n) -> p k n", k=kn)
            P.op(q, lambda e, sv=sv, k0=k0, kn=kn: e.dma_start(out=sv, in_=src_v[:, k0:k0 + kn, :]), writes=[f"stg{j}"])
            if engs[j % len(engs)] == "act":
                P.op("act", lambda e, sv=sv, k0=k0, kn=kn: e.copy(out=dst[:, k0:k0 + kn, 0:ncols], in_=sv), reads=[f"stg{j}"], writes=[dkey])
            else:
                P.op(engs[j % len(engs)], lambda e, sv=sv, k0=k0, kn=kn: e.tensor_copy(out=dst[:, k0:k0 + kn, 0:ncols], in_=sv),
                     reads=[f"stg{j}"], writes=[dkey])

    if glu:
        gT_v = kb.dram_in("gT", [128, KC, T])
        szT_v = kb.dram_in("szT", [128, KC, T])
        wglu_v = kb.dram_in("wglu", [128, KC, KC * 128])
        bglu = kb.dram_in("bglu", [128, KC])
        wgb = kb.sb("wgb", [128, KC, KC * 128], BF16)
        bg = kb.sb("bg", [128, KC], F32)
        g32 = kb.sb("g32", [128, KC, 512], F32)
        sz32 = kb.sb("sz32", [128, KC, 512], F32)
        gb = kb.sb("gb", [128, KC, 512], BF16)
        sig = [kb.sb(f"sig{i}", [128, 512], F32) for i in range(2)]
        P.op("sp", lambda e: e.dma_start(out=bg[:], in_=bglu[:, :]), writes=["bg"])
        load_cast(wgb, "wgb", wglu_v, KC * 128)
        for n in range(T // 512):
            P.op("sp", lambda e, n=n: e.dma_start(out=g32[:], in_=gT_v[:, :, n * 512:(n + 1) * 512]), writes=["g32"])
            P.op("qa", lambda e, n=n: e.dma_start(out=sz32[:], in_=szT_v[:, :, n * 512:(n + 1) * 512]), writes=["sz32"])
            P.op("pool", lambda e: e.tensor_copy(out=gb[:], in_=g32[:]), reads=["g32"], writes=["gb"])
            for m in range(KC):
                ps, pk = nextps()
                for k in range(KC):
                    P.op("pe", lambda e, ps=ps, k=k, m=m: e.matmul(ps[:], wgb[:, k, m * 128:(m + 1) * 128], gb[:, k, :],
                                                                   start=(k == 0), stop=(k == KC - 1)),
                         reads=["wgb", "gb"], writes=[pk])
                sg = sig[m % 2]
                P.op("act", lambda e, sg=sg, ps=ps, m=m: e.activation(out=sg[:], in_=ps[:], func=AF.Sigmoid, bias=bg[:, m:m + 1]),
                     reads=[pk, "bg"], writes=[f"sig{m % 2}"])
                P.op("dve", lambda e, sg=sg, m=m: e.tensor_tensor(out=sg[:], in0=sg[:], in1=g32[:, m, :], op=ALU.mult),
                     reads=["g32"], writes=[f"sig{m % 2}"])
                P.op("dve", lambda e, sg=sg, m=m, n=n: e.tensor_tensor(out=yb[:, m, n * 512:(n + 1) * 512], in0=sg[:], in1=sz32[:, m, :], op=ALU.mult),
                     reads=["sz32", f"sig{m % 2}"], writes=["yb"])
    else:
        yT_v = kb.dram_in("yT", [128, KC, T], ydt)
        if ydt == BF16:
            P.op("sp", lambda e: e.dma_start(out=yb[:, 0:KC // 2, :], in_=yT_v[:, 0:KC // 2, :]), writes=["yb"])
            P.op("qa", lambda e: e.dma_start(out=yb[:, KC // 2:KC, :], in_=yT_v[:, KC // 2:KC, :]), writes=["yb"])
        else:
            load_cast(yb, "yb", yT_v, T)
    load_cast(wb, "wb", w_v, D_MODEL)
    x_v = x.rearrange("(n p) d -> n p d", p=128)
    xo_v = xo.rearrange("(n p) d -> n p d", p=128)
    for tt in range(T // 128):
        xs = xt[tt % 2]
        xk = f"xt{tt % 2}"
        P.op("qa" if tt % 2 else "sp", lambda e, xs=xs, tt=tt: e.dma_start(out=xs[:], in_=x_v[tt]), writes=[xk])
        for n in range(4):
            ps, pk = nextps()
            for k in range(KC):
                P.op("pe", lambda e, ps=ps, k=k, tt=tt, n=n: e.matmul(ps[:], yb[:, k, tt * 128:(tt + 1) * 128], wb[:, k, n * 512:(n + 1) * 512],
                                                                      start=(k == 0), stop=(k == KC - 1)),
                     reads=["yb", "wb"], writes=[pk])
            P.op("dve", lambda e, xs=xs, ps=ps, n=n: e.scalar_tensor_tensor(out=xs[:, n * 512:(n + 1) * 512], in0=xs[:, n * 512:(n + 1) * 512], scalar=ALPHA,
                                                                            in1=ps[:], op0=ALU.mult, op1=ALU.add),
                 reads=[pk], writes=[xk])
            P.op("dve", lambda e, xs=xs, n=n: e.bn_stats(out=stats[:, n, :], in_=xs[:, n * 512:(n + 1) * 512]),
                 reads=[xk], writes=["stats"])
        P.op("dve", lambda e: e.bn_aggr(out=mv[:], in_=stats[:]), reads=["stats"], writes=["mv"])
        P.op("act", lambda e: e.activation(out=rstd[:], in_=mv[:, 1:2], func=AF.Sqrt, bias=epsc[:]), reads=["mv", "epsc"], writes=["rstd"])
        P.op("dve", lambda e: e.reciprocal(out=rstd[:], in_=rstd[:]), reads=["rstd"], writes=["rstd"])
        P.op("dve", lambda e, xs=xs: e.tensor_scalar(out=xs[:], in0=xs[:], scalar1=mv[:, 0:1], scalar2=rstd[:], op0=ALU.subtract, op1=ALU.mult),
             reads=["mv", "rstd"], writes=[xk])
        P.op("pool", lambda e, xs=xs: e.tensor_tensor(out=xs[:], in0=xs[:], in1=lng_s[:], op=ALU.mult), reads=["lng"], writes=[xk])
        P.op("pool", lambda e, xs=xs: e.tensor_tensor(out=xs[:], in0=xs[:], in1=lnb_s[:], op=ALU.add), reads=["lnb"], writes=[xk])
        P.op("sp", lambda e, xs=xs, tt=tt: e.dma_start(out=xo_v[tt], in_=xs[:]), reads=[xk])
    return kb.finish()


def build_scan(L=4096, NB=2):
    kb = KB()
    P = kb.P
    nc = kb.nc
    NT = L * NB
    NLEV = int(np.log2(L))
    CH = 1024
    NCH = L // CH
    uT = kb.dram_in("uT", [128, NT])
    lr_d = kb.dram_in("lr", [128, 4])
    li_d = kb.dram_in("li", [128, 4])
    ldt_d = kb.dram_in("ldt", [128, 4])
    br_d = kb.dram_in("br", [128, 4, 16])
    bi_d = kb.dram_in("bi", [128, 4, 16])
    cr_d = kb.dram_in("crT", [128, 4, 16])
    ci_d = kb.dram_in("ciT", [128, 4, 16])
    d_d = kb.dram_in("dvec", [128, 1])
    id_d = kb.dram_in("ident", [128, 128])
    gT = kb.dram_out("gT", [128, NT])

    ub = kb.sb("ub", [128, NT], BF16)
    yd = kb.sb("yd", [128, NT], F32)
    ident = kb.sb("ident_s", [128, 128], F32)
    sm = {n: kb.sb("sm_" + n, [128, 4], F32) for n in ["lr", "li", "dt", "th", "mag", "c", "s", "t1", "t2", "t3", "ar", "ai",
                                                       "den", "am1", "cr", "ci", "nci"]}
    br = kb.sb("br_s", [128, 4, 16], F32)
    bi = kb.sb("bi_s", [128, 4, 16], F32)
    crT = kb.sb("crT_s", [128, 4, 16], F32)
    ciT = kb.sb("ciT_s", [128, 4, 16], F32)
    bbr = kb.sb("bbr", [128, 4, 16], F32)
    bbi = kb.sb("bbi", [128, 4, 16], F32)
    tmpb = kb.sb("tmpb", [128, 4, 16], F32)
    Bblk = {n: kb.sb(n, [128, 4, 32], F32) for n in ["Bblk_r", "Bblk_i"]}
    BT = {n: kb.sb(n, [128, 128], BF16) for n in ["BT_r", "BT_i"]}
    Cpad = {n: kb.sb(n, [128, 4, 128], F32) for n in ["Cpad_r", "Cpad_ni"]}
    pc = kb.sb("pc", [128, NLEV, 4], F32)
    psn = kb.sb("psn", [128, NLEV, 4], F32)
    pns = kb.sb("pns", [128, NLEV, 4], F32)
    dv = kb.sb("dv", [128, 1], F32)
    halfpi = kb.sb("halfpi", [128, 1], F32)
    mtop = kb.sb("mtop", [128, 1], F32)
    mbot = kb.sb("mbot", [128, 1], F32)
    nmtop = kb.sb("nmtop", [128, 1], F32)
    nmbot = kb.sb("nmbot", [128, 1], F32)
    gt = [kb.sb(f"gt{i}", [128, 512], F32) for i in range(2)]
    gs = [kb.sb(f"gs{i}", [128, 512], F32) for i in range(2)]
    st0 = contextlib.ExitStack()
    u32 = st0.enter_context(nc.sbuf_tensor("u32", [128, NT], F32))
    pss = [kb.ps(f"ps{i}", [128, 512], F32) for i in range(8)]
    psi = [0]

    def nextps():
        i = psi[0] % 8
        psi[0] += 1
        return pss[i], f"ps{i}"

    def dma_in(dst, src, key, q="sp"):
        P.op(q, lambda e: e.dma_start(out=dst, in_=src), writes=[key])

    for q in range(4):
        P.op(("sp", "qa")[q % 2], lambda e, q=q: e.dma_start(out=u32[:, q * NT // 4:(q + 1) * NT // 4], in_=uT[:, q * NT // 4:(q + 1) * NT // 4]), writes=[f"u32_{q}"])
        P.op("dve", lambda e, q=q: e.tensor_copy(out=ub[:, q * NT // 4:(q + 1) * NT // 4], in_=u32[:, q * NT // 4:(q + 1) * NT // 4]), reads=[f"u32_{q}"], writes=[f"ub_{q}"])
    UBK = [f"ub_{q}" for q in range(4)]
    dma_in(sm["lr"][:], lr_d[:, :], "lr")
    dma_in(sm["li"][:], li_d[:, :], "li")
    dma_in(sm["dt"][:], ldt_d[:, :], "dt")
    dma_in(br[:], br_d[:, :, :], "br")
    dma_in(bi[:], bi_d[:, :, :], "bi")
    dma_in(crT[:], cr_d[:, :, :], "crT")
    dma_in(ciT[:], ci_d[:, :, :], "ciT")
    dma_in(dv[:], d_d[:, :], "dv")
    dma_in(ident[:], id_d[:, :], "ident")
    P.op("pool", lambda e: e.memset(halfpi[:], float(np.pi / 2)), writes=["halfpi"])
    for (t_, a, b) in [(mtop, 1.0, 0.0), (mbot, 0.0, 1.0), (nmtop, -1.0, 0.0), (nmbot, 0.0, -1.0)]:
        P.op("pool", lambda e, t_=t_, a=a: e.memset(t_[0:64, :], a), writes=["masks"])
        P.op("pool", lambda e, t_=t_, b=b: e.memset(t_[64:128, :], b), writes=["masks"])
    for n in ["Bblk_r", "Bblk_i"]:
        P.op("pool", lambda e, n=n: e.memset(Bblk[n][:], 0.0), writes=[n])
    for n in ["Cpad_r", "Cpad_ni"]:
        P.op("pool", lambda e, n=n: e.memset(Cpad[n][:], 0.0), writes=[n])

    def tt(out, a, b, op, rk, wk, eng="dve"):
        P.op(eng, lambda e: e.tensor_tensor(out=out, in0=a, in1=b, op=op), reads=rk, writes=wk)

    def S(n):
        return sm[n][:]

    def renorm(c_ap, s_ap, ck, sk):
        tt(S("t1"), c_ap, c_ap, ALU.mult, [ck], ["t1"])
        tt(S("t2"), s_ap, s_ap, ALU.mult, [sk], ["t2"])
        tt(S("t1"), S("t1"), S("t2"), ALU.add, ["t2"], ["t1"])
        P.op("dve", lambda e: e.tensor_scalar(out=S("t1"), in0=S("t1"), scalar1=-0.5, scalar2=1.5, op0=ALU.mult, op1=ALU.add), writes=["t1"])
        tt(c_ap, c_ap, S("t1"), ALU.mult, ["t1"], [ck])
        tt(s_ap, s_ap, S("t1"), ALU.mult, ["t1"], [sk])

    P.op("act", lambda e: e.activation(out=S("dt"), in_=S("dt"), func=AF.Exp), reads=["dt"], writes=["dt"])
    tt(S("th"), S("li"), S("dt"), ALU.mult, ["li", "dt"], ["th"])
    tt(S("t1"), S("lr"), S("dt"), ALU.mult, ["lr", "dt"], ["t1"])
    P.op("act", lambda e: e.activation(out=S("mag"), in_=S("t1"), func=AF.Exp), reads=["t1"], writes=["mag"])
    P.op("act", lambda e: e.activation(out=S("s"), in_=S("th"), func=AF.Sin, scale=1.0 / 16), reads=["th"], writes=["s"])
    P.op("act", lambda e: e.activation(out=S("c"), in_=S("th"), func=AF.Sin, scale=1.0 / 16, bias=halfpi[:]), reads=["th", "halfpi"], writes=["c"])
    for _ in range(4):
        tt(S("t1"), S("c"), S("c"), ALU.mult, ["c"], ["t1"])
        tt(S("t2"), S("s"), S("s"), ALU.mult, ["s"], ["t2"])
        tt(S("t3"), S("c"), S("s"), ALU.mult, ["c", "s"], ["t3"])
        tt(S("c"), S("t1"), S("t2"), ALU.subtract, ["t1", "t2"], ["c"])
        tt(S("s"), S("t3"), S("t3"), ALU.add, ["t3"], ["s"])
    renorm(S("c"), S("s"), "c", "s")
    tt(S("ar"), S("mag"), S("c"), ALU.mult, ["mag", "c"], ["ar"])
    tt(S("ai"), S("mag"), S("s"), ALU.mult, ["mag", "s"], ["ai"])
    P.op("dve", lambda e: e.tensor_copy(out=pc[:, 0, :], in_=S("c")), reads=["c"], writes=["pc"])
    P.op("dve", lambda e: e.tensor_copy(out=psn[:, 0, :], in_=S("s")), reads=["s"], writes=["psn"])
    for lev in range(1, NLEV):
        tt(S("t1"), pc[:, lev - 1, :], pc[:, lev - 1, :], ALU.mult, ["pc"], ["t1"])
        tt(S("t2"), psn[:, lev - 1, :], psn[:, lev - 1, :], ALU.mult, ["psn"], ["t2"])
        tt(S("t3"), pc[:, lev - 1, :], psn[:, lev - 1, :], ALU.mult, ["pc", "psn"], ["t3"])
        tt(pc[:, lev, :], S("t1"), S("t2"), ALU.subtract, ["t1", "t2"], ["pc"])
        tt(psn[:, lev, :], S("t3"), S("t3"), ALU.add, ["t3"], ["psn"])
        renorm(pc[:, lev, :], psn[:, lev, :], "pc", "psn")
    P.op("dve", lambda e: e.tensor_scalar(out=pns[:], in0=psn[:], scalar1=-1.0, scalar2=None, op0=ALU.mult), reads=["psn"], writes=["pns"])
    tt(S("t1"), S("lr"), S("lr"), ALU.mult, ["lr"], ["t1"])
    tt(S("t2"), S("li"), S("li"), ALU.mult, ["li"], ["t2"])
    tt(S("den"), S("t1"), S("t2"), ALU.add, ["t1", "t2"], ["den"])
    P.op("dve", lambda e: e.reciprocal(out=S("den"), in_=S("den")), reads=["den"], writes=["den"])
    P.op("dve", lambda e: e.tensor_scalar(out=S("am1"), in0=S("ar"), scalar1=-1.0, scalar2=None, op0=ALU.add), reads=["ar"], writes=["am1"])
    tt(S("t1"), S("am1"), S("lr"), ALU.mult, ["am1", "lr"], ["t1"])
    tt(S("t2"), S("ai"), S("li"), ALU.mult, ["ai", "li"], ["t2"])
    tt(S("t3"), S("t1"), S("t2"), ALU.add, ["t1", "t2"], ["t3"])
    tt(S("cr"), S("t3"), S("den"), ALU.mult, ["t3", "den"], ["cr"])
    tt(S("t1"), S("ai"), S("lr"), ALU.mult, ["ai", "lr"], ["t1"])
    tt(S("t2"), S("am1"), S("li"), ALU.mult, ["am1", "li"], ["t2"])
    tt(S("t3"), S("t1"), S("t2"), ALU.subtract, ["t1", "t2"], ["t3"])
    tt(S("ci"), S("t3"), S("den"), ALU.mult, ["t3", "den"], ["ci"])
    P.op("dve", lambda e: e.tensor_scalar(out=S("nci"), in0=S("ci"), scalar1=-1.0, scalar2=None, op0=ALU.mult), reads=["ci"], writes=["nci"])
    for j in range(4):
        P.op("dve", lambda e, j=j: e.tensor_scalar(out=tmpb[:, j, :], in0=bi[:, j, :], scalar1=sm["nci"][:, j:j + 1], scalar2=None, op0=ALU.mult),
             reads=["bi", "nci"], writes=["tmpb"])
        P.op("dve", lambda e, j=j: e.scalar_tensor_tensor(out=bbr[:, j, :], in0=br[:, j, :], scalar=sm["cr"][:, j:j + 1], in1=tmpb[:, j, :], op0=ALU.mult, op1=ALU.add),
             reads=["br", "cr", "tmpb"], writes=["bbr"])
        P.op("dve", lambda e, j=j: e.tensor_scalar(out=tmpb[:, j, :], in0=br[:, j, :], scalar1=sm["ci"][:, j:j + 1], scalar2=None, op0=ALU.mult),
             reads=["br", "ci"], writes=["tmpb"])
        P.op("dve", lambda e, j=j: e.scalar_tensor_tensor(out=bbi[:, j, :], in0=bi[:, j, :], scalar=sm["cr"][:, j:j + 1], in1=tmpb[:, j, :], op0=ALU.mult, op1=ALU.add),
             reads=["bi", "cr", "tmpb"], writes=["bbi"])
    for (src, n) in [(bbr, "Bblk_r"), (bbi, "Bblk_i")]:
        sk = "bbr" if n == "Bblk_r" else "bbi"
        P.op("dve", lambda e, src=src, n=n: e.tensor_scalar(out=Bblk[n][:, :, 0:16], in0=src[:], scalar1=mtop[:], scalar2=None, op0=ALU.mult),
             reads=[sk, "masks"], writes=[n])
        P.op("dve", lambda e, src=src, n=n: e.tensor_scalar(out=Bblk[n][:, :, 16:32], in0=src[:], scalar1=mbot[:], scalar2=None, op0=ALU.mult),
             reads=[sk, "masks"], writes=[n])
    for (n, bn) in [("Bblk_r", "BT_r"), ("Bblk_i", "BT_i")]:
        ps, pk = nextps()
        P.op("pe", lambda e, ps=ps, n=n: e.transpose(ps[:, 0:128], Bblk[n][:].rearrange("p j c -> p (j c)"), ident[:]), reads=[n, "ident"], writes=[pk])
        P.op("act", lambda e, ps=ps, bn=bn: e.copy(out=BT[bn][:], in_=ps[:, 0:128]), reads=[pk], writes=[bn])
    for j in range(4):
        for (cn, src, ma, mb, sk) in [("Cpad_r", crT, mtop, mbot, "crT"), ("Cpad_ni", ciT, nmtop, nmbot, "ciT")]:
            P.op("dve", lambda e, j=j, cn=cn, src=src, ma=ma: e.tensor_scalar(out=Cpad[cn][:, j, 32 * j:32 * j + 16], in0=src[:, j, :], scalar1=ma[:], scalar2=None, op0=ALU.mult),
                 reads=[sk, "masks"], writes=[cn])
            P.op("dve", lambda e, j=j, cn=cn, src=src, mb=mb: e.tensor_scalar(out=Cpad[cn][:, j, 32 * j + 16:32 * j + 32], in0=src[:, j, :], scalar1=mb[:], scalar2=None, op0=ALU.mult),
                 reads=[sk, "masks"], writes=[cn])
    NBLK = L // 512
    for q in range(4):
        P.op("pool", lambda e, q=q: e.tensor_scalar(out=yd[:, q * NT // 4:(q + 1) * NT // 4], in0=u32[:, q * NT // 4:(q + 1) * NT // 4], scalar1=dv[:], scalar2=None, op0=ALU.mult),
             reads=[f"u32_{q}", "dv"], writes=[f"yd_{q}"])
    P.barrier()
    st0.close()

    def ydk(tok):
        return f"yd_{tok // (NT // 4)}"

    Ec = kb.sb("Ec", [128, L], F32)
    Es = kb.sb("Es", [128, L], F32)
    magf = kb.sb("magf", [128, L], F32)
    Xr = kb.sb("Xr", [128, L], F32)
    Xi = kb.sb("Xi", [128, L], F32)
    T1 = [kb.sb(f"T1_{i}", [128, CH], F32) for i in range(2)]
    T2 = [kb.sb(f"T2_{i}", [128, CH], F32) for i in range(2)]
    XRK = [f"Xr{c}" for c in range(NCH)]
    XIK = [f"Xi{c}" for c in range(NCH)]
    tci = [0]

    def rotate(sign):
        for c in range(NCH):
            sl = slice(c * CH, (c + 1) * CH)
            k = tci[0] % 2
            tci[0] += 1
            t1, t2 = T1[k], T2[k]
            P.op("pool", lambda e, sl=sl, t1=t1: e.tensor_tensor(out=t1[:], in0=Xi[:, sl], in1=Es[:, sl], op=ALU.mult), reads=[XIK[c], "E"], writes=[f"T1_{k}"])
            P.op("pool", lambda e, sl=sl, t2=t2: e.tensor_tensor(out=t2[:], in0=Xr[:, sl], in1=Es[:, sl], op=ALU.mult), reads=[XRK[c], "E"], writes=[f"T2_{k}"])
            P.op("dve", lambda e, sl=sl: e.tensor_tensor(out=Xr[:, sl], in0=Xr[:, sl], in1=Ec[:, sl], op=ALU.mult), reads=["E"], writes=[XRK[c]])
            P.op("dve", lambda e, sl=sl: e.tensor_tensor(out=Xi[:, sl], in0=Xi[:, sl], in1=Ec[:, sl], op=ALU.mult), reads=["E"], writes=[XIK[c]])
            P.op("dve", lambda e, sl=sl, t1=t1: e.tensor_tensor(out=Xr[:, sl], in0=Xr[:, sl], in1=t1[:], op=(ALU.add if sign < 0 else ALU.subtract)), reads=[f"T1_{k}"], writes=[XRK[c]])
            P.op("dve", lambda e, sl=sl, t2=t2: e.tensor_tensor(out=Xi[:, sl], in0=Xi[:, sl], in1=t2[:], op=(ALU.subtract if sign < 0 else ALU.add)), reads=[f"T2_{k}"], writes=[XIK[c]])

    for j in range(4):
        P.op("dve", lambda e: e.memset(Ec[:, 0:1], 1.0), writes=["E"])
        P.op("dve", lambda e: e.memset(Es[:, 0:1], 0.0), writes=["E"])
        for lev in range(NLEV):
            n = 1 << lev
            C = pc[:, lev, j:j + 1]
            Sn = psn[:, lev, j:j + 1]
            Sm = pns[:, lev, j:j + 1]
            t1 = T1[0] if n <= CH else None
            for o in range(0, n, CH):
                w_ = min(CH, n - o)
                a, b = T1[0], T2[0]
                P.op("dve", lambda e, o=o, w_=w_, a=a, Sm=Sm: e.tensor_scalar(out=a[:, 0:w_], in0=Es[:, o:o + w_], scalar1=Sm, scalar2=None, op0=ALU.mult), reads=["E", "pns"], writes=["T1_0"])
                P.op("dve", lambda e, o=o, w_=w_, b=b, Sn=Sn: e.tensor_scalar(out=b[:, 0:w_], in0=Ec[:, o:o + w_], scalar1=Sn, scalar2=None, op0=ALU.mult), reads=["E", "psn"], writes=["T2_0"])
                P.op("dve", lambda e, o=o, w_=w_, a=a, C=C, n=n: e.scalar_tensor_tensor(out=Ec[:, n + o:n + o + w_], in0=Ec[:, o:o + w_], scalar=C, in1=a[:, 0:w_], op0=ALU.mult, op1=ALU.add),
                     reads=["T1_0", "pc"], writes=["E"])
                P.op("dve", lambda e, o=o, w_=w_, b=b, C=C, n=n: e.scalar_tensor_tensor(out=Es[:, n + o:n + o + w_], in0=Es[:, o:o + w_], scalar=C, in1=b[:, 0:w_], op0=ALU.mult, op1=ALU.add),
                     reads=["T2_0", "pc"], writes=["E"])
        P.op("act", lambda e, j=j: e.activation(out=magf[:], in_=Ec[:], func=AF.Identity, scale=0.0, bias=sm["mag"][:, j:j + 1]), reads=["E", "mag"], writes=["magf"])
        for b in range(NB):
            for blk in range(NBLK):
                t0 = b * L + blk * 512
                c = (blk * 512) // CH
                for (bn, X_, xk) in [("BT_r", Xr, XRK[c]), ("BT_i", Xi, XIK[c])]:
                    ps, pk = nextps()
                    P.op("pe", lambda e, ps=ps, bn=bn, t0=t0, j=j: e.matmul(ps[:], BT[bn][32 * j:32 * j + 32, :], ub[32 * j:32 * j + 32, t0:t0 + 512],
                                                                            start=True, stop=True, tile_position=(32 * j, 0)),
                         reads=[bn] + UBK, writes=[pk])
                    P.op("act", lambda e, ps=ps, X_=X_, blk=blk: e.copy(out=X_[:, blk * 512:(blk + 1) * 512], in_=ps[:]), reads=[pk], writes=[xk])
            rotate(-1)
            P.op("dve", lambda e: e.tensor_tensor_scan(out=Xr[:], data0=magf[:], data1=Xr[:], initial=0.0, op0=ALU.mult, op1=ALU.add), reads=["magf"], writes=XRK)
            P.op("dve", lambda e: e.tensor_tensor_scan(out=Xi[:], data0=magf[:], data1=Xi[:], initial=0.0, op0=ALU.mult, op1=ALU.add), reads=["magf"], writes=XIK)
            rotate(+1)
            for blk in range(NBLK):
                t0 = b * L + blk * 512
                c = (blk * 512) // CH
                ps, pk = nextps()
                P.op("pe", lambda e, ps=ps, blk=blk, j=j: e.matmul(ps[:], Cpad["Cpad_r"][:, j, :], Xr[:, blk * 512:(blk + 1) * 512], start=True, stop=False),
                     reads=["Cpad_r", XRK[c]], writes=[pk])
                P.op("pe", lambda e, ps=ps, blk=blk, j=j: e.matmul(ps[:], Cpad["Cpad_ni"][:, j, :], Xi[:, blk * 512:(blk + 1) * 512], start=False, stop=True),
                     reads=["Cpad_ni", XIK[c]], writes=[pk])
                P.op("dve", lambda e, ps=ps, t0=t0: e.tensor_tensor(out=yd[:, t0:t0 + 512], in0=yd[:, t0:t0 + 512], in1=ps[:], op=ALU.add),
                     reads=[pk], writes=[ydk(t0)])
    for i in range(NT // 512):
        t0 = i * 512
        a, ak = gt[i % 2], f"gt{i % 2}"
        sg, sk = gs[i % 2], f"gs{i % 2}"
        P.op("act", lambda e, a=a, t0=t0: e.activation(out=a[:], in_=yd[:, t0:t0 + 512], func=AF.Square), reads=[ydk(t0)], writes=[ak])
        P.op("dve", lambda e, a=a: e.tensor_scalar(out=a[:], in0=a[:], scalar1=0.044715, scalar2=1.0, op0=ALU.mult, op1=ALU.add), reads=[], writes=[ak])
        P.op("dve", lambda e, a=a, t0=t0: e.tensor_tensor(out=a[:], in0=a[:], in1=yd[:, t0:t0 + 512], op=ALU.mult), reads=[ydk(t0)], writes=[ak])
        P.op("act", lambda e, a=a, sg=sg: e.activation(out=sg[:], in_=a[:], func=AF.Sigmoid, scale=1.5957691216057308), reads=[ak], writes=[sk])
        P.op("dve", lambda e, sg=sg, t0=t0: e.tensor_tensor(out=sg[:], in0=sg[:], in1=yd[:, t0:t0 + 512], op=ALU.mult), reads=[ydk(t0)], writes=[sk])
        P.op(("sp", "qa")[i % 2], lambda e, sg=sg, t0=t0: e.dma_start(out=gT[:, t0:t0 + 512], in_=sg[:]), reads=[sk])
    return kb.finish()


def attn_consts():
    m = np.arange(128)[:, None]
    n = np.arange(512)[None, :]
    wm = np.zeros((8, 128, 512), np.float32)
    for rel in range(-4, 0):
        wm[rel + 4] = np.where((128 * rel + m) <= n - 512, NEG, 0.0)
    for rel in range(0, 4):
        wm[rel + 4] = np.where((128 * rel + m) > n, NEG, 0.0)
    wmask = np.ascontiguousarray(wm.transpose(1, 0, 2)).astype(NPBF)
    j = np.arange(64)[:, None, None]
    kt = np.arange(32)[None, :, None]
    mm = np.arange(128)[None, None, :]
    ekt = np.where(j == 2 * kt + mm // 64, NEG, 0.0).astype(NPBF)
    r = np.arange(128)[:, None]
    jj = np.arange(503)[None, :]
    cmpw = np.where(16 * (jj - 248) + 31 <= r, 0.0, NEG).astype(np.float32)
    jj = np.arange(126)[None, :]
    rel = (jj - 62) - (r >= 64)
    keepw = np.where((rel > 0) | (rel == 0) | (rel == -1), 0.0, 1.0).astype(np.float32)
    addw = np.where(rel > 0, -1e30, np.where((rel == 0) | (rel == -1), 1e6, 0.0)).astype(np.float32)
    onehot = np.zeros((12, 12, 128), np.float32)
    for g in range(12):
        onehot[g, g, :] = 1.0
    return dict(wmask=wmask, ekt=ekt, cmpw=cmpw, keepw=keepw, addw=addw, onehot=onehot,
                identf=np.eye(128, dtype=np.float32), identb=np.eye(128).astype(NPBF),
                onesb=np.ones((128, 128)).astype(NPBF))


def build_attn(L=4096):
    kb = KB()
    P = kb.P
    nc = kb.nc
    NQB = L // 512
    NKT = L // 128
    NC = (L - 32) // 16 + 1
    NC1 = NC - 128
    NSB = L // 64
    qT = kb.dram_in("qT", [128, 4, L], BF16)
    szT = kb.dram_in("szT", [128, 12, L], BF16)
    gateT = kb.dram_in("gateT", [12, L], F32)
    kcT = kb.dram_in("kcT", [128, L], BF16)
    vcT = kb.dram_in("vcT", [128, L], BF16)
    ksT = kb.dram_in("ksT", [128, L], BF16)
    kwT = kb.dram_in("kwT", [128, L], BF16)
    vs_d = kb.dram_in("vs", [128, NKT, 128], BF16)
    vw_d = kb.dram_in("vw", [128, NKT, 128], BF16)
    posk = kb.dram_in("poskT", [128, 32], F32)
    posv = kb.dram_in("posvT", [128, 32], F32)
    w1k = kb.dram_in("w1k", [128, 32, 128], F32)
    w1v = kb.dram_in("w1v", [128, 32, 128], F32)
    w2k = kb.dram_in("w2k", [128, 128], F32)
    w2v = kb.dram_in("w2v", [128, 128], F32)
    c_wmask = kb.dram_in("wmask", [128, 8, 512], BF16)
    c_ekt = kb.dram_in("ekt", [64, 32, 128], BF16)
    c_cmpw = kb.dram_in("cmpw", [128, 503], F32)
    c_keepw = kb.dram_in("keepw", [128, 126], F32)
    c_addw = kb.dram_in("addw", [128, 126], F32)
    c_onehot = kb.dram_in("onehot", [12, 12, 128], F32)
    c_identf = kb.dram_in("identf", [128, 128], F32)
    c_identb = kb.dram_in("identb", [128, 128], BF16)
    c_onesb = kb.dram_in("onesb", [128, 128], BF16)
    yT = kb.dram_out("yT", [128, 4, L], BF16)
    lq = [0]

    def load(name, src, shape, dt, st=None):
        t = (st or kb.st).enter_context(nc.sbuf_tensor(name, list(shape), dt))
        q = ("sp", "qa")[lq[0] % 2]
        lq[0] += 1
        P.op(q, lambda e: e.dma_start(out=t[:], in_=src), writes=[name])
        return t

    st0 = contextlib.ExitStack()

    def sb0(name, shape, dt):
        return st0.enter_context(nc.sbuf_tensor(name, list(shape), dt))

    ks_s = load("ks_s", ksT[:, :], [128, L], BF16)
    kw_s = load("kw_s", kwT[:, :], [128, L], BF16)
    vs_s = load("vs_s", vs_d[:, :, :], [128, NKT, 128], BF16)
    vw_s = load("vw_s", vw_d[:, :, :], [128, NKT, 128], BF16)
    wmask = load("wmask_s", c_wmask[:, :, :], [128, 8, 512], BF16)
    ekt = load("ekt_s", c_ekt[:, :, :], [64, 32, 128], BF16)
    cmpw = load("cmpw_s", c_cmpw[:, :], [128, 503], F32)
    keepw = load("keepw_s", c_keepw[:, :], [128, 126], F32)
    addw = load("addw_s", c_addw[:, :], [128, 126], F32)
    onehot = load("onehot_s", c_onehot[:, :, :], [12, 12, 128], F32)
    identf = load("identf_s", c_identf[:, :], [128, 128], F32)
    identb = load("identb_s", c_identb[:, :], [128, 128], BF16)
    onesb = load("onesb_s", c_onesb[:, :], [128, 128], BF16)
    kcmpT = kb.sb("kcmpT", [128, NC], BF16)
    vcmp = kb.sb("vcmp", [128, 2, 128], BF16)
    kc_s = load("kc_s", kcT[:, :], [128, L], BF16, st0)
    vc_s = load("vc_s", vcT[:, :], [128, L], BF16, st0)
    posk_s = load("posk_s", posk[:, :], [128, 32], F32, st0)
    posv_s = load("posv_s", posv[:, :], [128, 32], F32, st0)
    w2k_s = load("w2k_s", w2k[:, :], [128, 128], F32, st0)
    w2v_s = load("w2v_s", w2v[:, :], [128, 128], F32, st0)
    w1st = sb0("w1st", [128, 32, 128], F32)
    w1b = sb0("w1b", [128, 32, 128], BF16)
    w2b = sb0("w2b", [128, 128], BF16)
    tmpl = [sb0(f"tmpl{i}", [128, NC], BF16) for i in range(4)]
    h1 = sb0("h1", [128, NC], F32)
    h2 = sb0("h2", [128, NC], F32)
    hg = sb0("hg", [128, NC], BF16)

    pss = [kb.ps(f"ps{i}", [128, 512], F32) for i in range(8)]
    SROT = [0, 1]
    PS_O = [2, 3]
    PS_S = [4, 5]
    PS_C = [6, 7]
    srot = [0]
    crot = [0]

    def ps_s_next():
        i = SROT[srot[0] % 2]
        srot[0] += 1
        return pss[i], f"ps{i}"

    def ps_c_next():
        i = PS_C[crot[0] % 2]
        crot[0] += 1
        return pss[i], f"ps{i}"

    for side, (src_s, pos_s, w1d, w2s) in enumerate([(kc_s, posk_s, w1k, w2k_s), (vc_s, posv_s, w1v, w2v_s)]):
        srck = "kc_s" if side == 0 else "vc_s"
        posn = "posk_s" if side == 0 else "posv_s"
        w2n = "w2k_s" if side == 0 else "w2v_s"
        P.op("sp", lambda e, w1d=w1d: e.dma_start(out=w1st[:], in_=w1d[:, :, :]), writes=["w1st"])
        P.op("pool", lambda e: e.tensor_copy(out=w1b[:], in_=w1st[:]), reads=["w1st"], writes=["w1b"])
        P.op("pool", lambda e, w2s=w2s: e.tensor_copy(out=w2b[:], in_=w2s[:]), reads=[w2n], writes=["w2b"])
        ps, pk = ps_c_next()
        for l in range(32):
            tl, tk = tmpl[l % 4], f"tmpl{l % 4}"
            P.op("dve", lambda e, tl=tl, l=l, src_s=src_s, pos_s=pos_s: e.tensor_scalar(
                out=tl[:], in0=src_s[:, l:l + 16 * (NC - 1) + 1:16], scalar1=pos_s[:, l:l + 1], scalar2=None, op0=ALU.add),
                reads=[srck, posn], writes=[tk])
            P.op("pe", lambda e, ps=ps, tl=tl, l=l: e.matmul(ps[:, 0:NC], w1b[:, l, :], tl[:], start=(l == 0), stop=(l == 31)),
                 reads=["w1b", tk], writes=[pk])
        P.op("act", lambda e, ps=ps: e.copy(out=h1[:], in_=ps[:, 0:NC]), reads=[pk], writes=["h1"])
        P.op("act", lambda e: e.activation(out=h2[:], in_=h1[:], func=AF.Square), reads=["h1"], writes=["h2"])
        P.op("dve", lambda e: e.tensor_scalar(out=h2[:], in0=h2[:], scalar1=0.044715, scalar2=1.0, op0=ALU.mult, op1=ALU.add), writes=["h2"])
        P.op("dve", lambda e: e.tensor_tensor(out=h2[:], in0=h2[:], in1=h1[:], op=ALU.mult), reads=["h1"], writes=["h2"])
        P.op("act", lambda e: e.activation(out=h2[:], in_=h2[:], func=AF.Sigmoid, scale=1.5957691216057308), writes=["h2"])
        P.op("dve", lambda e: e.tensor_tensor(out=hg[:], in0=h2[:], in1=h1[:], op=ALU.mult), reads=["h1", "h2"], writes=["hg"])
        if side == 0:
            ps, pk = ps_c_next()
            P.op("pe", lambda e, ps=ps: e.matmul(ps[:, 0:NC], w2b[:], hg[:], start=True, stop=True), reads=["w2b", "hg"], writes=[pk])
            P.op("act", lambda e, ps=ps: e.copy(out=kcmpT[:], in_=ps[:, 0:NC]), reads=[pk], writes=["kcmpT"])
        else:
            for t2, (n0, nn) in enumerate([(0, 128), (128, NC1)]):
                ps, pk = ps_c_next()
                P.op("pe", lambda e, ps=ps, n0=n0, nn=nn: e.matmul(ps[0:nn, 0:128], hg[:, n0:n0 + nn], w2b[:], start=True, stop=True),
                     reads=["w2b", "hg"], writes=[pk])
                P.op("act", lambda e, ps=ps, nn=nn, t2=t2: e.copy(out=vcmp[0:nn, t2, :], in_=ps[0:nn, 0:128]), reads=[pk], writes=["vcmp"])
    P.barrier()
    st0.close()

    qb_s = [kb.sb(f"qb{i}", [128, 4, 512], BF16) for i in range(2)]
    sz_s = [kb.sb(f"sz{i}", [128, 12, 512], BF16) for i in range(2)]
    gt_s = [kb.sb(f"gt{i}", [12, 512], F32) for i in range(2)]
    gbc = [kb.sb(f"gbc{i}", [128, 12, 512], F32) for i in range(2)]
    sc = kb.sb("sc", [128, NC], F32)
    pe_ = kb.sb("pe_", [128, NC], F32)
    pn = [kb.sb(f"pn{i}", [128, NC], F32) for i in range(2)]
    ppad = kb.sb("ppad", [128, 260], F32)
    rs = kb.sb("rs", [128, 1], F32)
    pnT = [kb.sb(f"pnT{i}", [128, 2, 128], BF16) for i in range(2)]
    ocmp = [kb.sb(f"ocmp{i}", [128, 4, 512], F32) for i in range(2)]
    imp = kb.sb("imp", [128, NSB], F32)
    imp2 = kb.sb("imp2", [128, NSB], F32)
    v8a = kb.sb("v8a", [128, 8], F32)
    v8b = kb.sb("v8b", [128, 8], F32)
    nsel = kb.sb("nsel", [128, NSB], F32)
    nsT = [kb.sb(f"nsT{i}", [64, 512], BF16) for i in range(2)]
    pT = [kb.sb(f"pT{i}", [128, 512], BF16) for i in range(3)]
    osb = kb.sb("osb", [128, 512], F32)
    rden = kb.sb("rden", [128, 512], F32)
    wgt = kb.sb("wgt", [128, 512], F32)
    tt1 = kb.sb("tt1", [128, 512], F32)
    acc = kb.sb("acc", [128, 4, 512], F32)
    yb = [kb.sb(f"ybo{i}", [128, 4, 512], BF16) for i in range(2)]
    P.op("pool", lambda e: e.memset(ppad[:], 0.0), writes=["ppad"])
    pti = [0]

    def emit_cmp(qb):
        par = qb % 2
        q0 = qb * 512
        qs, qk = qb_s[par], f"qb{par}"
        szs, szk = sz_s[par], f"sz{par}"
        gts, gk = gt_s[par], f"gt{par}"
        P.op("sp", lambda e: e.dma_start(out=qs[:], in_=qT[:, :, q0:q0 + 512]), writes=[qk])
        P.op("qa", lambda e: e.dma_start(out=szs[:], in_=szT[:, :, q0:q0 + 512]), writes=[szk])
        P.op("sp", lambda e: e.dma_start(out=gts[:], in_=gateT[:, q0:q0 + 512]), writes=[gk])
        for gi in range(12):
            ps, pk = ps_c_next()
            P.op("pe", lambda e, ps=ps, gi=gi: e.matmul(ps[:], onehot[:, gi, :], gts[:], start=True, stop=True), reads=["onehot_s", gk], writes=[pk])
            P.op("act", lambda e, ps=ps, gi=gi: e.copy(out=gbc[par][:, gi, :], in_=ps[:]), reads=[pk], writes=[f"gbc{par}"])
        for qt in range(4):
            it = qb * 4 + qt
            c0 = 248 - 8 * it
            for h in range(4):
                ps, pk = ps_c_next()
                P.op("pe", lambda e, ps=ps, h=h, qt=qt: e.matmul(ps[:, 0:NC], qs[:, h, qt * 128:(qt + 1) * 128], kcmpT[:], start=True, stop=True),
                     reads=[qk, "kcmpT"], writes=[pk])
                P.op("dve", lambda e, ps=ps, c0=c0: e.tensor_tensor(out=sc[:], in0=ps[:, 0:NC], in1=cmpw[:, c0:c0 + NC], op=ALU.add),
                     reads=[pk, "cmpw_s"], writes=["sc"])
                P.op("act", lambda e: e.activation(out=pe_[:], in_=sc[:], func=AF.Exp, accum_out=rs[:]), reads=["sc"], writes=["pe_", "rs"])
                P.op("dve", lambda e: e.tensor_scalar(out=rs[:], in0=rs[:], scalar1=1e-30, scalar2=None, op0=ALU.max), writes=["rs"])
                P.op("dve", lambda e: e.reciprocal(out=rs[:], in_=rs[:]), writes=["rs"])
                pnh, pnk = pn[h % 2], f"pn{h % 2}"
                P.op("dve", lambda e, pnh=pnh: e.tensor_scalar(out=pnh[:], in0=pe_[:], scalar1=rs[:], scalar2=None, op0=ALU.mult),
                     reads=["pe_", "rs"], writes=[pnk])
                if h == 0:
                    P.op("pool", lambda e, pnh=pnh: e.tensor_copy(out=ppad[:, 1:1 + NC], in_=pnh[:]), reads=[pnk], writes=["ppad"])
                else:
                    P.op("pool", lambda e, pnh=pnh: e.tensor_tensor(out=ppad[:, 1:1 + NC], in0=ppad[:, 1:1 + NC], in1=pnh[:], op=ALU.add),
                         reads=[pnk], writes=["ppad"])
                ps2, pk2 = ps_c_next()
                P.op("pe", lambda e, ps2=ps2, pnh=pnh: e.transpose(ps2[:, 0:128], pnh[:, 0:128], identf[:]), reads=[pnk, "identf_s"], writes=[pk2])
                P.op("pe", lambda e, ps2=ps2, pnh=pnh: e.transpose(ps2[0:NC1, 128:256], pnh[:, 128:NC], identf[:]), reads=[pnk, "identf_s"], writes=[pk2])
                pt_, ptk = pnT[h % 2], f"pnT{h % 2}"
                P.op("act", lambda e, ps2=ps2, pt_=pt_: e.copy(out=pt_[:, 0, :], in_=ps2[:, 0:128]), reads=[pk2], writes=[ptk])
                P.op("act", lambda e, ps2=ps2, pt_=pt_: e.copy(out=pt_[0:NC1, 1, :], in_=ps2[0:NC1, 128:256]), reads=[pk2], writes=[ptk])
                ps3, pk3 = ps_c_next()
                P.op("pe", lambda e, ps3=ps3, pt_=pt_: e.matmul(ps3[:, 0:128], vcmp[:, 0, :], pt_[:, 0, :], start=True, stop=False),
                     reads=["vcmp", ptk], writes=[pk3])
                P.op("pe", lambda e, ps3=ps3, pt_=pt_: e.matmul(ps3[:, 0:128], vcmp[0:NC1, 1, :], pt_[0:NC1, 1, :], start=False, stop=True),
                     reads=["vcmp", ptk], writes=[pk3])
                P.op("act", lambda e, ps3=ps3, h=h, qt=qt: e.copy(out=ocmp[par][:, h, qt * 128:(qt + 1) * 128], in_=ps3[:, 0:128]),
                     reads=[pk3], writes=[f"ocmp{par}"])
            P.op("dve", lambda e: e.tensor_tensor(out=imp[:], in0=ppad[:, 0:256:4], in1=ppad[:, 1:257:4], op=ALU.add), reads=["ppad"], writes=["imp"])
            for m_ in (2, 3, 4):
                P.op("dve", lambda e, m_=m_: e.tensor_tensor(out=imp[:], in0=imp[:], in1=ppad[:, m_:m_ + 256:4], op=ALU.add), reads=["ppad"], writes=["imp"])
            w0 = 62 - 2 * it
            P.op("dve", lambda e, w0=w0: e.tensor_tensor(out=imp2[:], in0=imp[:], in1=keepw[:, w0:w0 + NSB], op=ALU.mult), reads=["imp", "keepw_s"], writes=["imp2"])
            P.op("dve", lambda e, w0=w0: e.tensor_tensor(out=imp2[:], in0=imp2[:], in1=addw[:, w0:w0 + NSB], op=ALU.add), reads=["addw_s"], writes=["imp2"])
            P.op("dve", lambda e: e.memset(imp2[:, 0:1], 1e6), writes=["imp2"])
            P.op("dve", lambda e: e.max(out=v8a[:], in_=imp2[:]), reads=["imp2"], writes=["v8a"])
            P.op("dve", lambda e: e.match_replace(out=imp[:], in_to_replace=v8a[:], in_values=imp2[:], imm_value=-3.0e38), reads=["imp2", "v8a"], writes=["imp"])
            P.op("dve", lambda e: e.max(out=v8b[:], in_=imp[:]), reads=["imp"], writes=["v8b"])
            P.op("dve", lambda e: e.tensor_scalar(out=nsel[:], in0=imp2[:], scalar1=v8b[:, 7:8], scalar2=None, op0=ALU.is_lt), reads=["imp2", "v8b"], writes=["nsel"])
            ps4, pk4 = ps_c_next()
            P.op("pe", lambda e, ps4=ps4: e.transpose(ps4[0:NSB, 0:128], nsel[:], identf[:]), reads=["nsel", "identf_s"], writes=[pk4])
            P.op("act", lambda e, ps4=ps4, qt=qt: e.copy(out=nsT[par][:, qt * 128:(qt + 1) * 128], in_=ps4[0:NSB, 0:128]), reads=[pk4], writes=[f"nsT{par}"])

    def emit_att(qb):
        par = qb % 2
        q0 = qb * 512
        qs, qk = qb_s[par], f"qb{par}"
        szs, szk = sz_s[par], f"sz{par}"
        nst, nsk = nsT[par], f"nsT{par}"
        jobs = []
        grp = 0
        for h in range(4):
            for br in (1, 2):
                kts = list(range(0, 4 * (qb + 1))) if br == 1 else list(range(max(0, 4 * qb - 4), 4 * qb + 4))
                for ki, kt in enumerate(kts):
                    jobs.append((h, br, kt, ki, len(kts), grp))
                grp += 1
        state = {}

        def emit_S(job):
            h, br, kt, ki, nk, g = job
            ps, pk = ps_s_next()
            rel = kt - 4 * qb
            if br == 1:
                diag = rel >= 0
                P.op("pe", lambda e: e.matmul(ps[:], ks_s[:, kt * 128:(kt + 1) * 128], qs[:, h, :], start=True, stop=False), reads=["ks_s", qk], writes=[pk])
                P.op("pe", lambda e: e.matmul(ps[:], ekt[:, kt, :], nst[:], start=False, stop=(not diag)), reads=["ekt_s", nsk], writes=[pk])
                if diag:
                    P.op("pe", lambda e: e.matmul(ps[:], identb[:], wmask[:, rel + 4, :], start=False, stop=True), reads=["identb_s", "wmask_s"], writes=[pk])
            else:
                P.op("pe", lambda e: e.matmul(ps[:], kw_s[:, kt * 128:(kt + 1) * 128], qs[:, h, :], start=True, stop=False), reads=["kw_s", qk], writes=[pk])
                P.op("pe", lambda e: e.matmul(ps[:], identb[:], wmask[:, rel + 4, :], start=False, stop=True), reads=["identb_s", "wmask_s"], writes=[pk])
            state[job] = (ps, pk)

        def emit_PV(job):
            h, br, kt, ki, nk, g = job
            ps, pk = state.pop(job)
            pt_, ptk = pT[pti[0] % 3], f"pT{pti[0] % 3}"
            pti[0] += 1
            v_s, vname = (vs_s, "vs_s") if br == 1 else (vw_s, "vw_s")
            pso, pko = pss[PS_O[g % 2]], f"ps{PS_O[g % 2]}"
            pss_, pks = pss[PS_S[g % 2]], f"ps{PS_S[g % 2]}"
            P.op("act", lambda e: e.activation(out=pt_[:], in_=ps[:], func=AF.Exp), reads=[pk], writes=[ptk])
            P.op("pe", lambda e: e.matmul(pso[:], v_s[:, kt, :], pt_[:], start=(ki == 0), stop=(ki == nk - 1)), reads=[vname, ptk], writes=[pko])
            P.op("pe", lambda e: e.matmul(pss_[:], onesb[:], pt_[:], start=(ki == 0), stop=(ki == nk - 1)), reads=["onesb_s", ptk], writes=[pks])
            if ki == nk - 1:
                gi = br * 4 + h
                P.op("act", lambda e: e.copy(out=osb[:], in_=pso[:]), reads=[pko], writes=["osb"])
                P.op("dve", lambda e: e.reciprocal(out=rden[:], in_=pss_[:]), reads=[pks], writes=["rden"])
                P.op("dve", lambda e: e.tensor_tensor(out=wgt[:], in0=rden[:], in1=gbc[par][:, gi, :], op=ALU.mult), reads=["rden", f"gbc{par}"], writes=["wgt"])
                P.op("dve", lambda e: e.tensor_tensor(out=tt1[:], in0=osb[:], in1=wgt[:], op=ALU.mult), reads=["osb", "wgt"], writes=["tt1"])
                if br == 1:
                    P.op("dve", lambda e: e.tensor_tensor(out=acc[:, h, :], in0=tt1[:], in1=szs[:, gi, :], op=ALU.mult), reads=["tt1", szk], writes=["acc"])
                else:
                    P.op("dve", lambda e: e.tensor_tensor(out=tt1[:], in0=tt1[:], in1=szs[:, gi, :], op=ALU.mult), reads=[szk], writes=["tt1"])
                    P.op("pool", lambda e: e.tensor_tensor(out=acc[:, h, :], in0=acc[:, h, :], in1=tt1[:], op=ALU.add), reads=["tt1"], writes=["acc"])
                    ybo, ybk = yb[par], f"ybo{par}"
                    P.op("dve", lambda e: e.tensor_tensor(out=tt1[:], in0=ocmp[par][:, h, :], in1=gbc[par][:, h, :], op=ALU.mult), reads=[f"ocmp{par}", f"gbc{par}"], writes=["tt1"])
                    P.op("dve", lambda e: e.tensor_tensor(out=tt1[:], in0=tt1[:], in1=szs[:, h, :], op=ALU.mult), reads=[szk], writes=["tt1"])
                    P.op("pool", lambda e: e.tensor_tensor(out=ybo[:, h, :], in0=acc[:, h, :], in1=tt1[:], op=ALU.add), reads=["tt1", "acc"], writes=[ybk])

        emit_S(jobs[0])
        for i, job in enumerate(jobs):
            if i + 1 < len(jobs):
                emit_S(jobs[i + 1])
            emit_PV(job)
        ybo, ybk = yb[par], f"ybo{par}"
        P.op("sp", lambda e: e.dma_start(out=yT[:, :, q0:q0 + 512], in_=ybo[:]), reads=[ybk])

    emit_cmp(0)
    for qb in range(NQB):
        P.capture()
        emit_att(qb)
        sa = P.end_capture()
        sb_ = []
        if qb + 1 < NQB:
            P.capture()
            emit_cmp(qb + 1)
            sb_ = P.end_capture()
        P.replay(sa, sb_)
    return kb.finish()


def _c(a):
    return np.ascontiguousarray(a)


def scan_maps(inp, layer, uT):
    maps = []
    ident = np.eye(128, dtype=np.float32)
    for c in range(NCORES):
        g0 = 8 * c

        def pl(a):
            return _c(a.reshape(4, 2, 64).transpose(1, 2, 0).reshape(128, 4))

        lr = pl(inp["a_lam_re"][layer][g0:g0 + 8])
        li = pl(inp["a_lam_im"][layer][g0:g0 + 8])
        ldt = pl(np.repeat(inp["a_log_dt"][layer][g0:g0 + 8][:, None], 64, axis=1))
        br = _c(inp["a_b_re"][layer][g0:g0 + 8].reshape(4, 2, 64, 16).transpose(1, 2, 0, 3).reshape(128, 4, 16))
        bi = _c(inp["a_b_im"][layer][g0:g0 + 8].reshape(4, 2, 64, 16).transpose(1, 2, 0, 3).reshape(128, 4, 16))
        crT = _c(inp["a_c_re"][layer][g0:g0 + 8].reshape(4, 2, 16, 64).transpose(1, 3, 0, 2).reshape(128, 4, 16))
        ciT = _c(inp["a_c_im"][layer][g0:g0 + 8].reshape(4, 2, 16, 64).transpose(1, 3, 0, 2).reshape(128, 4, 16))
        dv = _c(inp["a_d"][layer][128 * c:128 * c + 128].reshape(128, 1))
        maps.append(dict(uT=_c(uT[128 * c:128 * c + 128]), lr=lr, li=li, ldt=ldt, br=br, bi=bi, crT=crT, ciT=ciT,
                         dvec=dv, ident=ident))
    return maps


def attn_maps(o16, o32, kvT, inp, consts):
    maps = []
    for c in range(NCORES):
        b, G = c // 4, c % 4
        ts = slice(b * SEQ, (b + 1) * SEQ)
        m = dict(consts)
        m["qT"] = _c(o16[512 * G:512 * G + 512, ts].reshape(4, 128, SEQ).transpose(1, 0, 2))
        szr = np.stack([o16[2048 + br * 2048 + 512 * G:2048 + br * 2048 + 512 * G + 512, ts].reshape(4, 128, SEQ)
                        for br in range(3)], 0)
        m["szT"] = _c(szr.transpose(2, 0, 1, 3).reshape(128, 12, SEQ))
        m["gateT"] = _c(np.stack([o32[br * 16 + 4 * G:br * 16 + 4 * G + 4, ts] for br in range(3)], 0).reshape(12, SEQ))

        def slot(s_):
            return kvT[s_ * 512 + G * 128:s_ * 512 + G * 128 + 128, ts]

        m["kcT"] = _c(slot(0))
        m["vcT"] = _c(slot(1))
        m["ksT"] = _c(slot(2))
        m["kwT"] = _c(slot(4))
        m["vs"] = _c(slot(3).T.reshape(SEQ // 128, 128, 128).transpose(1, 0, 2))
        m["vw"] = _c(slot(5).T.reshape(SEQ // 128, 128, 128).transpose(1, 0, 2))
        m["poskT"] = _c(inp["cmp_pos_k"].T)
        m["posvT"] = _c(inp["cmp_pos_v"].T)
        m["w1k"] = _c(inp["cmp_w1_k"].reshape(32, 128, 128).transpose(1, 0, 2))
        m["w1v"] = _c(inp["cmp_w1_v"].reshape(32, 128, 128).transpose(1, 0, 2))
        m["w2k"] = _c(inp["cmp_w2_k"])
        m["w2v"] = _c(inp["cmp_w2_v"])
        maps.append(m)
    return maps


def kernel(**inputs):
    inp = {k: np.asarray(v) for k, v in inputs.items()}
    T = BATCH * SEQ // NCORES
    x = _c(inp["x"].reshape(BATCH * SEQ, D_MODEL).astype(np.float32))

    def tok(a, c):
        return a[c * T:(c + 1) * T]

    def bc(v):
        return _c(np.broadcast_to(v, (128, D_MODEL)))

    tiles_a = [(128, "id", "o32")] * 8 + [(128, "silu", "o32")] * 8
    tiles_kv = [(128, "id", "o16")] * 24
    tiles_b = [(128, "scale", "o16")] * 16 + [(128, "silu", "o16")] * 48 + [(48, "sigmoid", "o32")]
    nc_proj_a = build_proj(16, T, tiles_a)
    nc_scan = build_scan()
    nc_tail_a = build_tail(8, T, True)
    for layer in range(2):
        wa = proj_host_w(inp["a_w_in"][layer], tiles_a)
        res = run(nc_proj_a, [dict(xT=host_fm(tok(x, c).T), w=wa) for c in range(NCORES)])
        o = np.concatenate([res[c]["o32"] for c in range(NCORES)], axis=1)
        uT, szT = o[:1024], o[1024:]
        res = run(nc_scan, scan_maps(inp, layer, uT))
        gT = np.concatenate([res[c]["gT"] for c in range(NCORES)], axis=0)
        bglu = _c(inp["a_b_glu"][layer].reshape(8, 128).T)
        res = run(nc_tail_a, [dict(x=_c(tok(x, c)), w=host_fm(inp["a_w_out"][layer]), lng=bc(inp["ln_g"][layer]), lnb=bc(inp["ln_b"][layer]),
                                   gT=host_fm(gT[:, c * T:(c + 1) * T]), szT=host_fm(szT[:, c * T:(c + 1) * T]),
                                   wglu=host_fm(inp["a_w_glu"][layer]), bglu=bglu) for c in range(NCORES)])
        x = np.concatenate([res[c]["xo"] for c in range(NCORES)], axis=0)
    nc_proj_kv = build_proj(16, T, tiles_kv)
    wkv = proj_host_w(inp["kv_w"], tiles_kv)
    res = run(nc_proj_kv, [dict(xT=host_fm(tok(x, c).T), w=wkv) for c in range(NCORES)])
    kvT = np.concatenate([res[c]["o16"] for c in range(NCORES)], axis=1)
    nc_proj_b = build_proj(16, T, tiles_b)
    nc_attn = build_attn()
    nc_tail_b = build_tail(16, T, False, BF16)
    consts = attn_consts()
    for li in range(2):
        layer = 2 + li
        wbi = proj_host_w(inp["b_w_in"][li], tiles_b)
        res = run(nc_proj_b, [dict(xT=host_fm(tok(x, c).T), w=wbi) for c in range(NCORES)])
        o16 = np.concatenate([res[c]["o16"] for c in range(NCORES)], axis=1)
        o32 = np.concatenate([res[c]["o32"] for c in range(NCORES)], axis=1)
        res = run(nc_attn, attn_maps(o16, o32, kvT, inp, consts))
        yT = np.concatenate([np.concatenate([res[b * 4 + G]["yT"].transpose(1, 0, 2).reshape(512, SEQ) for G in range(4)], axis=0)
                             for b in range(BATCH)], axis=1)
        res = run(nc_tail_b, [dict(x=_c(tok(x, c)), w=host_fm(inp["b_w_out"][li]), lng=bc(inp["ln_g"][layer]), lnb=bc(inp["ln_b"][layer]),
                                   yT=host_fm(yT[:, c * T:(c + 1) * T])) for c in range(NCORES)])
        x = np.concatenate([res[c]["xo"] for c in range(NCORES)], axis=0)
    return x.reshape(BATCH, SEQ, D_MODEL).astype(np.float32)
```

```python
import contextlib
import numpy as np
import ml_dtypes
import concourse.bass as bass
import concourse.mybir as mybir
from concourse.bass_utils import run_bass_kernel_spmd

F32 = mybir.dt.float32
BF16 = mybir.dt.bfloat16
AF = mybir.ActivationFunctionType
ALU = mybir.AluOpType
NPBF = ml_dtypes.bfloat16

D_MODEL = 2048
BATCH = 2
SEQ = 4096
NCORES = 8
LN_EPS = 1e-5
ALPHA = 8 ** 0.25
SCALE = 128 ** -0.5
NEG = -30000.0


class Prog:
    COMPUTE = ["pe", "act", "dve", "pool"]
    QUEUES = {"sp": "sp", "qa": "act", "qp": "pool"}
    STREAMS = ["pe", "act", "dve", "pool", "sp"]
    NSLOT = 6

    def __init__(self, nc, same_sync=True):
        self.nc = nc
        self.owners = list(self.COMPUTE) + [f"{q}{j}" for q in self.QUEUES for j in range(self.NSLOT)]
        self.stream = {e: e for e in self.COMPUTE}
        for q, st in self.QUEUES.items():
            for j in range(self.NSLOT):
                self.stream[f"{q}{j}"] = st
        self.inc = {e: (1 if e in self.COMPUTE else 16) for e in self.owners}
        self.q = {e: [] for e in self.STREAMS}
        self.cnt = {e: 0 for e in self.owners}
        self.seen = {st: {o: 0 for o in self.owners} for st in self.STREAMS}
        self.qn = {q: 0 for q in self.QUEUES}
        self.lastw = {}
        self.reads = {}
        self.same_sync = same_sync
        self.seq = 0
        self.cap = None

    def _need(self, eng, other, idx):
        st = self.stream[eng]
        if other == eng and eng == "pe":
            return
        if other == eng and not self.same_sync and eng in self.COMPUTE:
            return
        if self.seen[st][other] >= idx:
            return
        self.seen[st][other] = idx
        self.seq += 1
        self.q[st].append(("wait", other, idx, self.seq))

    def barrier(self):
        rep = {"pe": "pe", "act": "act", "dve": "dve", "pool": "pool", "sp": "sp0"}
        for st in self.STREAMS:
            for o in self.owners:
                if self.cnt[o] > 0:
                    self._need(rep[st], o, self.cnt[o])

    def capture(self):
        self.cap = []
        return self.cap

    def end_capture(self):
        c, self.cap = self.cap, None
        return c

    def replay(self, *streams):
        pos = [0] * len(streams)
        tot = [max(1, len(x)) for x in streams]
        n = sum(len(x) for x in streams)
        for _ in range(n):
            best, bi = None, None
            for i, x in enumerate(streams):
                if pos[i] < len(x):
                    f = pos[i] / tot[i]
                    if best is None or f < best:
                        best, bi = f, i
            a = streams[bi][pos[bi]]
            pos[bi] += 1
            self.op(*a)

    def op(self, eng, fn, reads=(), writes=()):
        if self.cap is not None:
            self.cap.append((eng, fn, tuple(reads), tuple(writes)))
            return None
        if eng in self.QUEUES:
            j = self.qn[eng] % self.NSLOT
            self.qn[eng] += 1
            eng = f"{eng}{j}"
            if self.cnt[eng] > 0:
                self._need(eng, eng, self.cnt[eng])
        for k in reads:
            w = self.lastw.get(k)
            if w is not None:
                self._need(eng, *w)
        for k in writes:
            w = self.lastw.get(k)
            if w is not None:
                self._need(eng, *w)
            for (o, i) in self.reads.get(k, {}).items():
                self._need(eng, o, i)
        self.cnt[eng] += 1
        idx = self.cnt[eng]
        self.seq += 1
        self.q[self.stream[eng]].append(("op", fn, idx, self.seq, eng))
        for k in reads:
            self.reads.setdefault(k, {})[eng] = idx
        for k in writes:
            self.lastw[k] = (eng, idx)
            self.reads[k] = {}
        return idx

    def emit(self):
        nc = self.nc
        with contextlib.ExitStack() as st:
            sems = {e: st.enter_context(nc.semaphore("sem_" + e)) for e in self.owners}
            block = st.enter_context(nc.Block())
            engobj = {"pe": block.tensor, "act": block.scalar, "dve": block.vector,
                      "pool": block.gpsimd, "sp": block.sync}
            for e in self.STREAMS:
                items = self.q[e]

                def body(engine, items=items, e=e):
                    for it in items:
                        if it[0] == "wait":
                            _, o, idx, _s = it
                            engine.wait_ge(sems[o], idx * self.inc[o])
                        else:
                            _, fn, idx, _s, owner = it
                            ins = fn(engine)
                            ins.then_inc(sems[owner], self.inc[owner])
                    if e == "sp":
                        for o in self.owners:
                            if self.cnt[o] > 0:
                                engine.wait_ge(sems[o], self.cnt[o] * self.inc[o])
                engobj[e](body)


class KB:
    def __init__(self, name="k"):
        self.nc = bass.Bass("TRN2", target_bir_lowering=False)
        self.st = contextlib.ExitStack()
        self.P = Prog(self.nc)
        self.nps = 0
        self.uid = 0

    def dram_in(self, name, shape, dt=F32):
        return self.nc.dram_tensor(name, list(shape), dt, kind="ExternalInput").ap()

    def dram_out(self, name, shape, dt=F32):
        return self.nc.dram_tensor(name, list(shape), dt, kind="ExternalOutput").ap()

    def sb(self, name, shape, dt=F32):
        return self.st.enter_context(self.nc.sbuf_tensor(name, list(shape), dt))

    def ps(self, name, shape, dt=F32):
        return self.st.enter_context(self.nc.psum_tensor(name, list(shape), dt))

    def finish(self):
        self.P.emit()
        self.st.close()
        return self.nc


def run(nc, in_maps):
    res = run_bass_kernel_spmd(nc, in_maps, core_ids=list(range(len(in_maps))))
    return res.results


def act_apply(e, out, in_, act):
    if act == "id":
        return e.copy(out=out, in_=in_)
    if act == "scale":
        return e.mul(out=out, in_=in_, mul=SCALE)
    if act == "silu":
        return e.activation(out=out, in_=in_, func=AF.Silu)
    if act == "sigmoid":
        return e.activation(out=out, in_=in_, func=AF.Sigmoid)
    raise ValueError(act)


def proj_blocks(tiles, WB=512):
    blocks, cur, curw = [], [], 0
    for t in tiles:
        if curw + t[0] > WB:
            blocks.append(cur)
            cur, curw = [], 0
        cur.append(t)
        curw += t[0]
    if cur:
        blocks.append(cur)
    return blocks


def proj_host_w(w, tiles, WB=512):
    K_, N = w.shape
    KC = K_ // 128
    blocks = proj_blocks(tiles, WB)
    out = np.zeros((len(blocks), 128, KC, WB), np.float32)
    c0 = 0
    for bi, tl in enumerate(blocks):
        bw = sum(t[0] for t in tl)
        out[bi, :, :, :bw] = w[:, c0:c0 + bw].reshape(KC, 128, bw).transpose(1, 0, 2)
        c0 += bw
    return out


def host_fm(aT):
    K_, T = aT.shape
    return np.ascontiguousarray(aT.reshape(K_ // 128, 128, T).transpose(1, 0, 2))


def build_proj(KC, T, tiles):
    kb = KB()
    P = kb.P
    WB = 512
    blocks = proj_blocks(tiles, WB)
    n32 = sum(t[0] for t in tiles if t[2] == "o32")
    n16 = sum(t[0] for t in tiles if t[2] == "o16")
    xT = kb.dram_in("xT", [128, KC, T])
    w = kb.dram_in("w", [len(blocks), 128, KC, WB])
    outs = {}
    if n32:
        outs["o32"] = kb.dram_out("o32", [n32, T], F32)
    if n16:
        outs["o16"] = kb.dram_out("o16", [n16, T], BF16)
    xb = kb.sb("xb", [128, KC, T], BF16)
    KG = 4
    LQ = ["sp", "qa", "qp"]
    xst = [kb.sb(f"xst{i}", [128, KG, T], F32) for i in range(2)]
    wst = [kb.sb(f"wst{i}", [128, KC, WB], F32) for i in range(2)]
    wb = [kb.sb(f"wb{i}", [128, KC, WB], BF16) for i in range(2)]
    o32s = [kb.sb(f"o32s{i}", [128, 512], F32) for i in range(2)]
    o16s = [kb.sb(f"o16s{i}", [128, 512], BF16) for i in range(2)]
    pss = [kb.ps(f"ps{i}", [128, 512], F32) for i in range(8)]
    H = KC // 2

    def load_w(bi):
        bw = sum(t[0] for t in blocks[bi])
        ws, wbb = wst[bi % 2], wb[bi % 2]
        P.op(LQ[(2 * bi) % 3], lambda e: e.dma_start(out=ws[:, 0:H, 0:bw], in_=w[bi, :, 0:H, 0:bw]), writes=[f"wst{bi % 2}a"])
        P.op(LQ[(2 * bi + 1) % 3], lambda e: e.dma_start(out=ws[:, H:KC, 0:bw], in_=w[bi, :, H:KC, 0:bw]), writes=[f"wst{bi % 2}b"])
        P.op("dve", lambda e: e.tensor_copy(out=wbb[:, 0:H, 0:bw], in_=ws[:, 0:H, 0:bw]), reads=[f"wst{bi % 2}a"], writes=[f"wb{bi % 2}"])
        P.op("act", lambda e: e.copy(out=wbb[:, H:KC, 0:bw], in_=ws[:, H:KC, 0:bw]), reads=[f"wst{bi % 2}b"], writes=[f"wb{bi % 2}"])

    for i in range(KC // KG):
        s_ = xst[i % 2]
        P.op(LQ[i % 3], lambda e, s_=s_, i=i: e.dma_start(out=s_[:], in_=xT[:, i * KG:(i + 1) * KG, :]), writes=[f"xst{i % 2}"])
        P.op("dve" if i % 2 == 0 else "pool", lambda e, s_=s_, i=i: e.tensor_copy(out=xb[:, i * KG:(i + 1) * KG, :], in_=s_[:]),
             reads=[f"xst{i % 2}"], writes=["xb"])
    load_w(0)
    rowpos = {"o32": 0, "o16": 0}
    psi = 0
    oi = {"o32": 0, "o16": 0}
    for bi, tl in enumerate(blocks):
        if bi + 1 < len(blocks):
            load_w(bi + 1)
        wbb = wb[bi % 2]
        off = 0
        for (nc_, act, oname) in tl:
            for n in range(T // 512):
                ps = pss[psi % 8]
                pk = f"ps{psi % 8}"
                psi += 1
                for k in range(KC):
                    P.op("pe", lambda e, ps=ps, wbb=wbb, off=off, nc_=nc_, k=k, n=n: e.matmul(
                        ps[0:nc_, :], wbb[:, k, off:off + nc_], xb[:, k, n * 512:(n + 1) * 512],
                        start=(k == 0), stop=(k == KC - 1)),
                        reads=[f"wb{bi % 2}", "xb"], writes=[pk])
                j = oi[oname] % 2
                oi[oname] += 1
                osb = (o32s if oname == "o32" else o16s)[j]
                ok = f"{oname}s{j}"
                P.op("act", lambda e, osb=osb, ps=ps, nc_=nc_, act=act: act_apply(e, osb[0:nc_, :], ps[0:nc_, :], act),
                     reads=[pk], writes=[ok])
                r0 = rowpos[oname]
                od = outs[oname]
                P.op("sp", lambda e, od=od, osb=osb, r0=r0, nc_=nc_, n=n: e.dma_start(
                    out=od[r0:r0 + nc_, n * 512:(n + 1) * 512], in_=osb[0:nc_, :]),
                    reads=[ok])
            rowpos[oname] += nc_
            off += nc_
    return kb.finish()


def build_tail(KC, T, glu, ydt=F32):
    kb = KB()
    P = kb.P
    x = kb.dram_in("x", [T, D_MODEL])
    w_v = kb.dram_in("w", [128, KC, D_MODEL])
    lng = kb.dram_in("lng", [128, D_MODEL])
    lnb = kb.dram_in("lnb", [128, D_MODEL])
    xo = kb.dram_out("xo", [T, D_MODEL])
    wb = kb.sb("wb", [128, KC, D_MODEL], BF16)
    yb = kb.sb("yb", [128, KC, T], BF16)
    STF = KC * 256
    stg = [kb.sb(f"stg{i}", [128, STF], F32) for i in range(2)]
    lng_s = kb.sb("lng_s", [128, D_MODEL], F32)
    lnb_s = kb.sb("lnb_s", [128, D_MODEL], F32)
    xt = [kb.sb(f"xt{i}", [128, D_MODEL], F32) for i in range(2)]
    stats = kb.sb("stats", [128, 4, 6], F32)
    mv = kb.sb("mv", [128, 2], F32)
    rstd = kb.sb("rstd", [128, 1], F32)
    epsc = kb.sb("epsc", [128, 1], F32)
    pss = [kb.ps(f"ps{i}", [128, 512], F32) for i in range(8)]
    psi = [0]

    def nextps():
        i = psi[0] % 8
        psi[0] += 1
        return pss[i], f"ps{i}"

    P.op("sp", lambda e: e.dma_start(out=lng_s[:], in_=lng[:, :]), writes=["lng"])
    P.op("sp", lambda e: e.dma_start(out=lnb_s[:], in_=lnb[:, :]), writes=["lnb"])
    P.op("pool", lambda e: e.memset(epsc[:], LN_EPS), writes=["epsc"])
    si = [0]

    def load_cast(dst, dkey, src_v, ncols, engs=("dve", "act")):
        kc = src_v.shape[1]
        kg = max(1, STF // ncols)
        for k0 in range(0, kc, kg):
            kn = min(kg, kc - k0)
            j = si[0] % 2
            q = ("sp", "qa", "qp")[si[0] % 3]
            si[0] += 1
            sv = stg[j][:, 0:kn * ncols].rearrange("p (k n) -> p k n", k=kn)
            P.op(q, lambda e, sv=sv, k0=k0, kn=kn: e.dma_start(out=sv, in_=src_v[:, k0:k0 + kn, :]), writes=[f"stg{j}"])
            if engs[j % len(engs)] == "act":
                P.op("act", lambda e, sv=sv, k0=k0, kn=kn: e.copy(out=dst[:, k0:k0 + kn, 0:ncols], in_=sv), reads=[f"stg{j}"], writes=[dkey])
            else:
                P.op(engs[j % len(engs)], lambda e, sv=sv, k0=k0, kn=kn: e.tensor_copy(out=dst[:, k0:k0 + kn, 0:ncols], in_=sv),
                     reads=[f"stg{j}"], writes=[dkey])

    if glu:
        gT_v = kb.dram_in("gT", [128, KC, T])
        szT_v = kb.dram_in("szT", [128, KC, T])
        wglu_v = kb.dram_in("wglu", [128, KC, KC * 128])
        bglu = kb.dram_in("bglu", [128, KC])
        wgb = kb.sb("wgb", [128, KC, KC * 128], BF16)
        bg = kb.sb("bg", [128, KC], F32)
        g32 = kb.sb("g32", [128, KC, 512], F32)
        sz32 = kb.sb("sz32", [128, KC, 512], F32)
        gb = kb.sb("gb", [128, KC, 512], BF16)
        sig = [kb.sb(f"sig{i}", [128, 512], F32) for i in range(2)]
        P.op("sp", lambda e: e.dma_start(out=bg[:], in_=bglu[:, :]), writes=["bg"])
        load_cast(wgb, "wgb", wglu_v, KC * 128)
        for n in range(T // 512):
            P.op("sp", lambda e, n=n: e.dma_start(out=g32[:], in_=gT_v[:, :, n * 512:(n + 1) * 512]), writes=["g32"])
            P.op("qa", lambda e, n=n: e.dma_start(out=sz32[:], in_=szT_v[:, :, n * 512:(n + 1) * 512]), writes=["sz32"])
            P.op("pool", lambda e: e.tensor_copy(out=gb[:], in_=g32[:]), reads=["g32"], writes=["gb"])
            for m in range(KC):
                ps, pk = nextps()
                for k in range(KC):
                    P.op("pe", lambda e, ps=ps, k=k, m=m: e.matmul(ps[:], wgb[:, k, m * 128:(m + 1) * 128], gb[:, k, :],
                                                                   start=(k == 0), stop=(k == KC - 1)),
                         reads=["wgb", "gb"], writes=[pk])
                sg = sig[m % 2]
                P.op("act", lambda e, sg=sg, ps=ps, m=m: e.activation(out=sg[:], in_=ps[:], func=AF.Sigmoid, bias=bg[:, m:m + 1]),
                     reads=[pk, "bg"], writes=[f"sig{m % 2}"])
                P.op("dve", lambda e, sg=sg, m=m: e.tensor_tensor(out=sg[:], in0=sg[:], in1=g32[:, m, :], op=ALU.mult),
                     reads=["g32"], writes=[f"sig{m % 2}"])
                P.op("dve", lambda e, sg=sg, m=m, n=n: e.tensor_tensor(out=yb[:, m, n * 512:(n + 1) * 512], in0=sg[:], in1=sz32[:, m, :], op=ALU.mult),
                     reads=["sz32", f"sig{m % 2}"], writes=["yb"])
    else:
        yT_v = kb.dram_in("yT", [128, KC, T], ydt)
        if ydt == BF16:
            P.op("sp", lambda e: e.dma_start(out=yb[:, 0:KC // 2, :], in_=yT_v[:, 0:KC // 2, :]), writes=["yb"])
            P.op("qa", lambda e: e.dma_start(out=yb[:, KC // 2:KC, :], in_=yT_v[:, KC // 2:KC, :]), writes=["yb"])
        else:
            load_cast(yb, "yb", yT_v, T)
    load_cast(wb, "wb", w_v, D_MODEL)
    x_v = x.rearrange("(n p) d -> n p d", p=128)
    xo_v = xo.rearrange("(n p) d -> n p d", p=128)
    for tt in range(T // 128):
        xs = xt[tt % 2]
        xk = f"xt{tt % 2}"
        P.op("qa" if tt % 2 else "sp", lambda e, xs=xs, tt=tt: e.dma_start(out=xs[:], in_=x_v[tt]), writes=[xk])
        for n in range(4):
            ps, pk = nextps()
            for k in range(KC):
                P.op("pe", lambda e, ps=ps, k=k, tt=tt, n=n: e.matmul(ps[:], yb[:, k, tt * 128:(tt + 1) * 128], wb[:, k, n * 512:(n + 1) * 512],
                                                                      start=(k == 0), stop=(k == KC - 1)),
                     reads=["yb", "wb"], writes=[pk])
            P.op("dve", lambda e, xs=xs, ps=ps, n=n: e.scalar_tensor_tensor(out=xs[:, n * 512:(n + 1) * 512], in0=xs[:, n * 512:(n + 1) * 512], scalar=ALPHA,
                                                                            in1=ps[:], op0=ALU.mult, op1=ALU.add),
                 reads=[pk], writes=[xk])
            P.op("dve", lambda e, xs=xs, n=n: e.bn_stats(out=stats[:, n, :], in_=xs[:, n * 512:(n + 1) * 512]),
                 reads=[xk], writes=["stats"])
        P.op("dve", lambda e: e.bn_aggr(out=mv[:], in_=stats[:]), reads=["stats"], writes=["mv"])
        P.op("act", lambda e: e.activation(out=rstd[:], in_=mv[:, 1:2], func=AF.Sqrt, bias=epsc[:]), reads=["mv", "epsc"], writes=["rstd"])
        P.op("dve", lambda e: e.reciprocal(out=rstd[:], in_=rstd[:]), reads=["rstd"], writes=["rstd"])
        P.op("dve", lambda e, xs=xs: e.tensor_scalar(out=xs[:], in0=xs[:], scalar1=mv[:, 0:1], scalar2=rstd[:], op0=ALU.subtract, op1=ALU.mult),
             reads=["mv", "rstd"], writes=[xk])
        P.op("pool", lambda e, xs=xs: e.tensor_tensor(out=xs[:], in0=xs[:], in1=lng_s[:], op=ALU.mult), reads=["lng"], writes=[xk])
        P.op("pool", lambda e, xs=xs: e.tensor_tensor(out=xs[:], in0=xs[:], in1=lnb_s[:], op=ALU.add), reads=["lnb"], writes=[xk])
        P.op("sp", lambda e, xs=xs, tt=tt: e.dma_start(out=xo_v[tt], in_=xs[:]), reads=[xk])
    return kb.finish()


def build_scan(L=4096, NB=2):
    kb = KB()
    P = kb.P
    nc = kb.nc
    NT = L * NB
    NLEV = int(np.log2(L))
    CH = 1024
    NCH = L // CH
    uT = kb.dram_in("uT", [128, NT])
    lr_d = kb.dram_in("lr", [128, 4])
    li_d = kb.dram_in("li", [128, 4])
    ldt_d = kb.dram_in("ldt", [128, 4])
    br_d = kb.dram_in("br", [128, 4, 16])
    bi_d = kb.dram_in("bi", [128, 4, 16])
    cr_d = kb.dram_in("crT", [128, 4, 16])
    ci_d = kb.dram_in("ciT", [128, 4, 16])
    d_d = kb.dram_in("dvec", [128, 1])
    id_d = kb.dram_in("ident", [128, 128])
    gT = kb.dram_out("gT", [128, NT])

    ub = kb.sb("ub", [128, NT], BF16)
    yd = kb.sb("yd", [128, NT], F32)
    ident = kb.sb("ident_s", [128, 128], F32)
    sm = {n: kb.sb("sm_" + n, [128, 4], F32) for n in ["lr", "li", "dt", "th", "mag", "c", "s", "t1", "t2", "t3", "ar", "ai",
                                                       "den", "am1", "cr", "ci", "nci"]}
    br = kb.sb("br_s", [128, 4, 16], F32)
    bi = kb.sb("bi_s", [128, 4, 16], F32)
    crT = kb.sb("crT_s", [128, 4, 16], F32)
    ciT = kb.sb("ciT_s", [128, 4, 16], F32)
    bbr = kb.sb("bbr", [128, 4, 16], F32)
    bbi = kb.sb("bbi", [128, 4, 16], F32)
    tmpb = kb.sb("tmpb", [128, 4, 16], F32)
    Bblk = {n: kb.sb(n, [128, 4, 32], F32) for n in ["Bblk_r", "Bblk_i"]}
    BT = {n: kb.sb(n, [128, 128], BF16) for n in ["BT_r", "BT_i"]}
    Cpad = {n: kb.sb(n, [128, 4, 128], F32) for n in ["Cpad_r", "Cpad_ni"]}
    pc = kb.sb("pc", [128, NLEV, 4], F32)
    psn = kb.sb("psn", [128, NLEV, 4], F32)
    pns = kb.sb("pns", [128, NLEV, 4], F32)
    dv = kb.sb("dv", [128, 1], F32)
    halfpi = kb.sb("halfpi", [128, 1], F32)
    mtop = kb.sb("mtop", [128, 1], F32)
    mbot = kb.sb("mbot", [128, 1], F32)
    nmtop = kb.sb("nmtop", [128, 1], F32)
    nmbot = kb.sb("nmbot", [128, 1], F32)
    gt = [kb.sb(f"gt{i}", [128, 512], F32) for i in range(2)]
    gs = [kb.sb(f"gs{i}", [128, 512], F32) for i in range(2)]
    st0 = contextlib.ExitStack()
    u32 = st0.enter_context(nc.sbuf_tensor("u32", [128, NT], F32))
    pss = [kb.ps(f"ps{i}", [128, 512], F32) for i in range(8)]
    psi = [0]

    def nextps():
        i = psi[0] % 8
        psi[0] += 1
        return pss[i], f"ps{i}"

    def dma_in(dst, src, key, q="sp"):
        P.op(q, lambda e: e.dma_start(out=dst, in_=src), writes=[key])

    for q in range(4):
        P.op(("sp", "qa")[q % 2], lambda e, q=q: e.dma_start(out=u32[:, q * NT // 4:(q + 1) * NT // 4], in_=uT[:, q * NT // 4:(q + 1) * NT // 4]), writes=[f"u32_{q}"])
        P.op("dve", lambda e, q=q: e.tensor_copy(out=ub[:, q * NT // 4:(q + 1) * NT // 4], in_=u32[:, q * NT // 4:(q + 1) * NT // 4]), reads=[f"u32_{q}"], writes=[f"ub_{q}"])
    UBK = [f"ub_{q}" for q in range(4)]
    dma_in(sm["lr"][:], lr_d[:, :], "lr")
    dma_in(sm["li"][:], li_d[:, :], "li")
    dma_in(sm["dt"][:], ldt_d[:, :], "dt")
    dma_in(br[:], br_d[:, :, :], "br")
    dma_in(bi[:], bi_d[:, :, :], "bi")
    dma_in(crT[:], cr_d[:, :, :], "crT")
    dma_in(ciT[:], ci_d[:, :, :], "ciT")
    dma_in(dv[:], d_d[:, :], "dv")
    dma_in(ident[:], id_d[:, :], "ident")
    P.op("pool", lambda e: e.memset(halfpi[:], float(np.pi / 2)), writes=["halfpi"])
    for (t_, a, b) in [(mtop, 1.0, 0.0), (mbot, 0.0, 1.0), (nmtop, -1.0, 0.0), (nmbot, 0.0, -1.0)]:
        P.op("pool", lambda e, t_=t_, a=a: e.memset(t_[0:64, :], a), writes=["masks"])
        P.op("pool", lambda e, t_=t_, b=b: e.memset(t_[64:128, :], b), writes=["masks"])
    for n in ["Bblk_r", "Bblk_i"]:
        P.op("pool", lambda e, n=n: e.memset(Bblk[n][:], 0.0), writes=[n])
    for n in ["Cpad_r", "Cpad_ni"]:
        P.op("pool", lambda e, n=n: e.memset(Cpad[n][:], 0.0), writes=[n])

    def tt(out, a, b, op, rk, wk, eng="dve"):
        P.op(eng, lambda e: e.tensor_tensor(out=out, in0=a, in1=b, op=op), reads=rk, writes=wk)

    def S(n):
        return sm[n][:]

    def renorm(c_ap, s_ap, ck, sk):
        tt(S("t1"), c_ap, c_ap, ALU.mult, [ck], ["t1"])
        tt(S("t2"), s_ap, s_ap, ALU.mult, [sk], ["t2"])
        tt(S("t1"), S("t1"), S("t2"), ALU.add, ["t2"], ["t1"])
        P.op("dve", lambda e: e.tensor_scalar(out=S("t1"), in0=S("t1"), scalar1=-0.5, scalar2=1.5, op0=ALU.mult, op1=ALU.add), writes=["t1"])
        tt(c_ap, c_ap, S("t1"), ALU.mult, ["t1"], [ck])
        tt(s_ap, s_ap, S("t1"), ALU.mult, ["t1"], [sk])

    P.op("act", lambda e: e.activation(out=S("dt"), in_=S("dt"), func=AF.Exp), reads=["dt"], writes=["dt"])
    tt(S("th"), S("li"), S("dt"), ALU.mult, ["li", "dt"], ["th"])
    tt(S("t1"), S("lr"), S("dt"), ALU.mult, ["lr", "dt"], ["t1"])
    P.op("act", lambda e: e.activation(out=S("mag"), in_=S("t1"), func=AF.Exp), reads=["t1"], writes=["mag"])
    P.op("act", lambda e: e.activation(out=S("s"), in_=S("th"), func=AF.Sin, scale=1.0 / 16), reads=["th"], writes=["s"])
    P.op("act", lambda e: e.activation(out=S("c"), in_=S("th"), func=AF.Sin, scale=1.0 / 16, bias=halfpi[:]), reads=["th", "halfpi"], writes=["c"])
    for _ in range(4):
        tt(S("t1"), S("c"), S("c"), ALU.mult, ["c"], ["t1"])
        tt(S("t2"), S("s"), S("s"), ALU.mult, ["s"], ["t2"])
        tt(S("t3"), S("c"), S("s"), ALU.mult, ["c", "s"], ["t3"])
        tt(S("c"), S("t1"), S("t2"), ALU.subtract, ["t1", "t2"], ["c"])
        tt(S("s"), S("t3"), S("t3"), ALU.add, ["t3"], ["s"])
    renorm(S("c"), S("s"), "c", "s")
    tt(S("ar"), S("mag"), S("c"), ALU.mult, ["mag", "c"], ["ar"])
    tt(S("ai"), S("mag"), S("s"), ALU.mult, ["mag", "s"], ["ai"])
    P.op("dve", lambda e: e.tensor_copy(out=pc[:, 0, :], in_=S("c")), reads=["c"], writes=["pc"])
    P.op("dve", lambda e: e.tensor_copy(out=psn[:, 0, :], in_=S("s")), reads=["s"], writes=["psn"])
    for lev in range(1, NLEV):
        tt(S("t1"), pc[:, lev - 1, :], pc[:, lev - 1, :], ALU.mult, ["pc"], ["t1"])
        tt(S("t2"), psn[:, lev - 1, :], psn[:, lev - 1, :], ALU.mult, ["psn"], ["t2"])
        tt(S("t3"), pc[:, lev - 1, :], psn[:, lev - 1, :], ALU.mult, ["pc", "psn"], ["t3"])
        tt(pc[:, lev, :], S("t1"), S("t2"), ALU.subtract, ["t1", "t2"], ["pc"])
        tt(psn[:, lev, :], S("t3"), S("t3"), ALU.add, ["t3"], ["psn"])
        renorm(pc[:, lev, :], psn[:, lev, :], "pc", "psn")
    P.op("dve", lambda e: e.tensor_scalar(out=pns[:], in0=psn[:], scalar1=-1.0, scalar2=None, op0=ALU.mult), reads=["psn"], writes=["pns"])
    tt(S("t1"), S("lr"), S("lr"), ALU.mult, ["lr"], ["t1"])
    tt(S("t2"), S("li"), S("li"), ALU.mult, ["li"], ["t2"])
    tt(S("den"), S("t1"), S("t2"), ALU.add, ["t1", "t2"], ["den"])
    P.op("dve", lambda e: e.reciprocal(out=S("den"), in_=S("den")), reads=["den"], writes=["den"])
    P.op("dve", lambda e: e.tensor_scalar(out=S("am1"), in0=S("ar"), scalar1=-1.0, scalar2=None, op0=ALU.add), reads=["ar"], writes=["am1"])
    tt(S("t1"), S("am1"), S("lr"), ALU.mult, ["am1", "lr"], ["t1"])
    tt(S("t2"), S("ai"), S("li"), ALU.mult, ["ai", "li"], ["t2"])
    tt(S("t3"), S("t1"), S("t2"), ALU.add, ["t1", "t2"], ["t3"])
    tt(S("cr"), S("t3"), S("den"), ALU.mult, ["t3", "den"], ["cr"])
    tt(S("t1"), S("ai"), S("lr"), ALU.mult, ["ai", "lr"], ["t1"])
    tt(S("t2"), S("am1"), S("li"), ALU.mult, ["am1", "li"], ["t2"])
    tt(S("t3"), S("t1"), S("t2"), ALU.subtract, ["t1", "t2"], ["t3"])
    tt(S("ci"), S("t3"), S("den"), ALU.mult, ["t3", "den"], ["ci"])
    P.op("dve", lambda e: e.tensor_scalar(out=S("nci"), in0=S("ci"), scalar1=-1.0, scalar2=None, op0=ALU.mult), reads=["ci"], writes=["nci"])
    for j in range(4):
        P.op("dve", lambda e, j=j: e.tensor_scalar(out=tmpb[:, j, :], in0=bi[:, j, :], scalar1=sm["nci"][:, j:j + 1], scalar2=None, op0=ALU.mult),
             reads=["bi", "nci"], writes=["tmpb"])
        P.op("dve", lambda e, j=j: e.scalar_tensor_tensor(out=bbr[:, j, :], in0=br[:, j, :], scalar=sm["cr"][:, j:j + 1], in1=tmpb[:, j, :], op0=ALU.mult, op1=ALU.add),
             reads=["br", "cr", "tmpb"], writes=["bbr"])
        P.op("dve", lambda e, j=j: e.tensor_scalar(out=tmpb[:, j, :], in0=br[:, j, :], scalar1=sm["ci"][:, j:j + 1], scalar2=None, op0=ALU.mult),
             reads=["br", "ci"], writes=["tmpb"])
        P.op("dve", lambda e, j=j: e.scalar_tensor_tensor(out=bbi[:, j, :], in0=bi[:, j, :], scalar=sm["cr"][:, j:j + 1], in1=tmpb[:, j, :], op0=ALU.mult, op1=ALU.add),
             reads=["bi", "cr", "tmpb"], writes=["bbi"])
    for (src, n) in [(bbr, "Bblk_r"), (bbi, "Bblk_i")]:
        sk = "bbr" if n == "Bblk_r" else "bbi"
        P.op("dve", lambda e, src=src, n=n: e.tensor_scalar(out=Bblk[n][:, :, 0:16], in0=src[:], scalar1=mtop[:], scalar2=None, op0=ALU.mult),
             reads=[sk, "masks"], writes=[n])
        P.op("dve", lambda e, src=src, n=n: e.tensor_scalar(out=Bblk[n][:, :, 16:32], in0=src[:], scalar1=mbot[:], scalar2=None, op0=ALU.mult),
             reads=[sk, "masks"], writes=[n])
    for (n, bn) in [("Bblk_r", "BT_r"), ("Bblk_i", "BT_i")]:
        ps, pk = nextps()
        P.op("pe", lambda e, ps=ps, n=n: e.transpose(ps[:, 0:128], Bblk[n][:].rearrange("p j c -> p (j c)"), ident[:]), reads=[n, "ident"], writes=[pk])
        P.op("act", lambda e, ps=ps, bn=bn: e.copy(out=BT[bn][:], in_=ps[:, 0:128]), reads=[pk], writes=[bn])
    for j in range(4):
        for (cn, src, ma, mb, sk) in [("Cpad_r", crT, mtop, mbot, "crT"), ("Cpad_ni", ciT, nmtop, nmbot, "ciT")]:
            P.op("dve", lambda e, j=j, cn=cn, src=src, ma=ma: e.tensor_scalar(out=Cpad[cn][:, j, 32 * j:32 * j + 16], in0=src[:, j, :], scalar1=ma[:], scalar2=None, op0=ALU.mult),
                 reads=[sk, "masks"], writes=[cn])
            P.op("dve", lambda e, j=j, cn=cn, src=src, mb=mb: e.tensor_scalar(out=Cpad[cn][:, j, 32 * j + 16:32 * j + 32], in0=src[:, j, :], scalar1=mb[:], scalar2=None, op0=ALU.mult),
                 reads=[sk, "masks"], writes=[cn])
    NBLK = L // 512
    for q in range(4):
        P.op("pool", lambda e, q=q: e.tensor_scalar(out=yd[:, q * NT // 4:(q + 1) * NT // 4], in0=u32[:, q * NT // 4:(q + 1) * NT // 4], scalar1=dv[:], scalar2=None, op0=ALU.mult),
             reads=[f"u32_{q}", "dv"], writes=[f"yd_{q}"])
    P.barrier()
    st0.close()

    def ydk(tok):
        return f"yd_{tok // (NT // 4)}"

    Ec = kb.sb("Ec", [128, L], F32)
    Es = kb.sb("Es", [128, L], F32)
    magf = kb.sb("magf", [128, L], F32)
    Xr = kb.sb("Xr", [128, L], F32)
    Xi = kb.sb("Xi", [128, L], F32)
    T1 = [kb.sb(f"T1_{i}", [128, CH], F32) for i in range(2)]
    T2 = [kb.sb(f"T2_{i}", [128, CH], F32) for i in range(2)]
    XRK = [f"Xr{c}" for c in range(NCH)]
    XIK = [f"Xi{c}" for c in range(NCH)]
    tci = [0]

    def rotate(sign):
        for c in range(NCH):
            sl = slice(c * CH, (c + 1) * CH)
            k = tci[0] % 2
            tci[0] += 1
            t1, t2 = T1[k], T2[k]
            P.op("pool", lambda e, sl=sl, t1=t1: e.tensor_tensor(out=t1[:], in0=Xi[:, sl], in1=Es[:, sl], op=ALU.mult), reads=[XIK[c], "E"], writes=[f"T1_{k}"])
            P.op("pool", lambda e, sl=sl, t2=t2: e.tensor_tensor(out=t2[:], in0=Xr[:, sl], in1=Es[:, sl], op=ALU.mult), reads=[XRK[c], "E"], writes=[f"T2_{k}"])
            P.op("dve", lambda e, sl=sl: e.tensor_tensor(out=Xr[:, sl], in0=Xr[:, sl], in1=Ec[:, sl], op=ALU.mult), reads=["E"], writes=[XRK[c]])
            P.op("dve", lambda e, sl=sl: e.tensor_tensor(out=Xi[:, sl], in0=Xi[:, sl], in1=Ec[:, sl], op=ALU.mult), reads=["E"], writes=[XIK[c]])
            P.op("dve", lambda e, sl=sl, t1=t1: e.tensor_tensor(out=Xr[:, sl], in0=Xr[:, sl], in1=t1[:], op=(ALU.add if sign < 0 else ALU.subtract)), reads=[f"T1_{k}"], writes=[XRK[c]])
            P.op("dve", lambda e, sl=sl, t2=t2: e.tensor_tensor(out=Xi[:, sl], in0=Xi[:, sl], in1=t2[:], op=(ALU.subtract if sign < 0 else ALU.add)), reads=[f"T2_{k}"], writes=[XIK[c]])

    for j in range(4):
        P.op("dve", lambda e: e.memset(Ec[:, 0:1], 1.0), writes=["E"])
        P.op("dve", lambda e: e.memset(Es[:, 0:1], 0.0), writes=["E"])
        for lev in range(NLEV):
            n = 1 << lev
            C = pc[:, lev, j:j + 1]
            Sn = psn[:, lev, j:j + 1]
            Sm = pns[:, lev, j:j + 1]
            t1 = T1[0] if n <= CH else None
            for o in range(0, n, CH):
                w_ = min(CH, n - o)
                a, b = T1[0], T2[0]
                P.op("dve", lambda e, o=o, w_=w_, a=a, Sm=Sm: e.tensor_scalar(out=a[:, 0:w_], in0=Es[:, o:o + w_], scalar1=Sm, scalar2=None, op0=ALU.mult), reads=["E", "pns"], writes=["T1_0"])
                P.op("dve", lambda e, o=o, w_=w_, b=b, Sn=Sn: e.tensor_scalar(out=b[:, 0:w_], in0=Ec[:, o:o + w_], scalar1=Sn, scalar2=None, op0=ALU.mult), reads=["E", "psn"], writes=["T2_0"])
                P.op("dve", lambda e, o=o, w_=w_, a=a, C=C, n=n: e.scalar_tensor_tensor(out=Ec[:, n + o:n + o + w_], in0=Ec[:, o:o + w_], scalar=C, in1=a[:, 0:w_], op0=ALU.mult, op1=ALU.add),
                     reads=["T1_0", "pc"], writes=["E"])
                P.op("dve", lambda e, o=o, w_=w_, b=b, C=C, n=n: e.scalar_tensor_tensor(out=Es[:, n + o:n + o + w_], in0=Es[:, o:o + w_], scalar=C, in1=b[:, 0:w_], op0=ALU.mult, op1=ALU.add),
                     reads=["T2_0", "pc"], writes=["E"])
        P.op("act", lambda e, j=j: e.activation(out=magf[:], in_=Ec[:], func=AF.Identity, scale=0.0, bias=sm["mag"][:, j:j + 1]), reads=["E", "mag"], writes=["magf"])
        for b in range(NB):
            for blk in range(NBLK):
                t0 = b * L + blk * 512
                c = (blk * 512) // CH
                for (bn, X_, xk) in [("BT_r", Xr, XRK[c]), ("BT_i", Xi, XIK[c])]:
                    ps, pk = nextps()
                    P.op("pe", lambda e, ps=ps, bn=bn, t0=t0, j=j: e.matmul(ps[:], BT[bn][32 * j:32 * j + 32, :], ub[32 * j:32 * j + 32, t0:t0 + 512],
                                                                            start=True, stop=True, tile_position=(32 * j, 0)),
                         reads=[bn] + UBK, writes=[pk])
                    P.op("act", lambda e, ps=ps, X_=X_, blk=blk: e.copy(out=X_[:, blk * 512:(blk + 1) * 512], in_=ps[:]), reads=[pk], writes=[xk])
            rotate(-1)
            P.op("dve", lambda e: e.tensor_tensor_scan(out=Xr[:], data0=magf[:], data1=Xr[:], initial=0.0, op0=ALU.mult, op1=ALU.add), reads=["magf"], writes=XRK)
            P.op("dve", lambda e: e.tensor_tensor_scan(out=Xi[:], data0=magf[:], data1=Xi[:], initial=0.0, op0=ALU.mult, op1=ALU.add), reads=["magf"], writes=XIK)
            rotate(+1)
            for blk in range(NBLK):
                t0 = b * L + blk * 512
                c = (blk * 512) // CH
                ps, pk = nextps()
                P.op("pe", lambda e, ps=ps, blk=blk, j=j: e.matmul(ps[:], Cpad["Cpad_r"][:, j, :], Xr[:, blk * 512:(blk + 1) * 512], start=True, stop=False),
                     reads=["Cpad_r", XRK[c]], writes=[pk])
                P.op("pe", lambda e, ps=ps, blk=blk, j=j: e.matmul(ps[:], Cpad["Cpad_ni"][:, j, :], Xi[:, blk * 512:(blk + 1) * 512], start=False, stop=True),
                     reads=["Cpad_ni", XIK[c]], writes=[pk])
                P.op("dve", lambda e, ps=ps, t0=t0: e.tensor_tensor(out=yd[:, t0:t0 + 512], in0=yd[:, t0:t0 + 512], in1=ps[:], op=ALU.add),
                     reads=[pk], writes=[ydk(t0)])
    for i in range(NT // 512):
        t0 = i * 512
        a, ak = gt[i % 2], f"gt{i % 2}"
        sg, sk = gs[i % 2], f"gs{i % 2}"
        P.op("act", lambda e, a=a, t0=t0: e.activation(out=a[:], in_=yd[:, t0:t0 + 512], func=AF.Square), reads=[ydk(t0)], writes=[ak])
        P.op("dve", lambda e, a=a: e.tensor_scalar(out=a[:], in0=a[:], scalar1=0.044715, scalar2=1.0, op0=ALU.mult, op1=ALU.add), reads=[], writes=[ak])
        P.op("dve", lambda e, a=a, t0=t0: e.tensor_tensor(out=a[:], in0=a[:], in1=yd[:, t0:t0 + 512], op=ALU.mult), reads=[ydk(t0)], writes=[ak])
        P.op("act", lambda e, a=a, sg=sg: e.activation(out=sg[:], in_=a[:], func=AF.Sigmoid, scale=1.5957691216057308), reads=[ak], writes=[sk])
        P.op("dve", lambda e, sg=sg, t0=t0: e.tensor_tensor(out=sg[:], in0=sg[:], in1=yd[:, t0:t0 + 512], op=ALU.mult), reads=[ydk(t0)], writes=[sk])
        P.op(("sp", "qa")[i % 2], lambda e, sg=sg, t0=t0: e.dma_start(out=gT[:, t0:t0 + 512], in_=sg[:]), reads=[sk])
    return kb.finish()


def attn_consts():
    m = np.arange(128)[:, None]
    n = np.arange(512)[None, :]
    wm = np.zeros((8, 128, 512), np.float32)
    for rel in range(-4, 0):
        wm[rel + 4] = np.where((128 * rel + m) <= n - 512, NEG, 0.0)
    for rel in range(0, 4):
        wm[rel + 4] = np.where((128 * rel + m) > n, NEG, 0.0)
    wmask = np.ascontiguousarray(wm.transpose(1, 0, 2)).astype(NPBF)
    j = np.arange(64)[:, None, None]
    kt = np.arange(32)[None, :, None]
    mm = np.arange(128)[None, None, :]
    ekt = np.where(j == 2 * kt + mm // 64, NEG, 0.0).astype(NPBF)
    r = np.arange(128)[:, None]
    jj = np.arange(503)[None, :]
    cmpw = np.where(16 * (jj - 248) + 31 <= r, 0.0, NEG).astype(np.float32)
    jj = np.arange(126)[None, :]
    rel = (jj - 62) - (r >= 64)
    keepw = np.where((rel > 0) | (rel == 0) | (rel == -1), 0.0, 1.0).astype(np.float32)
    addw = np.where(rel > 0, -1e30, np.where((rel == 0) | (rel == -1), 1e6, 0.0)).astype(np.float32)
    onehot = np.zeros((12, 12, 128), np.float32)
    for g in range(12):
        onehot[g, g, :] = 1.0
    return dict(wmask=wmask, ekt=ekt, cmpw=cmpw, keepw=keepw, addw=addw, onehot=onehot,
                identf=np.eye(128, dtype=np.float32), identb=np.eye(128).astype(NPBF),
                onesb=np.ones((128, 128)).astype(NPBF))


def build_attn(L=4096):
    kb = KB()
    P = kb.P
    nc = kb.nc
    NQB = L // 512
    NKT = L // 128
    NC = (L - 32) // 16 + 1
    NC1 = NC - 128
    NSB = L // 64
    qT = kb.dram_in("qT", [128, 4, L], BF16)
    szT = kb.dram_in("szT", [128, 12, L], BF16)
    gateT = kb.dram_in("gateT", [12, L], F32)
    kcT = kb.dram_in("kcT", [128, L], BF16)
    vcT = kb.dram_in("vcT", [128, L], BF16)
    ksT = kb.dram_in("ksT", [128, L], BF16)
    kwT = kb.dram_in("kwT", [128, L], BF16)
    vs_d = kb.dram_in("vs", [128, NKT, 128], BF16)
    vw_d = kb.dram_in("vw", [128, NKT, 128], BF16)
    posk = kb.dram_in("poskT", [128, 32], F32)
    posv = kb.dram_in("posvT", [128, 32], F32)
    w1k = kb.dram_in("w1k", [128, 32, 128], F32)
    w1v = kb.dram_in("w1v", [128, 32, 128], F32)
    w2k = kb.dram_in("w2k", [128, 128], F32)
    w2v = kb.dram_in("w2v", [128, 128], F32)
    c_wmask = kb.dram_in("wmask", [128, 8, 512], BF16)
    c_ekt = kb.dram_in("ekt", [64, 32, 128], BF16)
    c_cmpw = kb.dram_in("cmpw", [128, 503], F32)
    c_keepw = kb.dram_in("keepw", [128, 126], F32)
    c_addw = kb.dram_in("addw", [128, 126], F32)
    c_onehot = kb.dram_in("onehot", [12, 12, 128], F32)
    c_identf = kb.dram_in("identf", [128, 128], F32)
    c_identb = kb.dram_in("identb", [128, 128], BF16)
    c_onesb = kb.dram_in("onesb", [128, 128], BF16)
    yT = kb.dram_out("yT", [128, 4, L], BF16)
    lq = [0]

    def load(name, src, shape, dt, st=None):
        t = (st or kb.st).enter_context(nc.sbuf_tensor(name, list(shape), dt))
        q = ("sp", "qa")[lq[0] % 2]
        lq[0] += 1
        P.op(q, lambda e: e.dma_start(out=t[:], in_=src), writes=[name])
        return t

    st0 = contextlib.ExitStack()

    def sb0(name, shape, dt):
        return st0.enter_context(nc.sbuf_tensor(name, list(shape), dt))

    ks_s = load("ks_s", ksT[:, :], [128, L], BF16)
    kw_s = load("kw_s", kwT[:, :], [128, L], BF16)
    vs_s = load("vs_s", vs_d[:, :, :], [128, NKT, 128], BF16)
    vw_s = load("vw_s", vw_d[:, :, :], [128, NKT, 128], BF16)
    wmask = load("wmask_s", c_wmask[:, :, :], [128, 8, 512], BF16)
    ekt = load("ekt_s", c_ekt[:, :, :], [64, 32, 128], BF16)
    cmpw = load("cmpw_s", c_cmpw[:, :], [128, 503], F32)
    keepw = load("keepw_s", c_keepw[:, :], [128, 126], F32)
    addw = load("addw_s", c_addw[:, :], [128, 126], F32)
    onehot = load("onehot_s", c_onehot[:, :, :], [12, 12, 128], F32)
    identf = load("identf_s", c_identf[:, :], [128, 128], F32)
    identb = load("identb_s", c_identb[:, :], [128, 128], BF16)
    onesb = load("onesb_s", c_onesb[:, :], [128, 128], BF16)
    kcmpT = kb.sb("kcmpT", [128, NC], BF16)
    vcmp = kb.sb("vcmp", [128, 2, 128], BF16)
    kc_s = load("kc_s", kcT[:, :], [128, L], BF16, st0)
    vc_s = load("vc_s", vcT[:, :], [128, L], BF16, st0)
    posk_s = load("posk_s", posk[:, :], [128, 32], F32, st0)
    posv_s = load("posv_s", posv[:, :], [128, 32], F32, st0)
    w2k_s = load("w2k_s", w2k[:, :], [128, 128], F32, st0)
    w2v_s = load("w2v_s", w2v[:, :], [128, 128], F32, st0)
    w1st = sb0("w1st", [128, 32, 128], F32)
    w1b = sb0("w1b", [128, 32, 128], BF16)
    w2b = sb0("w2b", [128, 128], BF16)
    tmpl = [sb0(f"tmpl{i}", [128, NC], BF16) for i in range(4)]
    h1 = sb0("h1", [128, NC], F32)
    h2 = sb0("h2", [128, NC], F32)
    hg = sb0("hg", [128, NC], BF16)

    pss = [kb.ps(f"ps{i}", [128, 512], F32) for i in range(8)]
    SROT = [0, 1]
    PS_O = [2, 3]
    PS_S = [4, 5]
    PS_C = [6, 7]
    srot = [0]
    crot = [0]

    def ps_s_next():
        i = SROT[srot[0] % 2]
        srot[0] += 1
        return pss[i], f"ps{i}"

    def ps_c_next():
        i = PS_C[crot[0] % 2]
        crot[0] += 1
        return pss[i], f"ps{i}"

    for side, (src_s, pos_s, w1d, w2s) in enumerate([(kc_s, posk_s, w1k, w2k_s), (vc_s, posv_s, w1v, w2v_s)]):
        srck = "kc_s" if side == 0 else "vc_s"
        posn = "posk_s" if side == 0 else "posv_s"
        w2n = "w2k_s" if side == 0 else "w2v_s"
        P.op("sp", lambda e, w1d=w1d: e.dma_start(out=w1st[:], in_=w1d[:, :, :]), writes=["w1st"])
        P.op("pool", lambda e: e.tensor_copy(out=w1b[:], in_=w1st[:]), reads=["w1st"], writes=["w1b"])
        P.op("pool", lambda e, w2s=w2s: e.tensor_copy(out=w2b[:], in_=w2s[:]), reads=[w2n], writes=["w2b"])
        ps, pk = ps_c_next()
        for l in range(32):
            tl, tk = tmpl[l % 4], f"tmpl{l % 4}"
            P.op("dve", lambda e, tl=tl, l=l, src_s=src_s, pos_s=pos_s: e.tensor_scalar(
                out=tl[:], in0=src_s[:, l:l + 16 * (NC - 1) + 1:16], scalar1=pos_s[:, l:l + 1], scalar2=None, op0=ALU.add),
                reads=[srck, posn], writes=[tk])
            P.op("pe", lambda e, ps=ps, tl=tl, l=l: e.matmul(ps[:, 0:NC], w1b[:, l, :], tl[:], start=(l == 0), stop=(l == 31)),
                 reads=["w1b", tk], writes=[pk])
        P.op("act", lambda e, ps=ps: e.copy(out=h1[:], in_=ps[:, 0:NC]), reads=[pk], writes=["h1"])
        P.op("act", lambda e: e.activation(out=h2[:], in_=h1[:], func=AF.Square), reads=["h1"], writes=["h2"])
        P.op("dve", lambda e: e.tensor_scalar(out=h2[:], in0=h2[:], scalar1=0.044715, scalar2=1.0, op0=ALU.mult, op1=ALU.add), writes=["h2"])
        P.op("dve", lambda e: e.tensor_tensor(out=h2[:], in0=h2[:], in1=h1[:], op=ALU.mult), reads=["h1"], writes=["h2"])
        P.op("act", lambda e: e.activation(out=h2[:], in_=h2[:], func=AF.Sigmoid, scale=1.5957691216057308), writes=["h2"])
        P.op("dve", lambda e: e.tensor_tensor(out=hg[:], in0=h2[:], in1=h1[:], op=ALU.mult), reads=["h1", "h2"], writes=["hg"])
        if side == 0:
            ps, pk = ps_c_next()
            P.op("pe", lambda e, ps=ps: e.matmul(ps[:, 0:NC], w2b[:], hg[:], start=True, stop=True), reads=["w2b", "hg"], writes=[pk])
            P.op("act", lambda e, ps=ps: e.copy(out=kcmpT[:], in_=ps[:, 0:NC]), reads=[pk], writes=["kcmpT"])
        else:
            for t2, (n0, nn) in enumerate([(0, 128), (128, NC1)]):
                ps, pk = ps_c_next()
                P.op("pe", lambda e, ps=ps, n0=n0, nn=nn: e.matmul(ps[0:nn, 0:128], hg[:, n0:n0 + nn], w2b[:], start=True, stop=True),
                     reads=["w2b", "hg"], writes=[pk])
                P.op("act", lambda e, ps=ps, nn=nn, t2=t2: e.copy(out=vcmp[0:nn, t2, :], in_=ps[0:nn, 0:128]), reads=[pk], writes=["vcmp"])
    P.barrier()
    st0.close()

    qb_s = [kb.sb(f"qb{i}", [128, 4, 512], BF16) for i in range(2)]
    sz_s = [kb.sb(f"sz{i}", [128, 12, 512], BF16) for i in range(2)]
    gt_s = [kb.sb(f"gt{i}", [12, 512], F32) for i in range(2)]
    gbc = [kb.sb(f"gbc{i}", [128, 12, 512], F32) for i in range(2)]
    sc = [kb.sb(f"sc{i}", [128, NC], F32) for i in range(2)]
    pe_ = [kb.sb(f"pe_{i}", [128, NC], F32) for i in range(2)]
    pn = [kb.sb(f"pn{i}", [128, NC], F32) for i in range(2)]
    ppad = kb.sb("ppad", [128, 260], F32)
    rs = [kb.sb(f"rs{i}", [128, 1], F32) for i in range(2)]
    pnT = [kb.sb(f"pnT{i}", [128, 2, 128], BF16) for i in range(2)]
    ocmp = [kb.sb(f"ocmp{i}", [128, 4, 512], F32) for i in range(2)]
    imp = kb.sb("imp", [128, NSB], F32)
    imp2 = kb.sb("imp2", [128, NSB], F32)
    v8a = kb.sb("v8a", [128, 8], F32)
    v8b = kb.sb("v8b", [128, 8], F32)
    nsel = kb.sb("nsel", [128, NSB], F32)
    nsT = [kb.sb(f"nsT{i}", [64, 512], BF16) for i in range(2)]
    pT = [kb.sb(f"pT{i}", [128, 512], BF16) for i in range(3)]
    osb = kb.sb("osb", [128, 512], F32)
    rden = kb.sb("rden", [128, 512], F32)
    wgt = kb.sb("wgt", [128, 512], F32)
    tt1 = kb.sb("tt1", [128, 512], F32)
    acc = kb.sb("acc", [128, 4, 512], F32)
    yb = [kb.sb(f"ybo{i}", [128, 4, 512], BF16) for i in range(2)]
    P.op("pool", lambda e: e.memset(ppad[:], 0.0), writes=["ppad"])
    pti = [0]

    def emit_cmp(qb):
        par = qb % 2
        q0 = qb * 512
        qs, qk = qb_s[par], f"qb{par}"
        szs, szk = sz_s[par], f"sz{par}"
        gts, gk = gt_s[par], f"gt{par}"
        P.op("sp", lambda e: e.dma_start(out=qs[:], in_=qT[:, :, q0:q0 + 512]), writes=[qk])
        P.op("qa", lambda e: e.dma_start(out=szs[:], in_=szT[:, :, q0:q0 + 512]), writes=[szk])
        P.op("sp", lambda e: e.dma_start(out=gts[:], in_=gateT[:, q0:q0 + 512]), writes=[gk])
        for gi in range(12):
            ps, pk = ps_c_next()
            P.op("pe", lambda e, ps=ps, gi=gi: e.matmul(ps[:], onehot[:, gi, :], gts[:], start=True, stop=True), reads=["onehot_s", gk], writes=[pk])
            P.op("act", lambda e, ps=ps, gi=gi: e.copy(out=gbc[par][:, gi, :], in_=ps[:]), reads=[pk], writes=[f"gbc{par}"])
        for qt in range(4):
            it = qb * 4 + qt
            c0 = 248 - 8 * it
            for pair in ((0, 1), (2, 3)):
                bank = {h: (pss[PS_C[h % 2]], f"ps{PS_C[h % 2]}") for h in pair}
                for h in pair:
                    ps, pk = bank[h]
                    P.op("pe", lambda e, ps=ps, h=h, qt=qt: e.matmul(ps[:, 0:NC], qs[:, h, qt * 128:(qt + 1) * 128], kcmpT[:], start=True, stop=True),
                         reads=[qk, "kcmpT"], writes=[pk])
                for h in pair:
                    ps, pk = bank[h]
                    i2 = h % 2
                    P.op("dve", lambda e, ps=ps, c0=c0, i2=i2: e.tensor_tensor(out=sc[i2][:], in0=ps[:, 0:NC], in1=cmpw[:, c0:c0 + NC], op=ALU.add),
                         reads=[pk, "cmpw_s"], writes=[f"sc{i2}"])
                for h in pair:
                    i2 = h % 2
                    P.op("act", lambda e, i2=i2: e.activation(out=pe_[i2][:], in_=sc[i2][:], func=AF.Exp, accum_out=rs[i2][:]), reads=[f"sc{i2}"], writes=[f"pe_{i2}", f"rs{i2}"])
                for h in pair:
                    i2 = h % 2
                    P.op("dve", lambda e, i2=i2: e.tensor_scalar(out=rs[i2][:], in0=rs[i2][:], scalar1=1e-30, scalar2=None, op0=ALU.max), writes=[f"rs{i2}"])
                for h in pair:
                    i2 = h % 2
                    P.op("dve", lambda e, i2=i2: e.reciprocal(out=rs[i2][:], in_=rs[i2][:]), writes=[f"rs{i2}"])
                for h in pair:
                    i2 = h % 2
                    P.op("dve", lambda e, i2=i2: e.tensor_scalar(out=pn[i2][:], in0=pe_[i2][:], scalar1=rs[i2][:], scalar2=None, op0=ALU.mult),
                         reads=[f"pe_{i2}", f"rs{i2}"], writes=[f"pn{i2}"])
                for h in pair:
                    i2 = h % 2
                    if h == 0:
                        P.op("pool", lambda e, i2=i2: e.tensor_copy(out=ppad[:, 1:1 + NC], in_=pn[i2][:]), reads=[f"pn{i2}"], writes=["ppad"])
                    else:
                        P.op("pool", lambda e, i2=i2: e.tensor_tensor(out=ppad[:, 1:1 + NC], in0=ppad[:, 1:1 + NC], in1=pn[i2][:], op=ALU.add),
                             reads=[f"pn{i2}"], writes=["ppad"])
                for h in pair:
                    ps, pk = bank[h]
                    i2 = h % 2
                    P.op("pe", lambda e, ps=ps, i2=i2: e.transpose(ps[:, 0:128], pn[i2][:, 0:128], identf[:]), reads=[f"pn{i2}", "identf_s"], writes=[pk])
                    P.op("pe", lambda e, ps=ps, i2=i2: e.transpose(ps[0:NC1, 128:256], pn[i2][:, 128:NC], identf[:]), reads=[f"pn{i2}", "identf_s"], writes=[pk])
                for h in pair:
                    ps, pk = bank[h]
                    i2 = h % 2
                    P.op("act", lambda e, ps=ps, i2=i2: e.copy(out=pnT[i2][:, 0, :], in_=ps[:, 0:128]), reads=[pk], writes=[f"pnT{i2}"])
                    P.op("act", lambda e, ps=ps, i2=i2: e.copy(out=pnT[i2][0:NC1, 1, :], in_=ps[0:NC1, 128:256]), reads=[pk], writes=[f"pnT{i2}"])
                for h in pair:
                    ps, pk = bank[h]
                    i2 = h % 2
                    P.op("pe", lambda e, ps=ps, i2=i2: e.matmul(ps[:, 256:384], vcmp[:, 0, :], pnT[i2][:, 0, :], start=True, stop=False),
                         reads=["vcmp", f"pnT{i2}"], writes=[pk])
                    P.op("pe", lambda e, ps=ps, i2=i2: e.matmul(ps[:, 256:384], vcmp[0:NC1, 1, :], pnT[i2][0:NC1, 1, :], start=False, stop=True),
                         reads=["vcmp", f"pnT{i2}"], writes=[pk])
                for h in pair:
                    ps, pk = bank[h]
                    P.op("act", lambda e, ps=ps, h=h, qt=qt: e.copy(out=ocmp[par][:, h, qt * 128:(qt + 1) * 128], in_=ps[:, 256:384]),
                         reads=[pk], writes=[f"ocmp{par}"])
            P.op("dve", lambda e: e.tensor_tensor(out=imp[:], in0=ppad[:, 0:256:4], in1=ppad[:, 1:257:4], op=ALU.add), reads=["ppad"], writes=["imp"])
            for m_ in (2, 3, 4):
                P.op("dve", lambda e, m_=m_: e.tensor_tensor(out=imp[:], in0=imp[:], in1=ppad[:, m_:m_ + 256:4], op=ALU.add), reads=["ppad"], writes=["imp"])
            w0 = 62 - 2 * it
            P.op("dve", lambda e, w0=w0: e.tensor_tensor(out=imp2[:], in0=imp[:], in1=keepw[:, w0:w0 + NSB], op=ALU.mult), reads=["imp", "keepw_s"], writes=["imp2"])
            P.op("dve", lambda e, w0=w0: e.tensor_tensor(out=imp2[:], in0=imp2[:], in1=addw[:, w0:w0 + NSB], op=ALU.add), reads=["addw_s"], writes=["imp2"])
            P.op("dve", lambda e: e.memset(imp2[:, 0:1], 1e6), writes=["imp2"])
            P.op("dve", lambda e: e.max(out=v8a[:], in_=imp2[:]), reads=["imp2"], writes=["v8a"])
            P.op("dve", lambda e: e.match_replace(out=imp[:], in_to_replace=v8a[:], in_values=imp2[:], imm_value=-3.0e38), reads=["imp2", "v8a"], writes=["imp"])
            P.op("dve", lambda e: e.max(out=v8b[:], in_=imp[:]), reads=["imp"], writes=["v8b"])
            P.op("dve", lambda e: e.tensor_scalar(out=nsel[:], in0=imp2[:], scalar1=v8b[:, 7:8], scalar2=None, op0=ALU.is_lt), reads=["imp2", "v8b"], writes=["nsel"])
            ps4, pk4 = ps_c_next()
            P.op("pe", lambda e, ps4=ps4: e.transpose(ps4[0:NSB, 0:128], nsel[:], identf[:]), reads=["nsel", "identf_s"], writes=[pk4])
            P.op("act", lambda e, ps4=ps4, qt=qt: e.copy(out=nsT[par][:, qt * 128:(qt + 1) * 128], in_=ps4[0:NSB, 0:128]), reads=[pk4], writes=[f"nsT{par}"])

    def emit_att(qb):
        par = qb % 2
        q0 = qb * 512
        qs, qk = qb_s[par], f"qb{par}"
        szs, szk = sz_s[par], f"sz{par}"
        nst, nsk = nsT[par], f"nsT{par}"
        jobs = []
        grp = 0
        for h in range(4):
            for br in (1, 2):
                kts = list(range(0, 4 * (qb + 1))) if br == 1 else list(range(max(0, 4 * qb - 4), 4 * qb + 4))
                if br == 2 and qb >= 1:
                    kts = [4 * qb - 1] + [k for k in kts if k != 4 * qb - 1]
                for ki, kt in enumerate(kts):
                    jobs.append((h, br, kt, ki, len(kts), grp))
                grp += 1
        state = {}

        def emit_S(job):
            h, br, kt, ki, nk, g = job
            ps, pk = ps_s_next()
            rel = kt - 4 * qb
            c0 = max(0, 128 * rel) if rel >= 0 else 0
            c1 = min(512, 128 * rel + 640) if br == 2 else 512
            if br == 1:
                diag = rel >= 0
                P.op("pe", lambda e: e.matmul(ps[:, c0:c1], ks_s[:, kt * 128:(kt + 1) * 128], qs[:, h, c0:c1], start=True, stop=False), reads=["ks_s", qk], writes=[pk])
                P.op("pe", lambda e: e.matmul(ps[:, c0:c1], ekt[:, kt, :], nst[:, c0:c1], start=False, stop=(not diag)), reads=["ekt_s", nsk], writes=[pk])
                if diag:
                    P.op("pe", lambda e: e.matmul(ps[:, c0:c1], identb[:], wmask[:, rel + 4, c0:c1], start=False, stop=True), reads=["identb_s", "wmask_s"], writes=[pk])
            else:
                P.op("pe", lambda e: e.matmul(ps[:, c0:c1], kw_s[:, kt * 128:(kt + 1) * 128], qs[:, h, c0:c1], start=True, stop=False), reads=["kw_s", qk], writes=[pk])
                P.op("pe", lambda e: e.matmul(ps[:, c0:c1], identb[:], wmask[:, rel + 4, c0:c1], start=False, stop=True), reads=["identb_s", "wmask_s"], writes=[pk])
            state[job] = (ps, pk, c0, c1)

        def emit_PV(job):
            h, br, kt, ki, nk, g = job
            ps, pk, c0, c1 = state.pop(job)
            pt_, ptk = pT[pti[0] % 3], f"pT{pti[0] % 3}"
            pti[0] += 1
            v_s, vname = (vs_s, "vs_s") if br == 1 else (vw_s, "vw_s")
            pso, pko = pss[PS_O[g % 2]], f"ps{PS_O[g % 2]}"
            pss_, pks = pss[PS_S[g % 2]], f"ps{PS_S[g % 2]}"
            P.op("act", lambda e: e.activation(out=pt_[:, c0:c1], in_=ps[:, c0:c1], func=AF.Exp), reads=[pk], writes=[ptk])
            P.op("pe", lambda e: e.matmul(pso[:, c0:c1], v_s[:, kt, :], pt_[:, c0:c1], start=(ki == 0), stop=(ki == nk - 1)), reads=[vname, ptk], writes=[pko])
            P.op("pe", lambda e: e.matmul(pss_[:, c0:c1], onesb[:], pt_[:, c0:c1], start=(ki == 0), stop=(ki == nk - 1)), reads=["onesb_s", ptk], writes=[pks])
            if ki == nk - 1:
                gi = br * 4 + h
                P.op("act", lambda e: e.copy(out=osb[:], in_=pso[:]), reads=[pko], writes=["osb"])
                P.op("dve", lambda e: e.reciprocal(out=rden[:], in_=pss_[:]), reads=[pks], writes=["rden"])
                P.op("dve", lambda e: e.tensor_tensor(out=wgt[:], in0=rden[:], in1=gbc[par][:, gi, :], op=ALU.mult), reads=["rden", f"gbc{par}"], writes=["wgt"])
                P.op("dve", lambda e: e.tensor_tensor(out=tt1[:], in0=osb[:], in1=wgt[:], op=ALU.mult), reads=["osb", "wgt"], writes=["tt1"])
                if br == 1:
                    P.op("dve", lambda e: e.tensor_tensor(out=acc[:, h, :], in0=tt1[:], in1=szs[:, gi, :], op=ALU.mult), reads=["tt1", szk], writes=["acc"])
                else:
                    P.op("dve", lambda e: e.tensor_tensor(out=tt1[:], in0=tt1[:], in1=szs[:, gi, :], op=ALU.mult), reads=[szk], writes=["tt1"])
                    P.op("pool", lambda e: e.tensor_tensor(out=acc[:, h, :], in0=acc[:, h, :], in1=tt1[:], op=ALU.add), reads=["tt1"], writes=["acc"])
                    ybo, ybk = yb[par], f"ybo{par}"
                    P.op("dve", lambda e: e.tensor_tensor(out=tt1[:], in0=ocmp[par][:, h, :], in1=gbc[par][:, h, :], op=ALU.mult), reads=[f"ocmp{par}", f"gbc{par}"], writes=["tt1"])
                    P.op("dve", lambda e: e.tensor_tensor(out=tt1[:], in0=tt1[:], in1=szs[:, h, :], op=ALU.mult), reads=[szk], writes=["tt1"])
                    P.op("pool", lambda e: e.tensor_tensor(out=ybo[:, h, :], in0=acc[:, h, :], in1=tt1[:], op=ALU.add), reads=["tt1", "acc"], writes=[ybk])

        emit_S(jobs[0])
        for i, job in enumerate(jobs):
            if i + 1 < len(jobs):
                emit_S(jobs[i + 1])
            emit_PV(job)
        ybo, ybk = yb[par], f"ybo{par}"
        P.op("sp", lambda e: e.dma_start(out=yT[:, :, q0:q0 + 512], in_=ybo[:]), reads=[ybk])

    emit_cmp(0)
    for qb in range(NQB):
        P.capture()
        emit_att(qb)
        sa = P.end_capture()
        sb_ = []
        if qb + 1 < NQB:
            P.capture()
            emit_cmp(qb + 1)
            sb_ = P.end_capture()
        P.replay(sa, sb_)
    return kb.finish()


def _c(a):
    return np.ascontiguousarray(a)


def scan_maps(inp, layer, uT):
    maps = []
    ident = np.eye(128, dtype=np.float32)
    for c in range(NCORES):
        g0 = 8 * c

        def pl(a):
            return _c(a.reshape(4, 2, 64).transpose(1, 2, 0).reshape(128, 4))

        lr = pl(inp["a_lam_re"][layer][g0:g0 + 8])
        li = pl(inp["a_lam_im"][layer][g0:g0 + 8])
        ldt = pl(np.repeat(inp["a_log_dt"][layer][g0:g0 + 8][:, None], 64, axis=1))
        br = _c(inp["a_b_re"][layer][g0:g0 + 8].reshape(4, 2, 64, 16).transpose(1, 2, 0, 3).reshape(128, 4, 16))
        bi = _c(inp["a_b_im"][layer][g0:g0 + 8].reshape(4, 2, 64, 16).transpose(1, 2, 0, 3).reshape(128, 4, 16))
        crT = _c(inp["a_c_re"][layer][g0:g0 + 8].reshape(4, 2, 16, 64).transpose(1, 3, 0, 2).reshape(128, 4, 16))
        ciT = _c(inp["a_c_im"][layer][g0:g0 + 8].reshape(4, 2, 16, 64).transpose(1, 3, 0, 2).reshape(128, 4, 16))
        dv = _c(inp["a_d"][layer][128 * c:128 * c + 128].reshape(128, 1))
        maps.append(dict(uT=_c(uT[128 * c:128 * c + 128]), lr=lr, li=li, ldt=ldt, br=br, bi=bi, crT=crT, ciT=ciT,
                         dvec=dv, ident=ident))
    return maps


def attn_maps(o16, o32, kvT, inp, consts):
    maps = []
    for c in range(NCORES):
        b, G = c // 4, c % 4
        ts = slice(b * SEQ, (b + 1) * SEQ)
        m = dict(consts)
        m["qT"] = _c(o16[512 * G:512 * G + 512, ts].reshape(4, 128, SEQ).transpose(1, 0, 2))
        szr = np.stack([o16[2048 + br * 2048 + 512 * G:2048 + br * 2048 + 512 * G + 512, ts].reshape(4, 128, SEQ)
                        for br in range(3)], 0)
        m["szT"] = _c(szr.transpose(2, 0, 1, 3).reshape(128, 12, SEQ))
        m["gateT"] = _c(np.stack([o32[br * 16 + 4 * G:br * 16 + 4 * G + 4, ts] for br in range(3)], 0).reshape(12, SEQ))

        def slot(s_):
            return kvT[s_ * 512 + G * 128:s_ * 512 + G * 128 + 128, ts]

        m["kcT"] = _c(slot(0))
        m["vcT"] = _c(slot(1))
        m["ksT"] = _c(slot(2))
        m["kwT"] = _c(slot(4))
        m["vs"] = _c(slot(3).T.reshape(SEQ // 128, 128, 128).transpose(1, 0, 2))
        m["vw"] = _c(slot(5).T.reshape(SEQ // 128, 128, 128).transpose(1, 0, 2))
        m["poskT"] = _c(inp["cmp_pos_k"].T)
        m["posvT"] = _c(inp["cmp_pos_v"].T)
        m["w1k"] = _c(inp["cmp_w1_k"].reshape(32, 128, 128).transpose(1, 0, 2))
        m["w1v"] = _c(inp["cmp_w1_v"].reshape(32, 128, 128).transpose(1, 0, 2))
        m["w2k"] = _c(inp["cmp_w2_k"])
        m["w2v"] = _c(inp["cmp_w2_v"])
        maps.append(m)
    return maps


def kernel(**inputs):
    inp = {k: np.asarray(v) for k, v in inputs.items()}
    T = BATCH * SEQ // NCORES
    x = _c(inp["x"].reshape(BATCH * SEQ, D_MODEL).astype(np.float32))

    def tok(a, c):
        return a[c * T:(c + 1) * T]

    def bc(v):
        return _c(np.broadcast_to(v, (128, D_MODEL)))

    tiles_a = [(128, "id", "o32")] * 8 + [(128, "silu", "o32")] * 8
    tiles_kv = [(128, "id", "o16")] * 24
    tiles_b = [(128, "scale", "o16")] * 16 + [(128, "silu", "o16")] * 48 + [(48, "sigmoid", "o32")]
    nc_proj_a = build_proj(16, T, tiles_a)
    nc_scan = build_scan()
    nc_tail_a = build_tail(8, T, True)
    for layer in range(2):
        wa = proj_host_w(inp["a_w_in"][layer], tiles_a)
        res = run(nc_proj_a, [dict(xT=host_fm(tok(x, c).T), w=wa) for c in range(NCORES)])
        o = np.concatenate([res[c]["o32"] for c in range(NCORES)], axis=1)
        uT, szT = o[:1024], o[1024:]
        res = run(nc_scan, scan_maps(inp, layer, uT))
        gT = np.concatenate([res[c]["gT"] for c in range(NCORES)], axis=0)
        bglu = _c(inp["a_b_glu"][layer].reshape(8, 128).T)
        res = run(nc_tail_a, [dict(x=_c(tok(x, c)), w=host_fm(inp["a_w_out"][layer]), lng=bc(inp["ln_g"][layer]), lnb=bc(inp["ln_b"][layer]),
                                   gT=host_fm(gT[:, c * T:(c + 1) * T]), szT=host_fm(szT[:, c * T:(c + 1) * T]),
                                   wglu=host_fm(inp["a_w_glu"][layer]), bglu=bglu) for c in range(NCORES)])
        x = np.concatenate([res[c]["xo"] for c in range(NCORES)], axis=0)
    nc_proj_kv = build_proj(16, T, tiles_kv)
    wkv = proj_host_w(inp["kv_w"], tiles_kv)
    res = run(nc_proj_kv, [dict(xT=host_fm(tok(x, c).T), w=wkv) for c in range(NCORES)])
    kvT = np.concatenate([res[c]["o16"] for c in range(NCORES)], axis=1)
    nc_proj_b = build_proj(16, T, tiles_b)
    nc_attn = build_attn()
    nc_tail_b = build_tail(16, T, False, BF16)
    consts = attn_consts()
    for li in range(2):
        layer = 2 + li
        wbi = proj_host_w(inp["b_w_in"][li], tiles_b)
        res = run(nc_proj_b, [dict(xT=host_fm(tok(x, c).T), w=wbi) for c in range(NCORES)])
        o16 = np.concatenate([res[c]["o16"] for c in range(NCORES)], axis=1)
        o32 = np.concatenate([res[c]["o32"] for c in range(NCORES)], axis=1)
        res = run(nc_attn, attn_maps(o16, o32, kvT, inp, consts))
        yT = np.concatenate([np.concatenate([res[b * 4 + G]["yT"].transpose(1, 0, 2).reshape(512, SEQ) for G in range(4)], axis=0)
                             for b in range(BATCH)], axis=1)
        res = run(nc_tail_b, [dict(x=_c(tok(x, c)), w=host_fm(inp["b_w_out"][li]), lng=bc(inp["ln_g"][layer]), lnb=bc(inp["ln_b"][layer]),
                                   yT=host_fm(yT[:, c * T:(c + 1) * T])) for c in range(NCORES)])
        x = np.concatenate([res[c]["xo"] for c in range(NCORES)], axis=0)
    return x.reshape(BATCH, SEQ, D_MODEL).astype(np.float32)
```

```python
import contextlib
import numpy as np
import ml_dtypes
import concourse.bass as bass
import concourse.mybir as mybir
from concourse.bass_utils import run_bass_kernel_spmd

F32 = mybir.dt.float32
BF16 = mybir.dt.bfloat16
AF = mybir.ActivationFunctionType
ALU = mybir.AluOpType
NPBF = ml_dtypes.bfloat16

D_MODEL = 2048
BATCH = 2
SEQ = 4096
NCORES = 8
LN_EPS = 1e-5
ALPHA = 8 ** 0.25
SCALE = 128 ** -0.5
NEG = -30000.0


class Prog:
    COMPUTE = ["pe", "act", "dve", "pool"]
    QUEUES = {"sp": "sp", "qa": "act", "qp": "pool"}
    STREAMS = ["pe", "act", "dve", "pool", "sp"]
    NSLOT = 6

    def __init__(self, nc, same_sync=True):
        self.nc = nc
        self.owners = list(self.COMPUTE) + [f"{q}{j}" for q in self.QUEUES for j in range(self.NSLOT)]
        self.stream = {e: e for e in self.COMPUTE}
        for q, st in self.QUEUES.items():
            for j in range(self.NSLOT):
                self.stream[f"{q}{j}"] = st
        self.inc = {e: (1 if e in self.COMPUTE else 16) for e in self.owners}
        self.q = {e: [] for e in self.STREAMS}
        self.cnt = {e: 0 for e in self.owners}
        self.seen = {st: {o: 0 for o in self.owners} for st in self.STREAMS}
        self.qn = {q: 0 for q in self.QUEUES}
        self.lastw = {}
        self.reads = {}
        self.same_sync = same_sync
        self.seq = 0
        self.cap = None

    def _need(self, eng, other, idx):
        st = self.stream[eng]
        if other == eng and eng == "pe":
            return
        if other == eng and not self.same_sync and eng in self.COMPUTE:
            return
        if self.seen[st][other] >= idx:
            return
        self.seen[st][other] = idx
        self.seq += 1
        self.q[st].append(("wait", other, idx, self.seq))

    def barrier(self):
        rep = {"pe": "pe", "act": "act", "dve": "dve", "pool": "pool", "sp": "sp0"}
        for st in self.STREAMS:
            for o in self.owners:
                if self.cnt[o] > 0:
                    self._need(rep[st], o, self.cnt[o])

    def capture(self):
        self.cap = []
        return self.cap

    def end_capture(self):
        c, self.cap = self.cap, None
        return c

    def replay(self, *streams):
        pos = [0] * len(streams)
        tot = [max(1, len(x)) for x in streams]
        n = sum(len(x) for x in streams)
        for _ in range(n):
            best, bi = None, None
            for i, x in enumerate(streams):
                if pos[i] < len(x):
                    f = pos[i] / tot[i]
                    if best is None or f < best:
                        best, bi = f, i
            a = streams[bi][pos[bi]]
            pos[bi] += 1
            self.op(*a)

    def op(self, eng, fn, reads=(), writes=()):
        if self.cap is not None:
            self.cap.append((eng, fn, tuple(reads), tuple(writes)))
            return None
        if eng in self.QUEUES:
            j = self.qn[eng] % self.NSLOT
            self.qn[eng] += 1
            eng = f"{eng}{j}"
            if self.cnt[eng] > 0:
                self._need(eng, eng, self.cnt[eng])
        for k in reads:
            w = self.lastw.get(k)
            if w is not None:
                self._need(eng, *w)
        for k in writes:
            w = self.lastw.get(k)
            if w is not None:
                self._need(eng, *w)
            for (o, i) in self.reads.get(k, {}).items():
                self._need(eng, o, i)
        self.cnt[eng] += 1
        idx = self.cnt[eng]
        self.seq += 1
        self.q[self.stream[eng]].append(("op", fn, idx, self.seq, eng))
        for k in reads:
            self.reads.setdefault(k, {})[eng] = idx
        for k in writes:
            self.lastw[k] = (eng, idx)
            self.reads[k] = {}
        return idx

    def emit(self):
        nc = self.nc
        with contextlib.ExitStack() as st:
            sems = {e: st.enter_context(nc.semaphore("sem_" + e)) for e in self.owners}
            block = st.enter_context(nc.Block())
            engobj = {"pe": block.tensor, "act": block.scalar, "dve": block.vector,
                      "pool": block.gpsimd, "sp": block.sync}
            for e in self.STREAMS:
                items = self.q[e]

                def body(engine, items=items, e=e):
                    for it in items:
                        if it[0] == "wait":
                            _, o, idx, _s = it
                            engine.wait_ge(sems[o], idx * self.inc[o])
                        else:
                            _, fn, idx, _s, owner = it
                            ins = fn(engine)
                            ins.then_inc(sems[owner], self.inc[owner])
                    if e == "sp":
                        for o in self.owners:
                            if self.cnt[o] > 0:
                                engine.wait_ge(sems[o], self.cnt[o] * self.inc[o])
                engobj[e](body)


class KB:
    def __init__(self, name="k"):
        self.nc = bass.Bass("TRN2", target_bir_lowering=False)
        self.st = contextlib.ExitStack()
        self.P = Prog(self.nc)
        self.nps = 0
        self.uid = 0

    def dram_in(self, name, shape, dt=F32):
        return self.nc.dram_tensor(name, list(shape), dt, kind="ExternalInput").ap()

    def dram_out(self, name, shape, dt=F32):
        return self.nc.dram_tensor(name, list(shape), dt, kind="ExternalOutput").ap()

    def sb(self, name, shape, dt=F32):
        return self.st.enter_context(self.nc.sbuf_tensor(name, list(shape), dt))

    def ps(self, name, shape, dt=F32):
        return self.st.enter_context(self.nc.psum_tensor(name, list(shape), dt))

    def finish(self):
        self.P.emit()
        self.st.close()
        return self.nc


def run(nc, in_maps):
    res = run_bass_kernel_spmd(nc, in_maps, core_ids=list(range(len(in_maps))))
    return res.results


def act_apply(e, out, in_, act):
    if act == "id":
        return e.copy(out=out, in_=in_)
    if act == "scale":
        return e.mul(out=out, in_=in_, mul=SCALE)
    if act == "silu":
        return e.activation(out=out, in_=in_, func=AF.Silu)
    if act == "sigmoid":
        return e.activation(out=out, in_=in_, func=AF.Sigmoid)
    raise ValueError(act)


def proj_blocks(tiles, WB=512):
    blocks, cur, curw = [], [], 0
    for t in tiles:
        if curw + t[0] > WB:
            blocks.append(cur)
            cur, curw = [], 0
        cur.append(t)
        curw += t[0]
    if cur:
        blocks.append(cur)
    return blocks


def proj_host_w(w, tiles, WB=512):
    K_, N = w.shape
    KC = K_ // 128
    blocks = proj_blocks(tiles, WB)
    out = np.zeros((len(blocks), 128, KC, WB), np.float32)
    c0 = 0
    for bi, tl in enumerate(blocks):
        bw = sum(t[0] for t in tl)
        out[bi, :, :, :bw] = w[:, c0:c0 + bw].reshape(KC, 128, bw).transpose(1, 0, 2)
        c0 += bw
    return out


def host_fm(aT):
    K_, T = aT.shape
    return np.ascontiguousarray(aT.reshape(K_ // 128, 128, T).transpose(1, 0, 2))


def build_proj(KC, T, tiles):
    kb = KB()
    P = kb.P
    WB = 512
    blocks = proj_blocks(tiles, WB)
    n32 = sum(t[0] for t in tiles if t[2] == "o32")
    n16 = sum(t[0] for t in tiles if t[2] == "o16")
    xT = kb.dram_in("xT", [128, KC, T])
    w = kb.dram_in("w", [len(blocks), 128, KC, WB])
    outs = {}
    if n32:
        outs["o32"] = kb.dram_out("o32", [n32, T], F32)
    if n16:
        outs["o16"] = kb.dram_out("o16", [n16, T], BF16)
    xb = kb.sb("xb", [128, KC, T], BF16)
    KG = 4
    LQ = ["sp", "qa", "qp"]
    xst = [kb.sb(f"xst{i}", [128, KG, T], F32) for i in range(2)]
    wst = [kb.sb(f"wst{i}", [128, KC, WB], F32) for i in range(2)]
    wb = [kb.sb(f"wb{i}", [128, KC, WB], BF16) for i in range(2)]
    o32s = [kb.sb(f"o32s{i}", [128, 512], F32) for i in range(2)]
    o16s = [kb.sb(f"o16s{i}", [128, 512], BF16) for i in range(2)]
    pss = [kb.ps(f"ps{i}", [128, 512], F32) for i in range(8)]
    H = KC // 2

    def load_w(bi):
        bw = sum(t[0] for t in blocks[bi])
        ws, wbb = wst[bi % 2], wb[bi % 2]
        P.op(LQ[(2 * bi) % 3], lambda e: e.dma_start(out=ws[:, 0:H, 0:bw], in_=w[bi, :, 0:H, 0:bw]), writes=[f"wst{bi % 2}a"])
        P.op(LQ[(2 * bi + 1) % 3], lambda e: e.dma_start(out=ws[:, H:KC, 0:bw], in_=w[bi, :, H:KC, 0:bw]), writes=[f"wst{bi % 2}b"])
        P.op("dve", lambda e: e.tensor_copy(out=wbb[:, 0:H, 0:bw], in_=ws[:, 0:H, 0:bw]), reads=[f"wst{bi % 2}a"], writes=[f"wb{bi % 2}a"])
        P.op("act", lambda e: e.copy(out=wbb[:, H:KC, 0:bw], in_=ws[:, H:KC, 0:bw]), reads=[f"wst{bi % 2}b"], writes=[f"wb{bi % 2}b"])

    for i in range(KC // KG):
        s_ = xst[i % 2]
        P.op(LQ[i % 3], lambda e, s_=s_, i=i: e.dma_start(out=s_[:], in_=xT[:, i * KG:(i + 1) * KG, :]), writes=[f"xst{i % 2}"])
        P.op("dve" if i % 2 == 0 else "pool", lambda e, s_=s_, i=i: e.tensor_copy(out=xb[:, i * KG:(i + 1) * KG, :], in_=s_[:]),
             reads=[f"xst{i % 2}"], writes=[f"xb{i}"])
    load_w(0)
    rowpos = {"o32": 0, "o16": 0}
    psi = 0
    oi = {"o32": 0, "o16": 0}
    for bi, tl in enumerate(blocks):
        if bi + 1 < len(blocks):
            load_w(bi + 1)
        wbb = wb[bi % 2]
        off = 0
        for (nc_, act, oname) in tl:
            for n in range(T // 512):
                ps = pss[psi % 8]
                pk = f"ps{psi % 8}"
                psi += 1
                for k in range(KC):
                    P.op("pe", lambda e, ps=ps, wbb=wbb, off=off, nc_=nc_, k=k, n=n: e.matmul(
                        ps[0:nc_, :], wbb[:, k, off:off + nc_], xb[:, k, n * 512:(n + 1) * 512],
                        start=(k == 0), stop=(k == KC - 1)),
                        reads=[f"wb{bi % 2}" + ("a" if k < H else "b"), f"xb{k // KG}"], writes=[pk])
                j = oi[oname] % 2
                oi[oname] += 1
                osb = (o32s if oname == "o32" else o16s)[j]
                ok = f"{oname}s{j}"
                P.op("act", lambda e, osb=osb, ps=ps, nc_=nc_, act=act: act_apply(e, osb[0:nc_, :], ps[0:nc_, :], act),
                     reads=[pk], writes=[ok])
                r0 = rowpos[oname]
                od = outs[oname]
                P.op("sp", lambda e, od=od, osb=osb, r0=r0, nc_=nc_, n=n: e.dma_start(
                    out=od[r0:r0 + nc_, n * 512:(n + 1) * 512], in_=osb[0:nc_, :]),
                    reads=[ok])
            rowpos[oname] += nc_
            off += nc_
    return kb.finish()


def build_tail(KC, T, glu, ydt=F32):
    kb = KB()
    P = kb.P
    x = kb.dram_in("x", [T, D_MODEL])
    w_v = kb.dram_in("w", [128, KC, D_MODEL])
    lng = kb.dram_in("lng", [128, D_MODEL])
    lnb = kb.dram_in("lnb", [128, D_MODEL])
    xo = kb.dram_out("xo", [T, D_MODEL])
    wb = kb.sb("wb", [128, KC, D_MODEL], BF16)
    yb = kb.sb("yb", [128, KC, T], BF16)
    STF = KC * 256
    stg = [kb.sb(f"stg{i}", [128, STF], F32) for i in range(2)]
    lng_s = kb.sb("lng_s", [128, D_MODEL], F32)
    lnb_s = kb.sb("lnb_s", [128, D_MODEL], F32)
    xt = [kb.sb(f"xt{i}", [128, D_MODEL], F32) for i in range(2)]
    stats = kb.sb("stats", [128, 4, 6], F32)
    mv = kb.sb("mv", [128, 2], F32)
    rstd = kb.sb("rstd", [128, 1], F32)
    epsc = kb.sb("epsc", [128, 1], F32)
    pss = [kb.ps(f"ps{i}", [128, 512], F32) for i in range(8)]
    psi = [0]

    def nextps():
        i = psi[0] % 8
        psi[0] += 1
        return pss[i], f"ps{i}"

    P.op("sp", lambda e: e.dma_start(out=lng_s[:], in_=lng[:, :]), writes=["lng"])
    P.op("sp", lambda e: e.dma_start(out=lnb_s[:], in_=lnb[:, :]), writes=["lnb"])
    P.op("pool", lambda e: e.memset(epsc[:], LN_EPS), writes=["epsc"])
    si = [0]

    def load_cast(dst, dkey, src_v, ncols, engs=("dve", "act")):
        kc = src_v.shape[1]
        kg = max(1, STF // ncols)
        for k0 in range(0, kc, kg):
            kn = min(kg, kc - k0)
            j = si[0] % 2
            q = ("sp", "qa", "qp")[si[0] % 3]
            si[0] += 1
            sv = stg[j][:, 0:kn * ncols].rearrange("p (k n) -> p k n", k=kn)
            P.op(q, lambda e, sv=sv, k0=k0, kn=kn: e.dma_start(out=sv, in_=src_v[:, k0:k0 + kn, :]), writes=[f"stg{j}"])
            wkeys = [f"{dkey}{kk}" for kk in range(k0, k0 + kn)]
            if engs[j % len(engs)] == "act":
                P.op("act", lambda e, sv=sv, k0=k0, kn=kn: e.copy(out=dst[:, k0:k0 + kn, 0:ncols], in_=sv), reads=[f"stg{j}"], writes=wkeys)
            else:
                P.op(engs[j % len(engs)], lambda e, sv=sv, k0=k0, kn=kn: e.tensor_copy(out=dst[:, k0:k0 + kn, 0:ncols], in_=sv),
                     reads=[f"stg{j}"], writes=wkeys)

    if glu:
        gT_v = kb.dram_in("gT", [128, KC, T])
        szT_v = kb.dram_in("szT", [128, KC, T])
        wglu_v = kb.dram_in("wglu", [128, KC, KC * 128])
        bglu = kb.dram_in("bglu", [128, KC])
        wgb = kb.sb("wgb", [128, KC, KC * 128], BF16)
        bg = kb.sb("bg", [128, KC], F32)
        g32 = kb.sb("g32", [128, KC, 512], F32)
        sz32 = kb.sb("sz32", [128, KC, 512], F32)
        gb = kb.sb("gb", [128, KC, 512], BF16)
        sig = [kb.sb(f"sig{i}", [128, 512], F32) for i in range(2)]
        P.op("sp", lambda e: e.dma_start(out=bg[:], in_=bglu[:, :]), writes=["bg"])
        load_cast(wgb, "wgb", wglu_v, KC * 128)
        for n in range(T // 512):
            P.op("sp", lambda e, n=n: e.dma_start(out=g32[:], in_=gT_v[:, :, n * 512:(n + 1) * 512]), writes=["g32"])
            P.op("qa", lambda e, n=n: e.dma_start(out=sz32[:], in_=szT_v[:, :, n * 512:(n + 1) * 512]), writes=["sz32"])
            P.op("pool", lambda e: e.tensor_copy(out=gb[:], in_=g32[:]), reads=["g32"], writes=["gb"])
            for m in range(KC):
                ps, pk = nextps()
                for k in range(KC):
                    P.op("pe", lambda e, ps=ps, k=k, m=m: e.matmul(ps[:], wgb[:, k, m * 128:(m + 1) * 128], gb[:, k, :],
                                                                   start=(k == 0), stop=(k == KC - 1)),
                         reads=[f"wgb{k}", "gb"], writes=[pk])
                sg = sig[m % 2]
                P.op("act", lambda e, sg=sg, ps=ps, m=m: e.activation(out=sg[:], in_=ps[:], func=AF.Sigmoid, bias=bg[:, m:m + 1]),
                     reads=[pk, "bg"], writes=[f"sig{m % 2}"])
                P.op("dve", lambda e, sg=sg, m=m: e.tensor_tensor(out=sg[:], in0=sg[:], in1=g32[:, m, :], op=ALU.mult),
                     reads=["g32"], writes=[f"sig{m % 2}"])
                P.op("dve", lambda e, sg=sg, m=m, n=n: e.tensor_tensor(out=yb[:, m, n * 512:(n + 1) * 512], in0=sg[:], in1=sz32[:, m, :], op=ALU.mult),
                     reads=["sz32", f"sig{m % 2}"], writes=["yb"])
    else:
        yT_v = kb.dram_in("yT", [128, KC, T], ydt)
        if ydt == BF16:
            P.op("sp", lambda e: e.dma_start(out=yb[:, 0:KC // 2, :], in_=yT_v[:, 0:KC // 2, :]), writes=["yb"])
            P.op("qa", lambda e: e.dma_start(out=yb[:, KC // 2:KC, :], in_=yT_v[:, KC // 2:KC, :]), writes=["yb"])
        else:
            load_cast(yb, "yb", yT_v, T)
    load_cast(wb, "wb", w_v, D_MODEL)
    x_v = x.rearrange("(n p) d -> n p d", p=128)
    xo_v = xo.rearrange("(n p) d -> n p d", p=128)
    for tt in range(T // 128):
        xs = xt[tt % 2]
        xk = f"xt{tt % 2}"
        P.op("qa" if tt % 2 else "sp", lambda e, xs=xs, tt=tt: e.dma_start(out=xs[:], in_=x_v[tt]), writes=[xk])
        for n in range(4):
            ps, pk = nextps()
            for k in range(KC):
                P.op("pe", lambda e, ps=ps, k=k, tt=tt, n=n: e.matmul(ps[:], yb[:, k, tt * 128:(tt + 1) * 128], wb[:, k, n * 512:(n + 1) * 512],
                                                                      start=(k == 0), stop=(k == KC - 1)),
                     reads=["yb", f"wb{k}"], writes=[pk])
            P.op("dve", lambda e, xs=xs, ps=ps, n=n: e.scalar_tensor_tensor(out=xs[:, n * 512:(n + 1) * 512], in0=xs[:, n * 512:(n + 1) * 512], scalar=ALPHA,
                                                                            in1=ps[:], op0=ALU.mult, op1=ALU.add),
                 reads=[pk], writes=[xk])
            P.op("dve", lambda e, xs=xs, n=n: e.bn_stats(out=stats[:, n, :], in_=xs[:, n * 512:(n + 1) * 512]),
                 reads=[xk], writes=["stats"])
        P.op("dve", lambda e: e.bn_aggr(out=mv[:], in_=stats[:]), reads=["stats"], writes=["mv"])
        P.op("act", lambda e: e.activation(out=rstd[:], in_=mv[:, 1:2], func=AF.Sqrt, bias=epsc[:]), reads=["mv", "epsc"], writes=["rstd"])
        P.op("dve", lambda e: e.reciprocal(out=rstd[:], in_=rstd[:]), reads=["rstd"], writes=["rstd"])
        P.op("dve", lambda e, xs=xs: e.tensor_scalar(out=xs[:], in0=xs[:], scalar1=mv[:, 0:1], scalar2=rstd[:], op0=ALU.subtract, op1=ALU.mult),
             reads=["mv", "rstd"], writes=[xk])
        P.op("pool", lambda e, xs=xs: e.tensor_tensor(out=xs[:], in0=xs[:], in1=lng_s[:], op=ALU.mult), reads=["lng"], writes=[xk])
        P.op("pool", lambda e, xs=xs: e.tensor_tensor(out=xs[:], in0=xs[:], in1=lnb_s[:], op=ALU.add), reads=["lnb"], writes=[xk])
        P.op("sp", lambda e, xs=xs, tt=tt: e.dma_start(out=xo_v[tt], in_=xs[:]), reads=[xk])
    return kb.finish()


def build_scan(L=4096, NB=2):
    kb = KB()
    P = kb.P
    nc = kb.nc
    NT = L * NB
    NLEV = int(np.log2(L))
    CH = 1024
    NCH = L // CH
    uT = kb.dram_in("uT", [128, NT])
    lr_d = kb.dram_in("lr", [128, 4])
    li_d = kb.dram_in("li", [128, 4])
    ldt_d = kb.dram_in("ldt", [128, 4])
    br_d = kb.dram_in("br", [128, 4, 16])
    bi_d = kb.dram_in("bi", [128, 4, 16])
    cr_d = kb.dram_in("crT", [128, 4, 16])
    ci_d = kb.dram_in("ciT", [128, 4, 16])
    d_d = kb.dram_in("dvec", [128, 1])
    id_d = kb.dram_in("ident", [128, 128])
    gT = kb.dram_out("gT", [128, NT])

    ub = kb.sb("ub", [128, NT], BF16)
    yd = kb.sb("yd", [128, NT], F32)
    ident = kb.sb("ident_s", [128, 128], F32)
    sm = {n: kb.sb("sm_" + n, [128, 4], F32) for n in ["lr", "li", "dt", "th", "mag", "c", "s", "t1", "t2", "t3", "ar", "ai",
                                                       "den", "am1", "cr", "ci", "nci"]}
    br = kb.sb("br_s", [128, 4, 16], F32)
    bi = kb.sb("bi_s", [128, 4, 16], F32)
    crT = kb.sb("crT_s", [128, 4, 16], F32)
    ciT = kb.sb("ciT_s", [128, 4, 16], F32)
    bbr = kb.sb("bbr", [128, 4, 16], F32)
    bbi = kb.sb("bbi", [128, 4, 16], F32)
    tmpb = kb.sb("tmpb", [128, 4, 16], F32)
    Bblk = {n: kb.sb(n, [128, 4, 32], F32) for n in ["Bblk_r", "Bblk_i"]}
    BT = {n: kb.sb(n, [128, 128], BF16) for n in ["BT_r", "BT_i"]}
    Cpad = {n: kb.sb(n, [128, 4, 128], F32) for n in ["Cpad_r", "Cpad_ni"]}
    pc = kb.sb("pc", [128, NLEV, 4], F32)
    psn = kb.sb("psn", [128, NLEV, 4], F32)
    pns = kb.sb("pns", [128, NLEV, 4], F32)
    dv = kb.sb("dv", [128, 1], F32)
    halfpi = kb.sb("halfpi", [128, 1], F32)
    mtop = kb.sb("mtop", [128, 1], F32)
    mbot = kb.sb("mbot", [128, 1], F32)
    nmtop = kb.sb("nmtop", [128, 1], F32)
    nmbot = kb.sb("nmbot", [128, 1], F32)
    gt = [kb.sb(f"gt{i}", [128, 512], F32) for i in range(2)]
    gs = [kb.sb(f"gs{i}", [128, 512], F32) for i in range(2)]
    st0 = contextlib.ExitStack()
    u32 = st0.enter_context(nc.sbuf_tensor("u32", [128, NT], F32))
    pss = [kb.ps(f"ps{i}", [128, 512], F32) for i in range(8)]
    psi = [0]

    def nextps():
        i = psi[0] % 8
        psi[0] += 1
        return pss[i], f"ps{i}"

    def dma_in(dst, src, key, q="sp"):
        P.op(q, lambda e: e.dma_start(out=dst, in_=src), writes=[key])

    for q in range(4):
        P.op(("sp", "qa")[q % 2], lambda e, q=q: e.dma_start(out=u32[:, q * NT // 4:(q + 1) * NT // 4], in_=uT[:, q * NT // 4:(q + 1) * NT // 4]), writes=[f"u32_{q}"])
        P.op("dve", lambda e, q=q: e.tensor_copy(out=ub[:, q * NT // 4:(q + 1) * NT // 4], in_=u32[:, q * NT // 4:(q + 1) * NT // 4]), reads=[f"u32_{q}"], writes=[f"ub_{q}"])
    UBK = [f"ub_{q}" for q in range(4)]
    dma_in(sm["lr"][:], lr_d[:, :], "lr")
    dma_in(sm["li"][:], li_d[:, :], "li")
    dma_in(sm["dt"][:], ldt_d[:, :], "dt")
    dma_in(br[:], br_d[:, :, :], "br")
    dma_in(bi[:], bi_d[:, :, :], "bi")
    dma_in(crT[:], cr_d[:, :, :], "crT")
    dma_in(ciT[:], ci_d[:, :, :], "ciT")
    dma_in(dv[:], d_d[:, :], "dv")
    dma_in(ident[:], id_d[:, :], "ident")
    P.op("pool", lambda e: e.memset(halfpi[:], float(np.pi / 2)), writes=["halfpi"])
    for (t_, a, b) in [(mtop, 1.0, 0.0), (mbot, 0.0, 1.0), (nmtop, -1.0, 0.0), (nmbot, 0.0, -1.0)]:
        P.op("pool", lambda e, t_=t_, a=a: e.memset(t_[0:64, :], a), writes=["masks"])
        P.op("pool", lambda e, t_=t_, b=b: e.memset(t_[64:128, :], b), writes=["masks"])
    for n in ["Bblk_r", "Bblk_i"]:
        P.op("pool", lambda e, n=n: e.memset(Bblk[n][:], 0.0), writes=[n])
    for n in ["Cpad_r", "Cpad_ni"]:
        P.op("pool", lambda e, n=n: e.memset(Cpad[n][:], 0.0), writes=[n])

    def tt(out, a, b, op, rk, wk, eng="dve"):
        P.op(eng, lambda e: e.tensor_tensor(out=out, in0=a, in1=b, op=op), reads=rk, writes=wk)

    def S(n):
        return sm[n][:]

    def renorm(c_ap, s_ap, ck, sk):
        tt(S("t1"), c_ap, c_ap, ALU.mult, [ck], ["t1"])
        tt(S("t2"), s_ap, s_ap, ALU.mult, [sk], ["t2"])
        tt(S("t1"), S("t1"), S("t2"), ALU.add, ["t2"], ["t1"])
        P.op("dve", lambda e: e.tensor_scalar(out=S("t1"), in0=S("t1"), scalar1=-0.5, scalar2=1.5, op0=ALU.mult, op1=ALU.add), writes=["t1"])
        tt(c_ap, c_ap, S("t1"), ALU.mult, ["t1"], [ck])
        tt(s_ap, s_ap, S("t1"), ALU.mult, ["t1"], [sk])

    P.op("act", lambda e: e.activation(out=S("dt"), in_=S("dt"), func=AF.Exp), reads=["dt"], writes=["dt"])
    tt(S("th"), S("li"), S("dt"), ALU.mult, ["li", "dt"], ["th"])
    tt(S("t1"), S("lr"), S("dt"), ALU.mult, ["lr", "dt"], ["t1"])
    P.op("act", lambda e: e.activation(out=S("mag"), in_=S("t1"), func=AF.Exp), reads=["t1"], writes=["mag"])
    P.op("act", lambda e: e.activation(out=S("s"), in_=S("th"), func=AF.Sin, scale=1.0 / 16), reads=["th"], writes=["s"])
    P.op("act", lambda e: e.activation(out=S("c"), in_=S("th"), func=AF.Sin, scale=1.0 / 16, bias=halfpi[:]), reads=["th", "halfpi"], writes=["c"])
    for _ in range(4):
        tt(S("t1"), S("c"), S("c"), ALU.mult, ["c"], ["t1"])
        tt(S("t2"), S("s"), S("s"), ALU.mult, ["s"], ["t2"])
        tt(S("t3"), S("c"), S("s"), ALU.mult, ["c", "s"], ["t3"])
        tt(S("c"), S("t1"), S("t2"), ALU.subtract, ["t1", "t2"], ["c"])
        tt(S("s"), S("t3"), S("t3"), ALU.add, ["t3"], ["s"])
    renorm(S("c"), S("s"), "c", "s")
    tt(S("ar"), S("mag"), S("c"), ALU.mult, ["mag", "c"], ["ar"])
    tt(S("ai"), S("mag"), S("s"), ALU.mult, ["mag", "s"], ["ai"])
    P.op("dve", lambda e: e.tensor_copy(out=pc[:, 0, :], in_=S("c")), reads=["c"], writes=["pc"])
    P.op("dve", lambda e: e.tensor_copy(out=psn[:, 0, :], in_=S("s")), reads=["s"], writes=["psn"])
    for lev in range(1, NLEV):
        tt(S("t1"), pc[:, lev - 1, :], pc[:, lev - 1, :], ALU.mult, ["pc"], ["t1"])
        tt(S("t2"), psn[:, lev - 1, :], psn[:, lev - 1, :], ALU.mult, ["psn"], ["t2"])
        tt(S("t3"), pc[:, lev - 1, :], psn[:, lev - 1, :], ALU.mult, ["pc", "psn"], ["t3"])
        tt(pc[:, lev, :], S("t1"), S("t2"), ALU.subtract, ["t1", "t2"], ["pc"])
        tt(psn[:, lev, :], S("t3"), S("t3"), ALU.add, ["t3"], ["psn"])
        renorm(pc[:, lev, :], psn[:, lev, :], "pc", "psn")
    P.op("dve", lambda e: e.tensor_scalar(out=pns[:], in0=psn[:], scalar1=-1.0, scalar2=None, op0=ALU.mult), reads=["psn"], writes=["pns"])
    tt(S("t1"), S("lr"), S("lr"), ALU.mult, ["lr"], ["t1"])
    tt(S("t2"), S("li"), S("li"), ALU.mult, ["li"], ["t2"])
    tt(S("den"), S("t1"), S("t2"), ALU.add, ["t1", "t2"], ["den"])
    P.op("dve", lambda e: e.reciprocal(out=S("den"), in_=S("den")), reads=["den"], writes=["den"])
    P.op("dve", lambda e: e.tensor_scalar(out=S("am1"), in0=S("ar"), scalar1=-1.0, scalar2=None, op0=ALU.add), reads=["ar"], writes=["am1"])
    tt(S("t1"), S("am1"), S("lr"), ALU.mult, ["am1", "lr"], ["t1"])
    tt(S("t2"), S("ai"), S("li"), ALU.mult, ["ai", "li"], ["t2"])
    tt(S("t3"), S("t1"), S("t2"), ALU.add, ["t1", "t2"], ["t3"])
    tt(S("cr"), S("t3"), S("den"), ALU.mult, ["t3", "den"], ["cr"])
    tt(S("t1"), S("ai"), S("lr"), ALU.mult, ["ai", "lr"], ["t1"])
    tt(S("t2"), S("am1"), S("li"), ALU.mult, ["am1", "li"], ["t2"])
    tt(S("t3"), S("t1"), S("t2"), ALU.subtract, ["t1", "t2"], ["t3"])
    tt(S("ci"), S("t3"), S("den"), ALU.mult, ["t3", "den"], ["ci"])
    P.op("dve", lambda e: e.tensor_scalar(out=S("nci"), in0=S("ci"), scalar1=-1.0, scalar2=None, op0=ALU.mult), reads=["ci"], writes=["nci"])
    for j in range(4):
        P.op("dve", lambda e, j=j: e.tensor_scalar(out=tmpb[:, j, :], in0=bi[:, j, :], scalar1=sm["nci"][:, j:j + 1], scalar2=None, op0=ALU.mult),
             reads=["bi", "nci"], writes=["tmpb"])
        P.op("dve", lambda e, j=j: e.scalar_tensor_tensor(out=bbr[:, j, :], in0=br[:, j, :], scalar=sm["cr"][:, j:j + 1], in1=tmpb[:, j, :], op0=ALU.mult, op1=ALU.add),
             reads=["br", "cr", "tmpb"], writes=["bbr"])
        P.op("dve", lambda e, j=j: e.tensor_scalar(out=tmpb[:, j, :], in0=br[:, j, :], scalar1=sm["ci"][:, j:j + 1], scalar2=None, op0=ALU.mult),
             reads=["br", "ci"], writes=["tmpb"])
        P.op("dve", lambda e, j=j: e.scalar_tensor_tensor(out=bbi[:, j, :], in0=bi[:, j, :], scalar=sm["cr"][:, j:j + 1], in1=tmpb[:, j, :], op0=ALU.mult, op1=ALU.add),
             reads=["bi", "cr", "tmpb"], writes=["bbi"])
    for (src, n) in [(bbr, "Bblk_r"), (bbi, "Bblk_i")]:
        sk = "bbr" if n == "Bblk_r" else "bbi"
        P.op("dve", lambda e, src=src, n=n: e.tensor_scalar(out=Bblk[n][:, :, 0:16], in0=src[:], scalar1=mtop[:], scalar2=None, op0=ALU.mult),
             reads=[sk, "masks"], writes=[n])
        P.op("dve", lambda e, src=src, n=n: e.tensor_scalar(out=Bblk[n][:, :, 16:32], in0=src[:], scalar1=mbot[:], scalar2=None, op0=ALU.mult),
             reads=[sk, "masks"], writes=[n])
    for (n, bn) in [("Bblk_r", "BT_r"), ("Bblk_i", "BT_i")]:
        ps, pk = nextps()
        P.op("pe", lambda e, ps=ps, n=n: e.transpose(ps[:, 0:128], Bblk[n][:].rearrange("p j c -> p (j c)"), ident[:]), reads=[n, "ident"], writes=[pk])
        P.op("act", lambda e, ps=ps, bn=bn: e.copy(out=BT[bn][:], in_=ps[:, 0:128]), reads=[pk], writes=[bn])
    for j in range(4):
        for (cn, src, ma, mb, sk) in [("Cpad_r", crT, mtop, mbot, "crT"), ("Cpad_ni", ciT, nmtop, nmbot, "ciT")]:
            P.op("dve", lambda e, j=j, cn=cn, src=src, ma=ma: e.tensor_scalar(out=Cpad[cn][:, j, 32 * j:32 * j + 16], in0=src[:, j, :], scalar1=ma[:], scalar2=None, op0=ALU.mult),
                 reads=[sk, "masks"], writes=[cn])
            P.op("dve", lambda e, j=j, cn=cn, src=src, mb=mb: e.tensor_scalar(out=Cpad[cn][:, j, 32 * j + 16:32 * j + 32], in0=src[:, j, :], scalar1=mb[:], scalar2=None, op0=ALU.mult),
                 reads=[sk, "masks"], writes=[cn])
    NBLK = L // 512
    for q in range(4):
        P.op("pool", lambda e, q=q: e.tensor_scalar(out=yd[:, q * NT // 4:(q + 1) * NT // 4], in0=u32[:, q * NT // 4:(q + 1) * NT // 4], scalar1=dv[:], scalar2=None, op0=ALU.mult),
             reads=[f"u32_{q}", "dv"], writes=[f"yd_{q}"])
    P.barrier()
    st0.close()

    def ydk(tok):
        return f"yd_{tok // (NT // 4)}"

    Ec = kb.sb("Ec", [128, L], F32)
    Es = kb.sb("Es", [128, L], F32)
    magf = kb.sb("magf", [128, L], F32)
    Xr = kb.sb("Xr", [128, L], F32)
    Xi = kb.sb("Xi", [128, L], F32)
    T1 = [kb.sb(f"T1_{i}", [128, CH], F32) for i in range(2)]
    T2 = [kb.sb(f"T2_{i}", [128, CH], F32) for i in range(2)]
    XRK = [f"Xr{c}" for c in range(NCH)]
    XIK = [f"Xi{c}" for c in range(NCH)]
    tci = [0]

    def rotate(sign):
        for c in range(NCH):
            sl = slice(c * CH, (c + 1) * CH)
            k = tci[0] % 2
            tci[0] += 1
            t1, t2 = T1[k], T2[k]
            P.op("pool", lambda e, sl=sl, t1=t1: e.tensor_tensor(out=t1[:], in0=Xi[:, sl], in1=Es[:, sl], op=ALU.mult), reads=[XIK[c], "E"], writes=[f"T1_{k}"])
            P.op("pool", lambda e, sl=sl, t2=t2: e.tensor_tensor(out=t2[:], in0=Xr[:, sl], in1=Es[:, sl], op=ALU.mult), reads=[XRK[c], "E"], writes=[f"T2_{k}"])
            P.op("dve", lambda e, sl=sl: e.tensor_tensor(out=Xr[:, sl], in0=Xr[:, sl], in1=Ec[:, sl], op=ALU.mult), reads=["E"], writes=[XRK[c]])
            P.op("dve", lambda e, sl=sl: e.tensor_tensor(out=Xi[:, sl], in0=Xi[:, sl], in1=Ec[:, sl], op=ALU.mult), reads=["E"], writes=[XIK[c]])
            P.op("dve", lambda e, sl=sl, t1=t1: e.tensor_tensor(out=Xr[:, sl], in0=Xr[:, sl], in1=t1[:], op=(ALU.add if sign < 0 else ALU.subtract)), reads=[f"T1_{k}"], writes=[XRK[c]])
            P.op("dve", lambda e, sl=sl, t2=t2: e.tensor_tensor(out=Xi[:, sl], in0=Xi[:, sl], in1=t2[:], op=(ALU.subtract if sign < 0 else ALU.add)), reads=[f"T2_{k}"], writes=[XIK[c]])

    for j in range(4):
        P.op("dve", lambda e: e.memset(Ec[:, 0:1], 1.0), writes=["E"])
        P.op("dve", lambda e: e.memset(Es[:, 0:1], 0.0), writes=["E"])
        for lev in range(NLEV):
            n = 1 << lev
            C = pc[:, lev, j:j + 1]
            Sn = psn[:, lev, j:j + 1]
            Sm = pns[:, lev, j:j + 1]
            t1 = T1[0] if n <= CH else None
            for o in range(0, n, CH):
                w_ = min(CH, n - o)
                a, b = T1[0], T2[0]
                P.op("dve", lambda e, o=o, w_=w_, a=a, Sm=Sm: e.tensor_scalar(out=a[:, 0:w_], in0=Es[:, o:o + w_], scalar1=Sm, scalar2=None, op0=ALU.mult), reads=["E", "pns"], writes=["T1_0"])
                P.op("dve", lambda e, o=o, w_=w_, b=b, Sn=Sn: e.tensor_scalar(out=b[:, 0:w_], in0=Ec[:, o:o + w_], scalar1=Sn, scalar2=None, op0=ALU.mult), reads=["E", "psn"], writes=["T2_0"])
                P.op("dve", lambda e, o=o, w_=w_, a=a, C=C, n=n: e.scalar_tensor_tensor(out=Ec[:, n + o:n + o + w_], in0=Ec[:, o:o + w_], scalar=C, in1=a[:, 0:w_], op0=ALU.mult, op1=ALU.add),
                     reads=["T1_0", "pc"], writes=["E"])
                P.op("dve", lambda e, o=o, w_=w_, b=b, C=C, n=n: e.scalar_tensor_tensor(out=Es[:, n + o:n + o + w_], in0=Es[:, o:o + w_], scalar=C, in1=b[:, 0:w_], op0=ALU.mult, op1=ALU.add),
                     reads=["T2_0", "pc"], writes=["E"])
        P.op("act", lambda e, j=j: e.activation(out=magf[:], in_=Ec[:], func=AF.Identity, scale=0.0, bias=sm["mag"][:, j:j + 1]), reads=["E", "mag"], writes=["magf"])
        for b in range(NB):
            for blk in range(NBLK):
                t0 = b * L + blk * 512
                c = (blk * 512) // CH
                for (bn, X_, xk) in [("BT_r", Xr, XRK[c]), ("BT_i", Xi, XIK[c])]:
                    ps, pk = nextps()
                    P.op("pe", lambda e, ps=ps, bn=bn, t0=t0, j=j: e.matmul(ps[:], BT[bn][32 * j:32 * j + 32, :], ub[32 * j:32 * j + 32, t0:t0 + 512],
                                                                            start=True, stop=True, tile_position=(32 * j, 0)),
                         reads=[bn] + UBK, writes=[pk])
                    P.op("act", lambda e, ps=ps, X_=X_, blk=blk: e.copy(out=X_[:, blk * 512:(blk + 1) * 512], in_=ps[:]), reads=[pk], writes=[xk])
            rotate(-1)
            P.op("dve", lambda e: e.tensor_tensor_scan(out=Xr[:], data0=magf[:], data1=Xr[:], initial=0.0, op0=ALU.mult, op1=ALU.add), reads=["magf"], writes=XRK)
            P.op("dve", lambda e: e.tensor_tensor_scan(out=Xi[:], data0=magf[:], data1=Xi[:], initial=0.0, op0=ALU.mult, op1=ALU.add), reads=["magf"], writes=XIK)
            rotate(+1)
            for blk in range(NBLK):
                t0 = b * L + blk * 512
                c = (blk * 512) // CH
                ps, pk = nextps()
                P.op("pe", lambda e, ps=ps, blk=blk, j=j: e.matmul(ps[:], Cpad["Cpad_r"][:, j, :], Xr[:, blk * 512:(blk + 1) * 512], start=True, stop=False),
                     reads=["Cpad_r", XRK[c]], writes=[pk])
                P.op("pe", lambda e, ps=ps, blk=blk, j=j: e.matmul(ps[:], Cpad["Cpad_ni"][:, j, :], Xi[:, blk * 512:(blk + 1) * 512], start=False, stop=True),
                     reads=["Cpad_ni", XIK[c]], writes=[pk])
                P.op("dve", lambda e, ps=ps, t0=t0: e.tensor_tensor(out=yd[:, t0:t0 + 512], in0=yd[:, t0:t0 + 512], in1=ps[:], op=ALU.add),
                     reads=[pk], writes=[ydk(t0)])
    for i in range(NT // 512):
        t0 = i * 512
        a, ak = gt[i % 2], f"gt{i % 2}"
        sg, sk = gs[i % 2], f"gs{i % 2}"
        P.op("act", lambda e, a=a, t0=t0: e.activation(out=a[:], in_=yd[:, t0:t0 + 512], func=AF.Square), reads=[ydk(t0)], writes=[ak])
        P.op("dve", lambda e, a=a: e.tensor_scalar(out=a[:], in0=a[:], scalar1=0.044715, scalar2=1.0, op0=ALU.mult, op1=ALU.add), reads=[], writes=[ak])
        P.op("dve", lambda e, a=a, t0=t0: e.tensor_tensor(out=a[:], in0=a[:], in1=yd[:, t0:t0 + 512], op=ALU.mult), reads=[ydk(t0)], writes=[ak])
        P.op("act", lambda e, a=a, sg=sg: e.activation(out=sg[:], in_=a[:], func=AF.Sigmoid, scale=1.5957691216057308), reads=[ak], writes=[sk])
        P.op("dve", lambda e, sg=sg, t0=t0: e.tensor_tensor(out=sg[:], in0=sg[:], in1=yd[:, t0:t0 + 512], op=ALU.mult), reads=[ydk(t0)], writes=[sk])
        P.op(("sp", "qa")[i % 2], lambda e, sg=sg, t0=t0: e.dma_start(out=gT[:, t0:t0 + 512], in_=sg[:]), reads=[sk])
    return kb.finish()


def attn_consts():
    m = np.arange(128)[:, None]
    n = np.arange(512)[None, :]
    wm = np.zeros((8, 128, 512), np.float32)
    for rel in range(-4, 0):
        wm[rel + 4] = np.where((128 * rel + m) <= n - 512, NEG, 0.0)
    for rel in range(0, 4):
        wm[rel + 4] = np.where((128 * rel + m) > n, NEG, 0.0)
    wmask = np.ascontiguousarray(wm.transpose(1, 0, 2)).astype(NPBF)
    j = np.arange(64)[:, None, None]
    kt = np.arange(32)[None, :, None]
    mm = np.arange(128)[None, None, :]
    ekt = np.where(j == 2 * kt + mm // 64, NEG, 0.0).astype(NPBF)
    r = np.arange(128)[:, None]
    jj = np.arange(503)[None, :]
    cmpw = np.where(16 * (jj - 248) + 31 <= r, 0.0, NEG).astype(np.float32)
    jj = np.arange(126)[None, :]
    rel = (jj - 62) - (r >= 64)
    keepw = np.where((rel > 0) | (rel == 0) | (rel == -1), 0.0, 1.0).astype(np.float32)
    addw = np.where(rel > 0, -1e30, np.where((rel == 0) | (rel == -1), 1e6, 0.0)).astype(np.float32)
    onehot = np.zeros((12, 12, 128), np.float32)
    for g in range(12):
        onehot[g, g, :] = 1.0
    return dict(wmask=wmask, ekt=ekt, cmpw=cmpw, keepw=keepw, addw=addw, onehot=onehot,
                identf=np.eye(128, dtype=np.float32), identb=np.eye(128).astype(NPBF),
                onesb=np.ones((128, 128)).astype(NPBF))


def build_attn(L=4096):
    kb = KB()
    P = kb.P
    nc = kb.nc
    NQB = L // 512
    NKT = L // 128
    NC = (L - 32) // 16 + 1
    NC1 = NC - 128
    NSB = L // 64
    qT = kb.dram_in("qT", [128, 4, L], BF16)
    szT = kb.dram_in("szT", [128, 12, L], BF16)
    gateT = kb.dram_in("gateT", [12, L], F32)
    kcT = kb.dram_in("kcT", [128, L], BF16)
    vcT = kb.dram_in("vcT", [128, L], BF16)
    ksT = kb.dram_in("ksT", [128, L], BF16)
    kwT = kb.dram_in("kwT", [128, L], BF16)
    vs_d = kb.dram_in("vs", [128, NKT, 128], BF16)
    vw_d = kb.dram_in("vw", [128, NKT, 128], BF16)
    posk = kb.dram_in("poskT", [128, 32], F32)
    posv = kb.dram_in("posvT", [128, 32], F32)
    w1k = kb.dram_in("w1k", [128, 32, 128], F32)
    w1v = kb.dram_in("w1v", [128, 32, 128], F32)
    w2k = kb.dram_in("w2k", [128, 128], F32)
    w2v = kb.dram_in("w2v", [128, 128], F32)
    c_wmask = kb.dram_in("wmask", [128, 8, 512], BF16)
    c_ekt = kb.dram_in("ekt", [64, 32, 128], BF16)
    c_cmpw = kb.dram_in("cmpw", [128, 503], F32)
    c_keepw = kb.dram_in("keepw", [128, 126], F32)
    c_addw = kb.dram_in("addw", [128, 126], F32)
    c_onehot = kb.dram_in("onehot", [12, 12, 128], F32)
    c_identf = kb.dram_in("identf", [128, 128], F32)
    c_identb = kb.dram_in("identb", [128, 128], BF16)
    c_onesb = kb.dram_in("onesb", [128, 128], BF16)
    yT = kb.dram_out("yT", [128, 4, L], BF16)
    lq = [0]

    def load(name, src, shape, dt, st=None):
        t = (st or kb.st).enter_context(nc.sbuf_tensor(name, list(shape), dt))
        q = ("sp", "qa")[lq[0] % 2]
        lq[0] += 1
        P.op(q, lambda e: e.dma_start(out=t[:], in_=src), writes=[name])
        return t

    st0 = contextlib.ExitStack()

    def sb0(name, shape, dt):
        return st0.enter_context(nc.sbuf_tensor(name, list(shape), dt))

    ks_s = load("ks_s", ksT[:, :], [128, L], BF16)
    kw_s = load("kw_s", kwT[:, :], [128, L], BF16)
    vs_s = load("vs_s", vs_d[:, :, :], [128, NKT, 128], BF16)
    vw_s = load("vw_s", vw_d[:, :, :], [128, NKT, 128], BF16)
    wmask = load("wmask_s", c_wmask[:, :, :], [128, 8, 512], BF16)
    ekt = load("ekt_s", c_ekt[:, :, :], [64, 32, 128], BF16)
    cmpw = load("cmpw_s", c_cmpw[:, :], [128, 503], F32)
    keepw = load("keepw_s", c_keepw[:, :], [128, 126], F32)
    addw = load("addw_s", c_addw[:, :], [128, 126], F32)
    onehot = load("onehot_s", c_onehot[:, :, :], [12, 12, 128], F32)
    identf = load("identf_s", c_identf[:, :], [128, 128], F32)
    identb = load("identb_s", c_identb[:, :], [128, 128], BF16)
    onesb = load("onesb_s", c_onesb[:, :], [128, 128], BF16)
    kcmpT = kb.sb("kcmpT", [128, NC], BF16)
    vcmp = kb.sb("vcmp", [128, 2, 128], BF16)
    kc_s = load("kc_s", kcT[:, :], [128, L], BF16, st0)
    vc_s = load("vc_s", vcT[:, :], [128, L], BF16, st0)
    posk_s = load("posk_s", posk[:, :], [128, 32], F32, st0)
    posv_s = load("posv_s", posv[:, :], [128, 32], F32, st0)
    w2k_s = load("w2k_s", w2k[:, :], [128, 128], F32, st0)
    w2v_s = load("w2v_s", w2v[:, :], [128, 128], F32, st0)
    w1st = sb0("w1st", [128, 32, 128], F32)
    w1b = sb0("w1b", [128, 32, 128], BF16)
    w2b = sb0("w2b", [128, 128], BF16)
    tmpl = [sb0(f"tmpl{i}", [128, NC], BF16) for i in range(4)]
    h1 = sb0("h1", [128, NC], F32)
    h2 = sb0("h2", [128, NC], F32)
    hg = sb0("hg", [128, NC], BF16)

    pss = [kb.ps(f"ps{i}", [128, 512], F32) for i in range(8)]
    SROT = [0, 1]
    PS_O = [2, 3]
    PS_S = [4, 5]
    PS_C = [6, 7]
    srot = [0]
    crot = [0]

    def ps_s_next():
        i = SROT[srot[0] % 2]
        srot[0] += 1
        return pss[i], f"ps{i}"

    def ps_c_next():
        i = PS_C[crot[0] % 2]
        crot[0] += 1
        return pss[i], f"ps{i}"

    for side, (src_s, pos_s, w1d, w2s) in enumerate([(kc_s, posk_s, w1k, w2k_s), (vc_s, posv_s, w1v, w2v_s)]):
        srck = "kc_s" if side == 0 else "vc_s"
        posn = "posk_s" if side == 0 else "posv_s"
        w2n = "w2k_s" if side == 0 else "w2v_s"
        P.op("sp", lambda e, w1d=w1d: e.dma_start(out=w1st[:], in_=w1d[:, :, :]), writes=["w1st"])
        P.op("pool", lambda e: e.tensor_copy(out=w1b[:], in_=w1st[:]), reads=["w1st"], writes=["w1b"])
        P.op("pool", lambda e, w2s=w2s: e.tensor_copy(out=w2b[:], in_=w2s[:]), reads=[w2n], writes=["w2b"])
        ps, pk = ps_c_next()
        for l in range(32):
            tl, tk = tmpl[l % 4], f"tmpl{l % 4}"
            P.op("dve", lambda e, tl=tl, l=l, src_s=src_s, pos_s=pos_s: e.tensor_scalar(
                out=tl[:], in0=src_s[:, l:l + 16 * (NC - 1) + 1:16], scalar1=pos_s[:, l:l + 1], scalar2=None, op0=ALU.add),
                reads=[srck, posn], writes=[tk])
            P.op("pe", lambda e, ps=ps, tl=tl, l=l: e.matmul(ps[:, 0:NC], w1b[:, l, :], tl[:], start=(l == 0), stop=(l == 31)),
                 reads=["w1b", tk], writes=[pk])
        P.op("act", lambda e, ps=ps: e.copy(out=h1[:], in_=ps[:, 0:NC]), reads=[pk], writes=["h1"])
        P.op("act", lambda e: e.activation(out=h2[:], in_=h1[:], func=AF.Square), reads=["h1"], writes=["h2"])
        P.op("dve", lambda e: e.tensor_scalar(out=h2[:], in0=h2[:], scalar1=0.044715, scalar2=1.0, op0=ALU.mult, op1=ALU.add), writes=["h2"])
        P.op("dve", lambda e: e.tensor_tensor(out=h2[:], in0=h2[:], in1=h1[:], op=ALU.mult), reads=["h1"], writes=["h2"])
        P.op("act", lambda e: e.activation(out=h2[:], in_=h2[:], func=AF.Sigmoid, scale=1.5957691216057308), writes=["h2"])
        P.op("dve", lambda e: e.tensor_tensor(out=hg[:], in0=h2[:], in1=h1[:], op=ALU.mult), reads=["h1", "h2"], writes=["hg"])
        if side == 0:
            ps, pk = ps_c_next()
            P.op("pe", lambda e, ps=ps: e.matmul(ps[:, 0:NC], w2b[:], hg[:], start=True, stop=True), reads=["w2b", "hg"], writes=[pk])
            P.op("act", lambda e, ps=ps: e.copy(out=kcmpT[:], in_=ps[:, 0:NC]), reads=[pk], writes=["kcmpT"])
        else:
            for t2, (n0, nn) in enumerate([(0, 128), (128, NC1)]):
                ps, pk = ps_c_next()
                P.op("pe", lambda e, ps=ps, n0=n0, nn=nn: e.matmul(ps[0:nn, 0:128], hg[:, n0:n0 + nn], w2b[:], start=True, stop=True),
                     reads=["w2b", "hg"], writes=[pk])
                P.op("act", lambda e, ps=ps, nn=nn, t2=t2: e.copy(out=vcmp[0:nn, t2, :], in_=ps[0:nn, 0:128]), reads=[pk], writes=["vcmp"])
    P.barrier()
    st0.close()

    qb_s = [kb.sb(f"qb{i}", [128, 4, 512], BF16) for i in range(2)]
    sz_s = [kb.sb(f"sz{i}", [128, 12, 512], BF16) for i in range(2)]
    gt_s = [kb.sb(f"gt{i}", [12, 512], F32) for i in range(2)]
    gbc = [kb.sb(f"gbc{i}", [128, 12, 512], F32) for i in range(2)]
    sc = [kb.sb(f"sc{i}", [128, NC], F32) for i in range(2)]
    pe_ = [kb.sb(f"pe_{i}", [128, NC], F32) for i in range(2)]
    pn = [kb.sb(f"pn{i}", [128, NC], F32) for i in range(2)]
    ppad = kb.sb("ppad", [128, 260], F32)
    rs = [kb.sb(f"rs{i}", [128, 1], F32) for i in range(2)]
    pnT = [kb.sb(f"pnT{i}", [128, 2, 128], BF16) for i in range(2)]
    ocmp = [kb.sb(f"ocmp{i}", [128, 4, 512], F32) for i in range(2)]
    imp = kb.sb("imp", [128, NSB], F32)
    imp2 = kb.sb("imp2", [128, NSB], F32)
    v8a = kb.sb("v8a", [128, 8], F32)
    v8b = kb.sb("v8b", [128, 8], F32)
    nsel = kb.sb("nsel", [128, NSB], F32)
    nsT = [kb.sb(f"nsT{i}", [64, 512], BF16) for i in range(2)]
    pT = [kb.sb(f"pT{i}", [128, 512], BF16) for i in range(3)]
    osb = kb.sb("osb", [128, 512], F32)
    rden = kb.sb("rden", [128, 512], F32)
    wgt = kb.sb("wgt", [128, 512], F32)
    tt1 = kb.sb("tt1", [128, 512], F32)
    acc = kb.sb("acc", [128, 4, 512], F32)
    yb = [kb.sb(f"ybo{i}", [128, 4, 512], BF16) for i in range(2)]
    P.op("pool", lambda e: e.memset(ppad[:], 0.0), writes=["ppad"])
    pti = [0]

    def emit_cmp(qb):
        par = qb % 2
        q0 = qb * 512
        qs, qk = qb_s[par], f"qb{par}"
        szs, szk = sz_s[par], f"sz{par}"
        gts, gk = gt_s[par], f"gt{par}"
        P.op("sp", lambda e: e.dma_start(out=qs[:], in_=qT[:, :, q0:q0 + 512]), writes=[qk])
        P.op("qa", lambda e: e.dma_start(out=szs[:], in_=szT[:, :, q0:q0 + 512]), writes=[szk])
        P.op("sp", lambda e: e.dma_start(out=gts[:], in_=gateT[:, q0:q0 + 512]), writes=[gk])
        for gi in range(12):
            ps, pk = ps_c_next()
            P.op("pe", lambda e, ps=ps, gi=gi: e.matmul(ps[:], onehot[:, gi, :], gts[:], start=True, stop=True), reads=["onehot_s", gk], writes=[pk])
            P.op("act", lambda e, ps=ps, gi=gi: e.copy(out=gbc[par][:, gi, :], in_=ps[:]), reads=[pk], writes=[f"gbc{par}"])
        for qt in range(4):
            it = qb * 4 + qt
            c0 = 248 - 8 * it
            for pair in ((0, 1), (2, 3)):
                bank = {h: (pss[PS_C[h % 2]], f"ps{PS_C[h % 2]}") for h in pair}
                for h in pair:
                    ps, pk = bank[h]
                    P.op("pe", lambda e, ps=ps, h=h, qt=qt: e.matmul(ps[:, 0:NC], qs[:, h, qt * 128:(qt + 1) * 128], kcmpT[:], start=True, stop=True),
                         reads=[qk, "kcmpT"], writes=[pk])
                for h in pair:
                    ps, pk = bank[h]
                    i2 = h % 2
                    P.op("dve", lambda e, ps=ps, c0=c0, i2=i2: e.tensor_tensor(out=sc[i2][:], in0=ps[:, 0:NC], in1=cmpw[:, c0:c0 + NC], op=ALU.add),
                         reads=[pk, "cmpw_s"], writes=[f"sc{i2}"])
                for h in pair:
                    i2 = h % 2
                    P.op("act", lambda e, i2=i2: e.activation(out=pe_[i2][:], in_=sc[i2][:], func=AF.Exp, accum_out=rs[i2][:]), reads=[f"sc{i2}"], writes=[f"pe_{i2}", f"rs{i2}"])
                for h in pair:
                    i2 = h % 2
                    P.op("dve", lambda e, i2=i2: e.tensor_scalar(out=rs[i2][:], in0=rs[i2][:], scalar1=1e-30, scalar2=None, op0=ALU.max), writes=[f"rs{i2}"])
                for h in pair:
                    i2 = h % 2
                    P.op("dve", lambda e, i2=i2: e.reciprocal(out=rs[i2][:], in_=rs[i2][:]), writes=[f"rs{i2}"])
                for h in pair:
                    i2 = h % 2
                    P.op("dve", lambda e, i2=i2: e.tensor_scalar(out=pn[i2][:], in0=pe_[i2][:], scalar1=rs[i2][:], scalar2=None, op0=ALU.mult),
                         reads=[f"pe_{i2}", f"rs{i2}"], writes=[f"pn{i2}"])
                for h in pair:
                    i2 = h % 2
                    if h == 0:
                        P.op("pool", lambda e, i2=i2: e.tensor_copy(out=ppad[:, 1:1 + NC], in_=pn[i2][:]), reads=[f"pn{i2}"], writes=["ppad"])
                    else:
                        P.op("pool", lambda e, i2=i2: e.tensor_tensor(out=ppad[:, 1:1 + NC], in0=ppad[:, 1:1 + NC], in1=pn[i2][:], op=ALU.add),
                             reads=[f"pn{i2}"], writes=["ppad"])
                for h in pair:
                    ps, pk = bank[h]
                    i2 = h % 2
                    P.op("pe", lambda e, ps=ps, i2=i2: e.transpose(ps[:, 0:128], pn[i2][:, 0:128], identf[:]), reads=[f"pn{i2}", "identf_s"], writes=[pk])
                    P.op("pe", lambda e, ps=ps, i2=i2: e.transpose(ps[0:NC1, 128:256], pn[i2][:, 128:NC], identf[:]), reads=[f"pn{i2}", "identf_s"], writes=[pk])
                for h in pair:
                    ps, pk = bank[h]
                    i2 = h % 2
                    P.op("act", lambda e, ps=ps, i2=i2: e.copy(out=pnT[i2][:, 0, :], in_=ps[:, 0:128]), reads=[pk], writes=[f"pnT{i2}"])
                    P.op("act", lambda e, ps=ps, i2=i2: e.copy(out=pnT[i2][0:NC1, 1, :], in_=ps[0:NC1, 128:256]), reads=[pk], writes=[f"pnT{i2}"])
                for h in pair:
                    ps, pk = bank[h]
                    i2 = h % 2
                    P.op("pe", lambda e, ps=ps, i2=i2: e.matmul(ps[:, 256:384], vcmp[:, 0, :], pnT[i2][:, 0, :], start=True, stop=False),
                         reads=["vcmp", f"pnT{i2}"], writes=[pk])
                    P.op("pe", lambda e, ps=ps, i2=i2: e.matmul(ps[:, 256:384], vcmp[0:NC1, 1, :], pnT[i2][0:NC1, 1, :], start=False, stop=True),
                         reads=["vcmp", f"pnT{i2}"], writes=[pk])
                for h in pair:
                    ps, pk = bank[h]
                    P.op("act", lambda e, ps=ps, h=h, qt=qt: e.copy(out=ocmp[par][:, h, qt * 128:(qt + 1) * 128], in_=ps[:, 256:384]),
                         reads=[pk], writes=[f"ocmp{par}"])
            P.op("dve", lambda e: e.tensor_tensor(out=imp[:], in0=ppad[:, 0:256:4], in1=ppad[:, 1:257:4], op=ALU.add), reads=["ppad"], writes=["imp"])
            for m_ in (2, 3, 4):
                P.op("dve", lambda e, m_=m_: e.tensor_tensor(out=imp[:], in0=imp[:], in1=ppad[:, m_:m_ + 256:4], op=ALU.add), reads=["ppad"], writes=["imp"])
            w0 = 62 - 2 * it
            P.op("dve", lambda e, w0=w0: e.tensor_tensor(out=imp2[:], in0=imp[:], in1=keepw[:, w0:w0 + NSB], op=ALU.mult), reads=["imp", "keepw_s"], writes=["imp2"])
            P.op("dve", lambda e, w0=w0: e.tensor_tensor(out=imp2[:], in0=imp2[:], in1=addw[:, w0:w0 + NSB], op=ALU.add), reads=["addw_s"], writes=["imp2"])
            P.op("dve", lambda e: e.memset(imp2[:, 0:1], 1e6), writes=["imp2"])
            P.op("dve", lambda e: e.max(out=v8a[:], in_=imp2[:]), reads=["imp2"], writes=["v8a"])
            P.op("dve", lambda e: e.match_replace(out=imp[:], in_to_replace=v8a[:], in_values=imp2[:], imm_value=-3.0e38), reads=["imp2", "v8a"], writes=["imp"])
            P.op("dve", lambda e: e.max(out=v8b[:], in_=imp[:]), reads=["imp"], writes=["v8b"])
            P.op("dve", lambda e: e.tensor_scalar(out=nsel[:], in0=imp2[:], scalar1=v8b[:, 7:8], scalar2=None, op0=ALU.is_lt), reads=["imp2", "v8b"], writes=["nsel"])
            ps4, pk4 = ps_c_next()
            P.op("pe", lambda e, ps4=ps4: e.transpose(ps4[0:NSB, 0:128], nsel[:], identf[:]), reads=["nsel", "identf_s"], writes=[pk4])
            P.op("act", lambda e, ps4=ps4, qt=qt: e.copy(out=nsT[par][:, qt * 128:(qt + 1) * 128], in_=ps4[0:NSB, 0:128]), reads=[pk4], writes=[f"nsT{par}"])

    def emit_att(qb):
        par = qb % 2
        q0 = qb * 512
        qs, qk = qb_s[par], f"qb{par}"
        szs, szk = sz_s[par], f"sz{par}"
        nst, nsk = nsT[par], f"nsT{par}"
        jobs = []
        grp = 0
        for h in range(4):
            for br in (1, 2):
                kts = list(range(0, 4 * (qb + 1))) if br == 1 else list(range(max(0, 4 * qb - 4), 4 * qb + 4))
                if br == 2 and qb >= 1:
                    kts = [4 * qb - 1] + [k for k in kts if k != 4 * qb - 1]
                for ki, kt in enumerate(kts):
                    jobs.append((h, br, kt, ki, len(kts), grp))
                grp += 1
        state = {}

        def emit_S(job):
            h, br, kt, ki, nk, g = job
            ps, pk = ps_s_next()
            rel = kt - 4 * qb
            c0 = max(0, 128 * rel) if rel >= 0 else 0
            c1 = min(512, 128 * rel + 640) if br == 2 else 512
            if br == 1:
                diag = rel >= 0
                P.op("pe", lambda e: e.matmul(ps[:, c0:c1], ks_s[:, kt * 128:(kt + 1) * 128], qs[:, h, c0:c1], start=True, stop=False), reads=["ks_s", qk], writes=[pk])
                P.op("pe", lambda e: e.matmul(ps[:, c0:c1], ekt[:, kt, :], nst[:, c0:c1], start=False, stop=(not diag)), reads=["ekt_s", nsk], writes=[pk])
                if diag:
                    P.op("pe", lambda e: e.matmul(ps[:, c0:c1], identb[:], wmask[:, rel + 4, c0:c1], start=False, stop=True), reads=["identb_s", "wmask_s"], writes=[pk])
            else:
                P.op("pe", lambda e: e.matmul(ps[:, c0:c1], kw_s[:, kt * 128:(kt + 1) * 128], qs[:, h, c0:c1], start=True, stop=False), reads=["kw_s", qk], writes=[pk])
                P.op("pe", lambda e: e.matmul(ps[:, c0:c1], identb[:], wmask[:, rel + 4, c0:c1], start=False, stop=True), reads=["identb_s", "wmask_s"], writes=[pk])
            state[job] = (ps, pk, c0, c1)

        def emit_PV(job):
            h, br, kt, ki, nk, g = job
            ps, pk, c0, c1 = state.pop(job)
            pt_, ptk = pT[pti[0] % 3], f"pT{pti[0] % 3}"
            pti[0] += 1
            v_s, vname = (vs_s, "vs_s") if br == 1 else (vw_s, "vw_s")
            pso, pko = pss[PS_O[g % 2]], f"ps{PS_O[g % 2]}"
            pss_, pks = pss[PS_S[g % 2]], f"ps{PS_S[g % 2]}"
            P.op("act", lambda e: e.activation(out=pt_[:, c0:c1], in_=ps[:, c0:c1], func=AF.Exp), reads=[pk], writes=[ptk])
            P.op("pe", lambda e: e.matmul(pso[:, c0:c1], v_s[:, kt, :], pt_[:, c0:c1], start=(ki == 0), stop=(ki == nk - 1)), reads=[vname, ptk], writes=[pko])
            P.op("pe", lambda e: e.matmul(pss_[:, c0:c1], onesb[:], pt_[:, c0:c1], start=(ki == 0), stop=(ki == nk - 1)), reads=["onesb_s", ptk], writes=[pks])
            if ki == nk - 1:
                gi = br * 4 + h
                P.op("act", lambda e: e.copy(out=osb[:], in_=pso[:]), reads=[pko], writes=["osb"])
                P.op("dve", lambda e: e.reciprocal(out=rden[:], in_=pss_[:]), reads=[pks], writes=["rden"])
                P.op("dve", lambda e: e.tensor_tensor(out=wgt[:], in0=rden[:], in1=gbc[par][:, gi, :], op=ALU.mult), reads=["rden", f"gbc{par}"], writes=["wgt"])
                P.op("dve", lambda e: e.tensor_tensor(out=tt1[:], in0=osb[:], in1=wgt[:], op=ALU.mult), reads=["osb", "wgt"], writes=["tt1"])
                if br == 1:
                    P.op("dve", lambda e: e.tensor_tensor(out=acc[:, h, :], in0=tt1[:], in1=szs[:, gi, :], op=ALU.mult), reads=["tt1", szk], writes=["acc"])
                else:
                    P.op("dve", lambda e: e.tensor_tensor(out=tt1[:], in0=tt1[:], in1=szs[:, gi, :], op=ALU.mult), reads=[szk], writes=["tt1"])
                    P.op("pool", lambda e: e.tensor_tensor(out=acc[:, h, :], in0=acc[:, h, :], in1=tt1[:], op=ALU.add), reads=["tt1"], writes=["acc"])
                    ybo, ybk = yb[par], f"ybo{par}"
                    P.op("dve", lambda e: e.tensor_tensor(out=tt1[:], in0=ocmp[par][:, h, :], in1=gbc[par][:, h, :], op=ALU.mult), reads=[f"ocmp{par}", f"gbc{par}"], writes=["tt1"])
                    P.op("dve", lambda e: e.tensor_tensor(out=tt1[:], in0=tt1[:], in1=szs[:, h, :], op=ALU.mult), reads=[szk], writes=["tt1"])
                    P.op("pool", lambda e: e.tensor_tensor(out=ybo[:, h, :], in0=acc[:, h, :], in1=tt1[:], op=ALU.add), reads=["tt1", "acc"], writes=[ybk])

        emit_S(jobs[0])
        for i, job in enumerate(jobs):
            if i + 1 < len(jobs):
                emit_S(jobs[i + 1])
            emit_PV(job)
        ybo, ybk = yb[par], f"ybo{par}"
        P.op("sp", lambda e: e.dma_start(out=yT[:, :, q0:q0 + 512], in_=ybo[:]), reads=[ybk])

    emit_cmp(0)
    for qb in range(NQB):
        P.capture()
        emit_att(qb)
        sa = P.end_capture()
        sb_ = []
        if qb + 1 < NQB:
            P.capture()
            emit_cmp(qb + 1)
            sb_ = P.end_capture()
        P.replay(sa, sb_)
    return kb.finish()


def _c(a):
    return np.ascontiguousarray(a)


def scan_maps(inp, layer, uT):
    maps = []
    ident = np.eye(128, dtype=np.float32)
    for c in range(NCORES):
        g0 = 8 * c

        def pl(a):
            return _c(a.reshape(4, 2, 64).transpose(1, 2, 0).reshape(128, 4))

        lr = pl(inp["a_lam_re"][layer][g0:g0 + 8])
        li = pl(inp["a_lam_im"][layer][g0:g0 + 8])
        ldt = pl(np.repeat(inp["a_log_dt"][layer][g0:g0 + 8][:, None], 64, axis=1))
        br = _c(inp["a_b_re"][layer][g0:g0 + 8].reshape(4, 2, 64, 16).transpose(1, 2, 0, 3).reshape(128, 4, 16))
        bi = _c(inp["a_b_im"][layer][g0:g0 + 8].reshape(4, 2, 64, 16).transpose(1, 2, 0, 3).reshape(128, 4, 16))
        crT = _c(inp["a_c_re"][layer][g0:g0 + 8].reshape(4, 2, 16, 64).transpose(1, 3, 0, 2).reshape(128, 4, 16))
        ciT = _c(inp["a_c_im"][layer][g0:g0 + 8].reshape(4, 2, 16, 64).transpose(1, 3, 0, 2).reshape(128, 4, 16))
        dv = _c(inp["a_d"][layer][128 * c:128 * c + 128].reshape(128, 1))
        maps.append(dict(uT=_c(uT[128 * c:128 * c + 128]), lr=lr, li=li, ldt=ldt, br=br, bi=bi, crT=crT, ciT=ciT,
                         dvec=dv, ident=ident))
    return maps


def attn_maps(o16, o32, kvT, inp, consts):
    maps = []
    for c in range(NCORES):
        b, G = c // 4, c % 4
        ts = slice(b * SEQ, (b + 1) * SEQ)
        m = dict(consts)
        m["qT"] = _c(o16[512 * G:512 * G + 512, ts].reshape(4, 128, SEQ).transpose(1, 0, 2))
        szr = np.stack([o16[2048 + br * 2048 + 512 * G:2048 + br * 2048 + 512 * G + 512, ts].reshape(4, 128, SEQ)
                        for br in range(3)], 0)
        m["szT"] = _c(szr.transpose(2, 0, 1, 3).reshape(128, 12, SEQ))
        m["gateT"] = _c(np.stack([o32[br * 16 + 4 * G:br * 16 + 4 * G + 4, ts] for br in range(3)], 0).reshape(12, SEQ))

        def slot(s_):
            return kvT[s_ * 512 + G * 128:s_ * 512 + G * 128 + 128, ts]

        m["kcT"] = _c(slot(0))
        m["vcT"] = _c(slot(1))
        m["ksT"] = _c(slot(2))
        m["kwT"] = _c(slot(4))
        m["vs"] = _c(slot(3).T.reshape(SEQ // 128, 128, 128).transpose(1, 0, 2))
        m["vw"] = _c(slot(5).T.reshape(SEQ // 128, 128, 128).transpose(1, 0, 2))
        m["poskT"] = _c(inp["cmp_pos_k"].T)
        m["posvT"] = _c(inp["cmp_pos_v"].T)
        m["w1k"] = _c(inp["cmp_w1_k"].reshape(32, 128, 128).transpose(1, 0, 2))
        m["w1v"] = _c(inp["cmp_w1_v"].reshape(32, 128, 128).transpose(1, 0, 2))
        m["w2k"] = _c(inp["cmp_w2_k"])
        m["w2v"] = _c(inp["cmp_w2_v"])
        maps.append(m)
    return maps


def kernel(**inputs):
    inp = {k: np.asarray(v) for k, v in inputs.items()}
    T = BATCH * SEQ // NCORES
    x = _c(inp["x"].reshape(BATCH * SEQ, D_MODEL).astype(np.float32))

    def tok(a, c):
        return a[c * T:(c + 1) * T]

    def bc(v):
        return _c(np.broadcast_to(v, (128, D_MODEL)))

    tiles_a = [(128, "id", "o32")] * 8 + [(128, "silu", "o32")] * 8
    tiles_kv = [(128, "id", "o16")] * 24
    tiles_b = [(128, "scale", "o16")] * 16 + [(128, "silu", "o16")] * 48 + [(48, "sigmoid", "o32")]
    nc_proj_a = build_proj(16, T, tiles_a)
    nc_scan = build_scan()
    nc_tail_a = build_tail(8, T, True)
    for layer in range(2):
        wa = proj_host_w(inp["a_w_in"][layer], tiles_a)
        res = run(nc_proj_a, [dict(xT=host_fm(tok(x, c).T), w=wa) for c in range(NCORES)])
        o = np.concatenate([res[c]["o32"] for c in range(NCORES)], axis=1)
        uT, szT = o[:1024], o[1024:]
        res = run(nc_scan, scan_maps(inp, layer, uT))
        gT = np.concatenate([res[c]["gT"] for c in range(NCORES)], axis=0)
        bglu = _c(inp["a_b_glu"][layer].reshape(8, 128).T)
        res = run(nc_tail_a, [dict(x=_c(tok(x, c)), w=host_fm(inp["a_w_out"][layer]), lng=bc(inp["ln_g"][layer]), lnb=bc(inp["ln_b"][layer]),
                                   gT=host_fm(gT[:, c * T:(c + 1) * T]), szT=host_fm(szT[:, c * T:(c + 1) * T]),
                                   wglu=host_fm(inp["a_w_glu"][layer]), bglu=bglu) for c in range(NCORES)])
        x = np.concatenate([res[c]["xo"] for c in range(NCORES)], axis=0)
    tiles_b_kv = tiles_b + tiles_kv
    nc_proj_b_kv = build_proj(16, T, tiles_b_kv)
    nc_proj_b = build_proj(16, T, tiles_b)
    nc_attn = build_attn()
    nc_tail_b = build_tail(16, T, False, BF16)
    consts = attn_consts()
    for li in range(2):
        layer = 2 + li
        if li == 0:
            wbi = proj_host_w(np.concatenate([inp["b_w_in"][li], inp["kv_w"]], axis=1), tiles_b_kv)
            res = run(nc_proj_b_kv, [dict(xT=host_fm(tok(x, c).T), w=wbi) for c in range(NCORES)])
            o16 = np.concatenate([res[c]["o16"] for c in range(NCORES)], axis=1)
            kvT = o16[8192:]
            o16 = o16[:8192]
        else:
            wbi = proj_host_w(inp["b_w_in"][li], tiles_b)
            res = run(nc_proj_b, [dict(xT=host_fm(tok(x, c).T), w=wbi) for c in range(NCORES)])
            o16 = np.concatenate([res[c]["o16"] for c in range(NCORES)], axis=1)
        o32 = np.concatenate([res[c]["o32"] for c in range(NCORES)], axis=1)
        res = run(nc_attn, attn_maps(o16, o32, kvT, inp, consts))
        yT = np.concatenate([np.concatenate([res[b * 4 + G]["yT"].transpose(1, 0, 2).reshape(512, SEQ) for G in range(4)], axis=0)
                             for b in range(BATCH)], axis=1)
        res = run(nc_tail_b, [dict(x=_c(tok(x, c)), w=host_fm(inp["b_w_out"][li]), lng=bc(inp["ln_g"][layer]), lnb=bc(inp["ln_b"][layer]),
                                   yT=host_fm(yT[:, c * T:(c + 1) * T])) for c in range(NCORES)])
        x = np.concatenate([res[c]["xo"] for c in range(NCORES)], axis=0)
    return x.reshape(BATCH, SEQ, D_MODEL).astype(np.float32)
```

```python
import contextlib
import numpy as np
import ml_dtypes
import concourse.bass as bass
import concourse.mybir as mybir
from concourse.bass_utils import run_bass_kernel_spmd

F32 = mybir.dt.float32
BF16 = mybir.dt.bfloat16
AF = mybir.ActivationFunctionType
ALU = mybir.AluOpType
NPBF = ml_dtypes.bfloat16

D_MODEL = 2048
BATCH = 2
SEQ = 4096
NCORES = 8
LN_EPS = 1e-5
ALPHA = 8 ** 0.25
SCALE = 128 ** -0.5
NEG = -30000.0


class Prog:
    COMPUTE = ["pe", "act", "dve", "pool"]
    QUEUES = {"sp": "sp", "qa": "act", "qp": "pool"}
    STREAMS = ["pe", "act", "dve", "pool", "sp"]
    NSLOT = 6

    def __init__(self, nc, same_sync=True):
        self.nc = nc
        self.owners = list(self.COMPUTE) + [f"{q}{j}" for q in self.QUEUES for j in range(self.NSLOT)]
        self.stream = {e: e for e in self.COMPUTE}
        for q, st in self.QUEUES.items():
            for j in range(self.NSLOT):
                self.stream[f"{q}{j}"] = st
        self.inc = {e: (1 if e in self.COMPUTE else 16) for e in self.owners}
        self.q = {e: [] for e in self.STREAMS}
        self.cnt = {e: 0 for e in self.owners}
        self.seen = {st: {o: 0 for o in self.owners} for st in self.STREAMS}
        self.qn = {q: 0 for q in self.QUEUES}
        self.lastw = {}
        self.reads = {}
        self.same_sync = same_sync
        self.seq = 0
        self.cap = None

    def _need(self, eng, other, idx):
        st = self.stream[eng]
        if other == eng and eng == "pe":
            return
        if other == eng and not self.same_sync and eng in self.COMPUTE:
            return
        if self.seen[st][other] >= idx:
            return
        self.seen[st][other] = idx
        self.seq += 1
        self.q[st].append(("wait", other, idx, self.seq))

    def barrier(self):
        rep = {"pe": "pe", "act": "act", "dve": "dve", "pool": "pool", "sp": "sp0"}
        for st in self.STREAMS:
            for o in self.owners:
                if self.cnt[o] > 0:
                    self._need(rep[st], o, self.cnt[o])

    def capture(self):
        self.cap = []
        return self.cap

    def end_capture(self):
        c, self.cap = self.cap, None
        return c

    def replay(self, *streams):
        pos = [0] * len(streams)
        tot = [max(1, len(x)) for x in streams]
        n = sum(len(x) for x in streams)
        for _ in range(n):
            best, bi = None, None
            for i, x in enumerate(streams):
                if pos[i] < len(x):
                    f = pos[i] / tot[i]
                    if best is None or f < best:
                        best, bi = f, i
            a = streams[bi][pos[bi]]
            pos[bi] += 1
            self.op(*a)

    def op(self, eng, fn, reads=(), writes=()):
        if self.cap is not None:
            self.cap.append((eng, fn, tuple(reads), tuple(writes)))
            return None
        if eng in self.QUEUES:
            j = self.qn[eng] % self.NSLOT
            self.qn[eng] += 1
            eng = f"{eng}{j}"
            if self.cnt[eng] > 0:
                self._need(eng, eng, self.cnt[eng])
        for k in reads:
            w = self.lastw.get(k)
            if w is not None:
                self._need(eng, *w)
        for k in writes:
            w = self.lastw.get(k)
            if w is not None:
                self._need(eng, *w)
            for (o, i) in self.reads.get(k, {}).items():
                self._need(eng, o, i)
        self.cnt[eng] += 1
        idx = self.cnt[eng]
        self.seq += 1
        self.q[self.stream[eng]].append(("op", fn, idx, self.seq, eng))
        for k in reads:
            self.reads.setdefault(k, {})[eng] = idx
        for k in writes:
            self.lastw[k] = (eng, idx)
            self.reads[k] = {}
        return idx

    def emit(self):
        nc = self.nc
        with contextlib.ExitStack() as st:
            sems = {e: st.enter_context(nc.semaphore("sem_" + e)) for e in self.owners}
            block = st.enter_context(nc.Block())
            engobj = {"pe": block.tensor, "act": block.scalar, "dve": block.vector,
                      "pool": block.gpsimd, "sp": block.sync}
            for e in self.STREAMS:
                items = self.q[e]

                def body(engine, items=items, e=e):
                    for it in items:
                        if it[0] == "wait":
                            _, o, idx, _s = it
                            engine.wait_ge(sems[o], idx * self.inc[o])
                        else:
                            _, fn, idx, _s, owner = it
                            ins = fn(engine)
                            ins.then_inc(sems[owner], self.inc[owner])
                    if e == "sp":
                        for o in self.owners:
                            if self.cnt[o] > 0:
                                engine.wait_ge(sems[o], self.cnt[o] * self.inc[o])
                engobj[e](body)


class KB:
    def __init__(self, name="k"):
        self.nc = bass.Bass("TRN2", target_bir_lowering=False)
        self.st = contextlib.ExitStack()
        self.P = Prog(self.nc)
        self.nps = 0
        self.uid = 0

    def dram_in(self, name, shape, dt=F32):
        return self.nc.dram_tensor(name, list(shape), dt, kind="ExternalInput").ap()

    def dram_out(self, name, shape, dt=F32):
        return self.nc.dram_tensor(name, list(shape), dt, kind="ExternalOutput").ap()

    def sb(self, name, shape, dt=F32):
        return self.st.enter_context(self.nc.sbuf_tensor(name, list(shape), dt))

    def ps(self, name, shape, dt=F32):
        return self.st.enter_context(self.nc.psum_tensor(name, list(shape), dt))

    def finish(self):
        self.P.emit()
        self.st.close()
        return self.nc


def run(nc, in_maps):
    res = run_bass_kernel_spmd(nc, in_maps, core_ids=list(range(len(in_maps))))
    return res.results


def act_apply(e, out, in_, act):
    if act == "id":
        return e.copy(out=out, in_=in_)
    if act == "scale":
        return e.mul(out=out, in_=in_, mul=SCALE)
    if act == "silu":
        return e.activation(out=out, in_=in_, func=AF.Silu)
    if act == "sigmoid":
        return e.activation(out=out, in_=in_, func=AF.Sigmoid)
    raise ValueError(act)


def proj_blocks(tiles, WB=512):
    blocks, cur, curw = [], [], 0
    for t in tiles:
        if curw + t[0] > WB:
            blocks.append(cur)
            cur, curw = [], 0
        cur.append(t)
        curw += t[0]
    if cur:
        blocks.append(cur)
    return blocks


def proj_host_w(w, tiles, WB=512):
    K_, N = w.shape
    KC = K_ // 128
    blocks = proj_blocks(tiles, WB)
    out = np.zeros((len(blocks), 128, KC, WB), np.float32)
    c0 = 0
    for bi, tl in enumerate(blocks):
        bw = sum(t[0] for t in tl)
        out[bi, :, :, :bw] = w[:, c0:c0 + bw].reshape(KC, 128, bw).transpose(1, 0, 2)
        c0 += bw
    return out


def host_fm(aT):
    K_, T = aT.shape
    return np.ascontiguousarray(aT.reshape(K_ // 128, 128, T).transpose(1, 0, 2))


def build_proj(KC, T, tiles):
    kb = KB()
    P = kb.P
    WB = 512
    blocks = proj_blocks(tiles, WB)
    n32 = sum(t[0] for t in tiles if t[2] == "o32")
    n16 = sum(t[0] for t in tiles if t[2] == "o16")
    xT = kb.dram_in("xT", [128, KC, T])
    w = kb.dram_in("w", [len(blocks), 128, KC, WB])
    outs = {}
    if n32:
        outs["o32"] = kb.dram_out("o32", [n32, T], F32)
    if n16:
        outs["o16"] = kb.dram_out("o16", [n16, T], BF16)
    xb = kb.sb("xb", [128, KC, T], BF16)
    KG = 4
    LQ = ["sp", "qa", "qp"]
    xst = [kb.sb(f"xst{i}", [128, KG, T], F32) for i in range(2)]
    wst = [kb.sb(f"wst{i}", [128, KC, WB], F32) for i in range(2)]
    wb = [kb.sb(f"wb{i}", [128, KC, WB], BF16) for i in range(2)]
    o32s = [kb.sb(f"o32s{i}", [128, 512], F32) for i in range(2)]
    o16s = [kb.sb(f"o16s{i}", [128, 512], BF16) for i in range(2)]
    pss = [kb.ps(f"ps{i}", [128, 512], F32) for i in range(8)]
    H = KC // 2

    def load_w(bi):
        bw = sum(t[0] for t in blocks[bi])
        ws, wbb = wst[bi % 2], wb[bi % 2]
        P.op(LQ[(2 * bi) % 3], lambda e: e.dma_start(out=ws[:, 0:H, 0:bw], in_=w[bi, :, 0:H, 0:bw]), writes=[f"wst{bi % 2}a"])
        P.op(LQ[(2 * bi + 1) % 3], lambda e: e.dma_start(out=ws[:, H:KC, 0:bw], in_=w[bi, :, H:KC, 0:bw]), writes=[f"wst{bi % 2}b"])
        P.op("dve", lambda e: e.tensor_copy(out=wbb[:, 0:H, 0:bw], in_=ws[:, 0:H, 0:bw]), reads=[f"wst{bi % 2}a"], writes=[f"wb{bi % 2}a"])
        P.op("act", lambda e: e.copy(out=wbb[:, H:KC, 0:bw], in_=ws[:, H:KC, 0:bw]), reads=[f"wst{bi % 2}b"], writes=[f"wb{bi % 2}b"])

    for i in range(KC // KG):
        s_ = xst[i % 2]
        P.op(LQ[i % 3], lambda e, s_=s_, i=i: e.dma_start(out=s_[:], in_=xT[:, i * KG:(i + 1) * KG, :]), writes=[f"xst{i % 2}"])
        if i % 2 == 0:
            P.op("dve", lambda e, s_=s_, i=i: e.tensor_copy(out=xb[:, i * KG:(i + 1) * KG, :], in_=s_[:]),
                 reads=[f"xst{i % 2}"], writes=[f"xb{i}"])
        else:
            P.op("act", lambda e, s_=s_, i=i: e.copy(out=xb[:, i * KG:(i + 1) * KG, :], in_=s_[:]),
                 reads=[f"xst{i % 2}"], writes=[f"xb{i}"])
    load_w(0)
    rowpos = {"o32": 0, "o16": 0}
    psi = 0
    oi = {"o32": 0, "o16": 0}
    for bi, tl in enumerate(blocks):
        if bi + 1 < len(blocks):
            load_w(bi + 1)
        wbb = wb[bi % 2]
        off = 0
        for (nc_, act, oname) in tl:
            for n in range(T // 512):
                ps = pss[psi % 8]
                pk = f"ps{psi % 8}"
                psi += 1
                for k in range(KC):
                    P.op("pe", lambda e, ps=ps, wbb=wbb, off=off, nc_=nc_, k=k, n=n: e.matmul(
                        ps[0:nc_, :], wbb[:, k, off:off + nc_], xb[:, k, n * 512:(n + 1) * 512],
                        start=(k == 0), stop=(k == KC - 1)),
                        reads=[f"wb{bi % 2}" + ("a" if k < H else "b"), f"xb{k // KG}"], writes=[pk])
                j = oi[oname] % 2
                oi[oname] += 1
                osb = (o32s if oname == "o32" else o16s)[j]
                ok = f"{oname}s{j}"
                P.op("act", lambda e, osb=osb, ps=ps, nc_=nc_, act=act: act_apply(e, osb[0:nc_, :], ps[0:nc_, :], act),
                     reads=[pk], writes=[ok])
                r0 = rowpos[oname]
                od = outs[oname]
                P.op("sp", lambda e, od=od, osb=osb, r0=r0, nc_=nc_, n=n: e.dma_start(
                    out=od[r0:r0 + nc_, n * 512:(n + 1) * 512], in_=osb[0:nc_, :]),
                    reads=[ok])
            rowpos[oname] += nc_
            off += nc_
    return kb.finish()


def build_tail(KC, T, glu, ydt=F32):
    kb = KB()
    P = kb.P
    x = kb.dram_in("x", [T, D_MODEL])
    w_v = kb.dram_in("w", [128, KC, D_MODEL])
    lng = kb.dram_in("lng", [128, D_MODEL])
    lnb = kb.dram_in("lnb", [128, D_MODEL])
    xo = kb.dram_out("xo", [T, D_MODEL])
    wb = kb.sb("wb", [128, KC, D_MODEL], BF16)
    yb = kb.sb("yb", [128, KC, T], BF16)
    STF = KC * 256
    stg = [kb.sb(f"stg{i}", [128, STF], F32) for i in range(2)]
    lng_s = kb.sb("lng_s", [128, D_MODEL], F32)
    lnb_s = kb.sb("lnb_s", [128, D_MODEL], F32)
    xt = [kb.sb(f"xt{i}", [128, D_MODEL], F32) for i in range(2)]
    stats = kb.sb("stats", [128, 4, 6], F32)
    mv = kb.sb("mv", [128, 2], F32)
    rstd = kb.sb("rstd", [128, 1], F32)
    epsc = kb.sb("epsc", [128, 1], F32)
    pss = [kb.ps(f"ps{i}", [128, 512], F32) for i in range(8)]
    psi = [0]

    def nextps():
        i = psi[0] % 8
        psi[0] += 1
        return pss[i], f"ps{i}"

    P.op("sp", lambda e: e.dma_start(out=lng_s[:], in_=lng[:, :]), writes=["lng"])
    P.op("sp", lambda e: e.dma_start(out=lnb_s[:], in_=lnb[:, :]), writes=["lnb"])
    P.op("pool", lambda e: e.memset(epsc[:], LN_EPS), writes=["epsc"])
    si = [0]

    def load_cast(dst, dkey, src_v, ncols, engs=("dve", "act")):
        kc = src_v.shape[1]
        kg = max(1, STF // ncols)
        for k0 in range(0, kc, kg):
            kn = min(kg, kc - k0)
            j = si[0] % 2
            q = ("sp", "qa", "qp")[si[0] % 3]
            si[0] += 1
            sv = stg[j][:, 0:kn * ncols].rearrange("p (k n) -> p k n", k=kn)
            P.op(q, lambda e, sv=sv, k0=k0, kn=kn: e.dma_start(out=sv, in_=src_v[:, k0:k0 + kn, :]), writes=[f"stg{j}"])
            wkeys = [f"{dkey}{kk}" for kk in range(k0, k0 + kn)]
            if engs[j % len(engs)] == "act":
                P.op("act", lambda e, sv=sv, k0=k0, kn=kn: e.copy(out=dst[:, k0:k0 + kn, 0:ncols], in_=sv), reads=[f"stg{j}"], writes=wkeys)
            else:
                P.op(engs[j % len(engs)], lambda e, sv=sv, k0=k0, kn=kn: e.tensor_copy(out=dst[:, k0:k0 + kn, 0:ncols], in_=sv),
                     reads=[f"stg{j}"], writes=wkeys)

    if glu:
        gT_v = kb.dram_in("gT", [128, KC, T])
        szT_v = kb.dram_in("szT", [128, KC, T])
        wglu_v = kb.dram_in("wglu", [128, KC, KC * 128])
        bglu = kb.dram_in("bglu", [128, KC])
        wgb = kb.sb("wgb", [128, KC, KC * 128], BF16)
        bg = kb.sb("bg", [128, KC], F32)
        g32 = kb.sb("g32", [128, KC, 512], F32)
        sz32 = kb.sb("sz32", [128, KC, 512], F32)
        gb = kb.sb("gb", [128, KC, 512], BF16)
        sig = [kb.sb(f"sig{i}", [128, 512], F32) for i in range(2)]
        P.op("sp", lambda e: e.dma_start(out=bg[:], in_=bglu[:, :]), writes=["bg"])
        load_cast(wgb, "wgb", wglu_v, KC * 128)
        for n in range(T // 512):
            P.op("sp", lambda e, n=n: e.dma_start(out=g32[:], in_=gT_v[:, :, n * 512:(n + 1) * 512]), writes=["g32"])
            P.op("qa", lambda e, n=n: e.dma_start(out=sz32[:], in_=szT_v[:, :, n * 512:(n + 1) * 512]), writes=["sz32"])
            P.op("act", lambda e: e.copy(out=gb[:], in_=g32[:]), reads=["g32"], writes=["gb"])
            for m in range(KC):
                ps, pk = nextps()
                for k in range(KC):
                    P.op("pe", lambda e, ps=ps, k=k, m=m: e.matmul(ps[:], wgb[:, k, m * 128:(m + 1) * 128], gb[:, k, :],
                                                                   start=(k == 0), stop=(k == KC - 1)),
                         reads=[f"wgb{k}", "gb"], writes=[pk])
                sg = sig[m % 2]
                P.op("act", lambda e, sg=sg, ps=ps, m=m: e.activation(out=sg[:], in_=ps[:], func=AF.Sigmoid, bias=bg[:, m:m + 1]),
                     reads=[pk, "bg"], writes=[f"sig{m % 2}"])
                P.op("dve", lambda e, sg=sg, m=m: e.tensor_tensor(out=sg[:], in0=sg[:], in1=g32[:, m, :], op=ALU.mult),
                     reads=["g32"], writes=[f"sig{m % 2}"])
                P.op("dve", lambda e, sg=sg, m=m, n=n: e.tensor_tensor(out=yb[:, m, n * 512:(n + 1) * 512], in0=sg[:], in1=sz32[:, m, :], op=ALU.mult),
                     reads=["sz32", f"sig{m % 2}"], writes=["yb"])
    else:
        yT_v = kb.dram_in("yT", [128, KC, T], ydt)
        if ydt == BF16:
            P.op("sp", lambda e: e.dma_start(out=yb[:, 0:KC // 2, :], in_=yT_v[:, 0:KC // 2, :]), writes=["yb"])
            P.op("qa", lambda e: e.dma_start(out=yb[:, KC // 2:KC, :], in_=yT_v[:, KC // 2:KC, :]), writes=["yb"])
        else:
            load_cast(yb, "yb", yT_v, T)
    load_cast(wb, "wb", w_v, D_MODEL)
    x_v = x.rearrange("(n p) d -> n p d", p=128)
    xo_v = xo.rearrange("(n p) d -> n p d", p=128)
    for tt in range(T // 128):
        xs = xt[tt % 2]
        xk = f"xt{tt % 2}"
        P.op("qa" if tt % 2 else "sp", lambda e, xs=xs, tt=tt: e.dma_start(out=xs[:], in_=x_v[tt]), writes=[xk])
        for n in range(4):
            ps, pk = nextps()
            for k in range(KC):
                P.op("pe", lambda e, ps=ps, k=k, tt=tt, n=n: e.matmul(ps[:], yb[:, k, tt * 128:(tt + 1) * 128], wb[:, k, n * 512:(n + 1) * 512],
                                                                      start=(k == 0), stop=(k == KC - 1)),
                     reads=["yb", f"wb{k}"], writes=[pk])
            P.op("dve", lambda e, xs=xs, ps=ps, n=n: e.scalar_tensor_tensor(out=xs[:, n * 512:(n + 1) * 512], in0=xs[:, n * 512:(n + 1) * 512], scalar=ALPHA,
                                                                            in1=ps[:], op0=ALU.mult, op1=ALU.add),
                 reads=[pk], writes=[xk])
            P.op("dve", lambda e, xs=xs, n=n: e.bn_stats(out=stats[:, n, :], in_=xs[:, n * 512:(n + 1) * 512]),
                 reads=[xk], writes=["stats"])
        P.op("dve", lambda e: e.bn_aggr(out=mv[:], in_=stats[:]), reads=["stats"], writes=["mv"])
        P.op("act", lambda e: e.activation(out=rstd[:], in_=mv[:, 1:2], func=AF.Sqrt, bias=epsc[:]), reads=["mv", "epsc"], writes=["rstd"])
        P.op("dve", lambda e: e.reciprocal(out=rstd[:], in_=rstd[:]), reads=["rstd"], writes=["rstd"])
        P.op("dve", lambda e, xs=xs: e.tensor_scalar(out=xs[:], in0=xs[:], scalar1=mv[:, 0:1], scalar2=rstd[:], op0=ALU.subtract, op1=ALU.mult),
             reads=["mv", "rstd"], writes=[xk])
        P.op("pool", lambda e, xs=xs: e.tensor_tensor(out=xs[:], in0=xs[:], in1=lng_s[:], op=ALU.mult), reads=["lng"], writes=[xk])
        P.op("pool", lambda e, xs=xs: e.tensor_tensor(out=xs[:], in0=xs[:], in1=lnb_s[:], op=ALU.add), reads=["lnb"], writes=[xk])
        P.op("sp", lambda e, xs=xs, tt=tt: e.dma_start(out=xo_v[tt], in_=xs[:]), reads=[xk])
    return kb.finish()


def build_scan(L=4096, NB=2):
    kb = KB()
    P = kb.P
    nc = kb.nc
    NT = L * NB
    NLEV = int(np.log2(L))
    CH = 1024
    NCH = L // CH
    uT = kb.dram_in("uT", [128, NT])
    lr_d = kb.dram_in("lr", [128, 4])
    li_d = kb.dram_in("li", [128, 4])
    ldt_d = kb.dram_in("ldt", [128, 4])
    br_d = kb.dram_in("br", [128, 4, 16])
    bi_d = kb.dram_in("bi", [128, 4, 16])
    cr_d = kb.dram_in("crT", [128, 4, 16])
    ci_d = kb.dram_in("ciT", [128, 4, 16])
    d_d = kb.dram_in("dvec", [128, 1])
    id_d = kb.dram_in("ident", [128, 128])
    gT = kb.dram_out("gT", [128, NT])

    ub = kb.sb("ub", [128, NT], BF16)
    yd = kb.sb("yd", [128, NT], F32)
    ident = kb.sb("ident_s", [128, 128], F32)
    sm = {n: kb.sb("sm_" + n, [128, 4], F32) for n in ["lr", "li", "dt", "th", "mag", "c", "s", "t1", "t2", "t3", "ar", "ai",
                                                       "den", "am1", "cr", "ci", "nci"]}
    br = kb.sb("br_s", [128, 4, 16], F32)
    bi = kb.sb("bi_s", [128, 4, 16], F32)
    crT = kb.sb("crT_s", [128, 4, 16], F32)
    ciT = kb.sb("ciT_s", [128, 4, 16], F32)
    bbr = kb.sb("bbr", [128, 4, 16], F32)
    bbi = kb.sb("bbi", [128, 4, 16], F32)
    tmpb = kb.sb("tmpb", [128, 4, 16], F32)
    Bblk = {n: kb.sb(n, [128, 4, 32], F32) for n in ["Bblk_r", "Bblk_i"]}
    BT = {n: kb.sb(n, [128, 128], BF16) for n in ["BT_r", "BT_i"]}
    Cpad = {n: kb.sb(n, [128, 4, 128], F32) for n in ["Cpad_r", "Cpad_ni"]}
    pc = kb.sb("pc", [128, NLEV, 4], F32)
    psn = kb.sb("psn", [128, NLEV, 4], F32)
    pns = kb.sb("pns", [128, NLEV, 4], F32)
    dv = kb.sb("dv", [128, 1], F32)
    halfpi = kb.sb("halfpi", [128, 1], F32)
    mtop = kb.sb("mtop", [128, 1], F32)
    mbot = kb.sb("mbot", [128, 1], F32)
    nmtop = kb.sb("nmtop", [128, 1], F32)
    nmbot = kb.sb("nmbot", [128, 1], F32)
    gt = [kb.sb(f"gt{i}", [128, 512], F32) for i in range(2)]
    gs = [kb.sb(f"gs{i}", [128, 512], F32) for i in range(2)]
    st0 = contextlib.ExitStack()
    u32 = st0.enter_context(nc.sbuf_tensor("u32", [128, NT], F32))
    pss = [kb.ps(f"ps{i}", [128, 512], F32) for i in range(8)]
    psi = [0]

    def nextps():
        i = psi[0] % 8
        psi[0] += 1
        return pss[i], f"ps{i}"

    def dma_in(dst, src, key, q="sp"):
        P.op(q, lambda e: e.dma_start(out=dst, in_=src), writes=[key])

    for q in range(4):
        P.op(("sp", "qa")[q % 2], lambda e, q=q: e.dma_start(out=u32[:, q * NT // 4:(q + 1) * NT // 4], in_=uT[:, q * NT // 4:(q + 1) * NT // 4]), writes=[f"u32_{q}"])
        P.op("dve", lambda e, q=q: e.tensor_copy(out=ub[:, q * NT // 4:(q + 1) * NT // 4], in_=u32[:, q * NT // 4:(q + 1) * NT // 4]), reads=[f"u32_{q}"], writes=[f"ub_{q}"])
    UBK = [f"ub_{q}" for q in range(4)]
    dma_in(sm["lr"][:], lr_d[:, :], "lr")
    dma_in(sm["li"][:], li_d[:, :], "li")
    dma_in(sm["dt"][:], ldt_d[:, :], "dt")
    dma_in(br[:], br_d[:, :, :], "br")
    dma_in(bi[:], bi_d[:, :, :], "bi")
    dma_in(crT[:], cr_d[:, :, :], "crT")
    dma_in(ciT[:], ci_d[:, :, :], "ciT")
    dma_in(dv[:], d_d[:, :], "dv")
    dma_in(ident[:], id_d[:, :], "ident")
    P.op("pool", lambda e: e.memset(halfpi[:], float(np.pi / 2)), writes=["halfpi"])
    for (t_, a, b) in [(mtop, 1.0, 0.0), (mbot, 0.0, 1.0), (nmtop, -1.0, 0.0), (nmbot, 0.0, -1.0)]:
        P.op("pool", lambda e, t_=t_, a=a: e.memset(t_[0:64, :], a), writes=["masks"])
        P.op("pool", lambda e, t_=t_, b=b: e.memset(t_[64:128, :], b), writes=["masks"])
    for n in ["Bblk_r", "Bblk_i"]:
        P.op("pool", lambda e, n=n: e.memset(Bblk[n][:], 0.0), writes=[n])
    for n in ["Cpad_r", "Cpad_ni"]:
        P.op("pool", lambda e, n=n: e.memset(Cpad[n][:], 0.0), writes=[n])

    def tt(out, a, b, op, rk, wk, eng="dve"):
        P.op(eng, lambda e: e.tensor_tensor(out=out, in0=a, in1=b, op=op), reads=rk, writes=wk)

    def S(n):
        return sm[n][:]

    def renorm(c_ap, s_ap, ck, sk):
        tt(S("t1"), c_ap, c_ap, ALU.mult, [ck], ["t1"])
        tt(S("t2"), s_ap, s_ap, ALU.mult, [sk], ["t2"])
        tt(S("t1"), S("t1"), S("t2"), ALU.add, ["t2"], ["t1"])
        P.op("dve", lambda e: e.tensor_scalar(out=S("t1"), in0=S("t1"), scalar1=-0.5, scalar2=1.5, op0=ALU.mult, op1=ALU.add), writes=["t1"])
        tt(c_ap, c_ap, S("t1"), ALU.mult, ["t1"], [ck])
        tt(s_ap, s_ap, S("t1"), ALU.mult, ["t1"], [sk])

    P.op("act", lambda e: e.activation(out=S("dt"), in_=S("dt"), func=AF.Exp), reads=["dt"], writes=["dt"])
    tt(S("th"), S("li"), S("dt"), ALU.mult, ["li", "dt"], ["th"])
    tt(S("t1"), S("lr"), S("dt"), ALU.mult, ["lr", "dt"], ["t1"])
    P.op("act", lambda e: e.activation(out=S("mag"), in_=S("t1"), func=AF.Exp), reads=["t1"], writes=["mag"])
    P.op("act", lambda e: e.activation(out=S("s"), in_=S("th"), func=AF.Sin, scale=1.0 / 16), reads=["th"], writes=["s"])
    P.op("act", lambda e: e.activation(out=S("c"), in_=S("th"), func=AF.Sin, scale=1.0 / 16, bias=halfpi[:]), reads=["th", "halfpi"], writes=["c"])
    for _ in range(4):
        tt(S("t1"), S("c"), S("c"), ALU.mult, ["c"], ["t1"])
        tt(S("t2"), S("s"), S("s"), ALU.mult, ["s"], ["t2"])
        tt(S("t3"), S("c"), S("s"), ALU.mult, ["c", "s"], ["t3"])
        tt(S("c"), S("t1"), S("t2"), ALU.subtract, ["t1", "t2"], ["c"])
        tt(S("s"), S("t3"), S("t3"), ALU.add, ["t3"], ["s"])
    renorm(S("c"), S("s"), "c", "s")
    tt(S("ar"), S("mag"), S("c"), ALU.mult, ["mag", "c"], ["ar"])
    tt(S("ai"), S("mag"), S("s"), ALU.mult, ["mag", "s"], ["ai"])
    P.op("dve", lambda e: e.tensor_copy(out=pc[:, 0, :], in_=S("c")), reads=["c"], writes=["pc"])
    P.op("dve", lambda e: e.tensor_copy(out=psn[:, 0, :], in_=S("s")), reads=["s"], writes=["psn"])
    for lev in range(1, NLEV):
        tt(S("t1"), pc[:, lev - 1, :], pc[:, lev - 1, :], ALU.mult, ["pc"], ["t1"])
        tt(S("t2"), psn[:, lev - 1, :], psn[:, lev - 1, :], ALU.mult, ["psn"], ["t2"])
        tt(S("t3"), pc[:, lev - 1, :], psn[:, lev - 1, :], ALU.mult, ["pc", "psn"], ["t3"])
        tt(pc[:, lev, :], S("t1"), S("t2"), ALU.subtract, ["t1", "t2"], ["pc"])
        tt(psn[:, lev, :], S("t3"), S("t3"), ALU.add, ["t3"], ["psn"])
        renorm(pc[:, lev, :], psn[:, lev, :], "pc", "psn")
    P.op("dve", lambda e: e.tensor_scalar(out=pns[:], in0=psn[:], scalar1=-1.0, scalar2=None, op0=ALU.mult), reads=["psn"], writes=["pns"])
    tt(S("t1"), S("lr"), S("lr"), ALU.mult, ["lr"], ["t1"])
    tt(S("t2"), S("li"), S("li"), ALU.mult, ["li"], ["t2"])
    tt(S("den"), S("t1"), S("t2"), ALU.add, ["t1", "t2"], ["den"])
    P.op("dve", lambda e: e.reciprocal(out=S("den"), in_=S("den")), reads=["den"], writes=["den"])
    P.op("dve", lambda e: e.tensor_scalar(out=S("am1"), in0=S("ar"), scalar1=-1.0, scalar2=None, op0=ALU.add), reads=["ar"], writes=["am1"])
    tt(S("t1"), S("am1"), S("lr"), ALU.mult, ["am1", "lr"], ["t1"])
    tt(S("t2"), S("ai"), S("li"), ALU.mult, ["ai", "li"], ["t2"])
    tt(S("t3"), S("t1"), S("t2"), ALU.add, ["t1", "t2"], ["t3"])
    tt(S("cr"), S("t3"), S("den"), ALU.mult, ["t3", "den"], ["cr"])
    tt(S("t1"), S("ai"), S("lr"), ALU.mult, ["ai", "lr"], ["t1"])
    tt(S("t2"), S("am1"), S("li"), ALU.mult, ["am1", "li"], ["t2"])
    tt(S("t3"), S("t1"), S("t2"), ALU.subtract, ["t1", "t2"], ["t3"])
    tt(S("ci"), S("t3"), S("den"), ALU.mult, ["t3", "den"], ["ci"])
    P.op("dve", lambda e: e.tensor_scalar(out=S("nci"), in0=S("ci"), scalar1=-1.0, scalar2=None, op0=ALU.mult), reads=["ci"], writes=["nci"])
    for j in range(4):
        P.op("dve", lambda e, j=j: e.tensor_scalar(out=tmpb[:, j, :], in0=bi[:, j, :], scalar1=sm["nci"][:, j:j + 1], scalar2=None, op0=ALU.mult),
             reads=["bi", "nci"], writes=["tmpb"])
        P.op("dve", lambda e, j=j: e.scalar_tensor_tensor(out=bbr[:, j, :], in0=br[:, j, :], scalar=sm["cr"][:, j:j + 1], in1=tmpb[:, j, :], op0=ALU.mult, op1=ALU.add),
             reads=["br", "cr", "tmpb"], writes=["bbr"])
        P.op("dve", lambda e, j=j: e.tensor_scalar(out=tmpb[:, j, :], in0=br[:, j, :], scalar1=sm["ci"][:, j:j + 1], scalar2=None, op0=ALU.mult),
             reads=["br", "ci"], writes=["tmpb"])
        P.op("dve", lambda e, j=j: e.scalar_tensor_tensor(out=bbi[:, j, :], in0=bi[:, j, :], scalar=sm["cr"][:, j:j + 1], in1=tmpb[:, j, :], op0=ALU.mult, op1=ALU.add),
             reads=["bi", "cr", "tmpb"], writes=["bbi"])
    for (src, n) in [(bbr, "Bblk_r"), (bbi, "Bblk_i")]:
        sk = "bbr" if n == "Bblk_r" else "bbi"
        P.op("dve", lambda e, src=src, n=n: e.tensor_scalar(out=Bblk[n][:, :, 0:16], in0=src[:], scalar1=mtop[:], scalar2=None, op0=ALU.mult),
             reads=[sk, "masks"], writes=[n])
        P.op("dve", lambda e, src=src, n=n: e.tensor_scalar(out=Bblk[n][:, :, 16:32], in0=src[:], scalar1=mbot[:], scalar2=None, op0=ALU.mult),
             reads=[sk, "masks"], writes=[n])
    for (n, bn) in [("Bblk_r", "BT_r"), ("Bblk_i", "BT_i")]:
        ps, pk = nextps()
        P.op("pe", lambda e, ps=ps, n=n: e.transpose(ps[:, 0:128], Bblk[n][:].rearrange("p j c -> p (j c)"), ident[:]), reads=[n, "ident"], writes=[pk])
        P.op("act", lambda e, ps=ps, bn=bn: e.copy(out=BT[bn][:], in_=ps[:, 0:128]), reads=[pk], writes=[bn])
    for j in range(4):
        for (cn, src, ma, mb, sk) in [("Cpad_r", crT, mtop, mbot, "crT"), ("Cpad_ni", ciT, nmtop, nmbot, "ciT")]:
            P.op("dve", lambda e, j=j, cn=cn, src=src, ma=ma: e.tensor_scalar(out=Cpad[cn][:, j, 32 * j:32 * j + 16], in0=src[:, j, :], scalar1=ma[:], scalar2=None, op0=ALU.mult),
                 reads=[sk, "masks"], writes=[cn])
            P.op("dve", lambda e, j=j, cn=cn, src=src, mb=mb: e.tensor_scalar(out=Cpad[cn][:, j, 32 * j + 16:32 * j + 32], in0=src[:, j, :], scalar1=mb[:], scalar2=None, op0=ALU.mult),
                 reads=[sk, "masks"], writes=[cn])
    NBLK = L // 512
    for q in range(4):
        P.op("pool", lambda e, q=q: e.tensor_scalar(out=yd[:, q * NT // 4:(q + 1) * NT // 4], in0=u32[:, q * NT // 4:(q + 1) * NT // 4], scalar1=dv[:], scalar2=None, op0=ALU.mult),
             reads=[f"u32_{q}", "dv"], writes=[f"yd_{q}"])
    P.barrier()
    st0.close()

    def ydk(tok):
        return f"yd_{tok // (NT // 4)}"

    Ec = kb.sb("Ec", [128, L], F32)
    Es = kb.sb("Es", [128, L], F32)
    magf = kb.sb("magf", [128, L], F32)
    Xr = kb.sb("Xr", [128, L], F32)
    Xi = kb.sb("Xi", [128, L], F32)
    T1 = [kb.sb(f"T1_{i}", [128, CH], F32) for i in range(2)]
    T2 = [kb.sb(f"T2_{i}", [128, CH], F32) for i in range(2)]
    XRK = [f"Xr{c}" for c in range(NCH)]
    XIK = [f"Xi{c}" for c in range(NCH)]
    tci = [0]

    def rotate(sign):
        for c in range(NCH):
            sl = slice(c * CH, (c + 1) * CH)
            k = tci[0] % 2
            tci[0] += 1
            t1, t2 = T1[k], T2[k]
            P.op("pool", lambda e, sl=sl, t1=t1: e.tensor_tensor(out=t1[:], in0=Xi[:, sl], in1=Es[:, sl], op=ALU.mult), reads=[XIK[c], "E"], writes=[f"T1_{k}"])
            P.op("pool", lambda e, sl=sl, t2=t2: e.tensor_tensor(out=t2[:], in0=Xr[:, sl], in1=Es[:, sl], op=ALU.mult), reads=[XRK[c], "E"], writes=[f"T2_{k}"])
            P.op("dve", lambda e, sl=sl: e.tensor_tensor(out=Xr[:, sl], in0=Xr[:, sl], in1=Ec[:, sl], op=ALU.mult), reads=["E"], writes=[XRK[c]])
            P.op("dve", lambda e, sl=sl: e.tensor_tensor(out=Xi[:, sl], in0=Xi[:, sl], in1=Ec[:, sl], op=ALU.mult), reads=["E"], writes=[XIK[c]])
            P.op("dve", lambda e, sl=sl, t1=t1: e.tensor_tensor(out=Xr[:, sl], in0=Xr[:, sl], in1=t1[:], op=(ALU.add if sign < 0 else ALU.subtract)), reads=[f"T1_{k}"], writes=[XRK[c]])
            P.op("dve", lambda e, sl=sl, t2=t2: e.tensor_tensor(out=Xi[:, sl], in0=Xi[:, sl], in1=t2[:], op=(ALU.subtract if sign < 0 else ALU.add)), reads=[f"T2_{k}"], writes=[XIK[c]])

    for j in range(4):
        P.op("dve", lambda e: e.memset(Ec[:, 0:1], 1.0), writes=["E"])
        P.op("dve", lambda e: e.memset(Es[:, 0:1], 0.0), writes=["E"])
        for lev in range(NLEV):
            n = 1 << lev
            C = pc[:, lev, j:j + 1]
            Sn = psn[:, lev, j:j + 1]
            Sm = pns[:, lev, j:j + 1]
            t1 = T1[0] if n <= CH else None
            for o in range(0, n, CH):
                w_ = min(CH, n - o)
                a, b = T1[0], T2[0]
                P.op("dve", lambda e, o=o, w_=w_, a=a, Sm=Sm: e.tensor_scalar(out=a[:, 0:w_], in0=Es[:, o:o + w_], scalar1=Sm, scalar2=None, op0=ALU.mult), reads=["E", "pns"], writes=["T1_0"])
                P.op("dve", lambda e, o=o, w_=w_, b=b, Sn=Sn: e.tensor_scalar(out=b[:, 0:w_], in0=Ec[:, o:o + w_], scalar1=Sn, scalar2=None, op0=ALU.mult), reads=["E", "psn"], writes=["T2_0"])
                P.op("dve", lambda e, o=o, w_=w_, a=a, C=C, n=n: e.scalar_tensor_tensor(out=Ec[:, n + o:n + o + w_], in0=Ec[:, o:o + w_], scalar=C, in1=a[:, 0:w_], op0=ALU.mult, op1=ALU.add),
                     reads=["T1_0", "pc"], writes=["E"])
                P.op("dve", lambda e, o=o, w_=w_, b=b, C=C, n=n: e.scalar_tensor_tensor(out=Es[:, n + o:n + o + w_], in0=Es[:, o:o + w_], scalar=C, in1=b[:, 0:w_], op0=ALU.mult, op1=ALU.add),
                     reads=["T2_0", "pc"], writes=["E"])
        P.op("act", lambda e, j=j: e.activation(out=magf[:], in_=Ec[:], func=AF.Identity, scale=0.0, bias=sm["mag"][:, j:j + 1]), reads=["E", "mag"], writes=["magf"])
        for b in range(NB):
            for blk in range(NBLK):
                t0 = b * L + blk * 512
                c = (blk * 512) // CH
                for (bn, X_, xk) in [("BT_r", Xr, XRK[c]), ("BT_i", Xi, XIK[c])]:
                    ps, pk = nextps()
                    P.op("pe", lambda e, ps=ps, bn=bn, t0=t0, j=j: e.matmul(ps[:], BT[bn][32 * j:32 * j + 32, :], ub[32 * j:32 * j + 32, t0:t0 + 512],
                                                                            start=True, stop=True, tile_position=(32 * j, 0)),
                         reads=[bn] + UBK, writes=[pk])
                    P.op("act", lambda e, ps=ps, X_=X_, blk=blk: e.copy(out=X_[:, blk * 512:(blk + 1) * 512], in_=ps[:]), reads=[pk], writes=[xk])
            rotate(-1)
            P.op("dve", lambda e: e.tensor_tensor_scan(out=Xr[:], data0=magf[:], data1=Xr[:], initial=0.0, op0=ALU.mult, op1=ALU.add), reads=["magf"], writes=XRK)
            P.op("dve", lambda e: e.tensor_tensor_scan(out=Xi[:], data0=magf[:], data1=Xi[:], initial=0.0, op0=ALU.mult, op1=ALU.add), reads=["magf"], writes=XIK)
            rotate(+1)
            for blk in range(NBLK):
                t0 = b * L + blk * 512
                c = (blk * 512) // CH
                ps, pk = nextps()
                P.op("pe", lambda e, ps=ps, blk=blk, j=j: e.matmul(ps[:], Cpad["Cpad_r"][:, j, :], Xr[:, blk * 512:(blk + 1) * 512], start=True, stop=False),
                     reads=["Cpad_r", XRK[c]], writes=[pk])
                P.op("pe", lambda e, ps=ps, blk=blk, j=j: e.matmul(ps[:], Cpad["Cpad_ni"][:, j, :], Xi[:, blk * 512:(blk + 1) * 512], start=False, stop=True),
                     reads=["Cpad_ni", XIK[c]], writes=[pk])
                P.op("dve", lambda e, ps=ps, t0=t0: e.tensor_tensor(out=yd[:, t0:t0 + 512], in0=yd[:, t0:t0 + 512], in1=ps[:], op=ALU.add),
                     reads=[pk], writes=[ydk(t0)])
    for i in range(NT // 512):
        t0 = i * 512
        a, ak = gt[i % 2], f"gt{i % 2}"
        sg, sk = gs[i % 2], f"gs{i % 2}"
        P.op("act", lambda e, a=a, t0=t0: e.activation(out=a[:], in_=yd[:, t0:t0 + 512], func=AF.Square), reads=[ydk(t0)], writes=[ak])
        P.op("dve", lambda e, a=a: e.tensor_scalar(out=a[:], in0=a[:], scalar1=0.044715, scalar2=1.0, op0=ALU.mult, op1=ALU.add), reads=[], writes=[ak])
        P.op("dve", lambda e, a=a, t0=t0: e.tensor_tensor(out=a[:], in0=a[:], in1=yd[:, t0:t0 + 512], op=ALU.mult), reads=[ydk(t0)], writes=[ak])
        P.op("act", lambda e, a=a, sg=sg: e.activation(out=sg[:], in_=a[:], func=AF.Sigmoid, scale=1.5957691216057308), reads=[ak], writes=[sk])
        P.op("dve", lambda e, sg=sg, t0=t0: e.tensor_tensor(out=sg[:], in0=sg[:], in1=yd[:, t0:t0 + 512], op=ALU.mult), reads=[ydk(t0)], writes=[sk])
        P.op(("sp", "qa")[i % 2], lambda e, sg=sg, t0=t0: e.dma_start(out=gT[:, t0:t0 + 512], in_=sg[:]), reads=[sk])
    return kb.finish()


def attn_consts():
    m = np.arange(128)[:, None]
    n = np.arange(512)[None, :]
    wm = np.zeros((8, 128, 512), np.float32)
    for rel in range(-4, 0):
        wm[rel + 4] = np.where((128 * rel + m) <= n - 512, NEG, 0.0)
    for rel in range(0, 4):
        wm[rel + 4] = np.where((128 * rel + m) > n, NEG, 0.0)
    wmask = np.ascontiguousarray(wm.transpose(1, 0, 2)).astype(NPBF)
    j = np.arange(64)[:, None, None]
    kt = np.arange(32)[None, :, None]
    mm = np.arange(128)[None, None, :]
    ekt = np.where(j == 2 * kt + mm // 64, NEG, 0.0).astype(NPBF)
    r = np.arange(128)[:, None]
    jj = np.arange(503)[None, :]
    cmpw = np.where(16 * (jj - 248) + 31 <= r, 0.0, NEG).astype(np.float32)
    jj = np.arange(126)[None, :]
    rel = (jj - 62) - (r >= 64)
    keepw = np.where((rel > 0) | (rel == 0) | (rel == -1), 0.0, 1.0).astype(np.float32)
    addw = np.where(rel > 0, -1e30, np.where((rel == 0) | (rel == -1), 1e6, 0.0)).astype(np.float32)
    onehot = np.zeros((12, 12, 128), np.float32)
    for g in range(12):
        onehot[g, g, :] = 1.0
    return dict(wmask=wmask, ekt=ekt, cmpw=cmpw, keepw=keepw, addw=addw, onehot=onehot,
                identf=np.eye(128, dtype=np.float32), identb=np.eye(128).astype(NPBF),
                onesb=np.ones((128, 128)).astype(NPBF))


def build_attn(L=4096):
    kb = KB()
    P = kb.P
    nc = kb.nc
    NQB = L // 512
    NKT = L // 128
    NC = (L - 32) // 16 + 1
    NC1 = NC - 128
    NSB = L // 64
    qT = kb.dram_in("qT", [128, 4, L], BF16)
    szT = kb.dram_in("szT", [128, 12, L], BF16)
    gateT = kb.dram_in("gateT", [12, L], F32)
    kcT = kb.dram_in("kcT", [128, L], BF16)
    vcT = kb.dram_in("vcT", [128, L], BF16)
    ksT = kb.dram_in("ksT", [128, L], BF16)
    kwT = kb.dram_in("kwT", [128, L], BF16)
    vs_d = kb.dram_in("vs", [128, NKT, 128], BF16)
    vw_d = kb.dram_in("vw", [128, NKT, 128], BF16)
    posk = kb.dram_in("poskT", [128, 32], F32)
    posv = kb.dram_in("posvT", [128, 32], F32)
    w1k = kb.dram_in("w1k", [128, 32, 128], F32)
    w1v = kb.dram_in("w1v", [128, 32, 128], F32)
    w2k = kb.dram_in("w2k", [128, 128], F32)
    w2v = kb.dram_in("w2v", [128, 128], F32)
    c_wmask = kb.dram_in("wmask", [128, 8, 512], BF16)
    c_ekt = kb.dram_in("ekt", [64, 32, 128], BF16)
    c_cmpw = kb.dram_in("cmpw", [128, 503], F32)
    c_keepw = kb.dram_in("keepw", [128, 126], F32)
    c_addw = kb.dram_in("addw", [128, 126], F32)
    c_onehot = kb.dram_in("onehot", [12, 12, 128], F32)
    c_identf = kb.dram_in("identf", [128, 128], F32)
    c_identb = kb.dram_in("identb", [128, 128], BF16)
    c_onesb = kb.dram_in("onesb", [128, 128], BF16)
    yT = kb.dram_out("yT", [128, 4, L], BF16)
    lq = [0]

    def load(name, src, shape, dt, st=None):
        t = (st or kb.st).enter_context(nc.sbuf_tensor(name, list(shape), dt))
        q = ("sp", "qa")[lq[0] % 2]
        lq[0] += 1
        P.op(q, lambda e: e.dma_start(out=t[:], in_=src), writes=[name])
        return t

    st0 = contextlib.ExitStack()

    def sb0(name, shape, dt):
        return st0.enter_context(nc.sbuf_tensor(name, list(shape), dt))

    ks_s = load("ks_s", ksT[:, :], [128, L], BF16)
    kw_s = load("kw_s", kwT[:, :], [128, L], BF16)
    vs_s = load("vs_s", vs_d[:, :, :], [128, NKT, 128], BF16)
    vw_s = load("vw_s", vw_d[:, :, :], [128, NKT, 128], BF16)
    wmask = load("wmask_s", c_wmask[:, :, :], [128, 8, 512], BF16)
    ekt = load("ekt_s", c_ekt[:, :, :], [64, 32, 128], BF16)
    cmpw = load("cmpw_s", c_cmpw[:, :], [128, 503], F32)
    keepw = load("keepw_s", c_keepw[:, :], [128, 126], F32)
    addw = load("addw_s", c_addw[:, :], [128, 126], F32)
    onehot = load("onehot_s", c_onehot[:, :, :], [12, 12, 128], F32)
    identf = load("identf_s", c_identf[:, :], [128, 128], F32)
    identb = load("identb_s", c_identb[:, :], [128, 128], BF16)
    onesb = load("onesb_s", c_onesb[:, :], [128, 128], BF16)
    kcmpT = kb.sb("kcmpT", [128, NC], BF16)
    vcmp = kb.sb("vcmp", [128, 2, 128], BF16)
    kc_s = load("kc_s", kcT[:, :], [128, L], BF16, st0)
    vc_s = load("vc_s", vcT[:, :], [128, L], BF16, st0)
    posk_s = load("posk_s", posk[:, :], [128, 32], F32, st0)
    posv_s = load("posv_s", posv[:, :], [128, 32], F32, st0)
    w2k_s = load("w2k_s", w2k[:, :], [128, 128], F32, st0)
    w2v_s = load("w2v_s", w2v[:, :], [128, 128], F32, st0)
    w1st = sb0("w1st", [128, 32, 128], F32)
    w1b = sb0("w1b", [128, 32, 128], BF16)
    w2b = sb0("w2b", [128, 128], BF16)
    tmpl = [sb0(f"tmpl{i}", [128, NC], BF16) for i in range(4)]
    h1 = sb0("h1", [128, NC], F32)
    h2 = sb0("h2", [128, NC], F32)
    hg = sb0("hg", [128, NC], BF16)

    pss = [kb.ps(f"ps{i}", [128, 512], F32) for i in range(8)]
    SROT = [0, 1]
    PS_O = [2, 3]
    PS_S = [4, 5]
    PS_C = [6, 7]
    srot = [0]
    crot = [0]

    def ps_s_next():
        i = SROT[srot[0] % 2]
        srot[0] += 1
        return pss[i], f"ps{i}"

    def ps_c_next():
        i = PS_C[crot[0] % 2]
        crot[0] += 1
        return pss[i], f"ps{i}"

    for side, (src_s, pos_s, w1d, w2s) in enumerate([(kc_s, posk_s, w1k, w2k_s), (vc_s, posv_s, w1v, w2v_s)]):
        srck = "kc_s" if side == 0 else "vc_s"
        posn = "posk_s" if side == 0 else "posv_s"
        w2n = "w2k_s" if side == 0 else "w2v_s"
        P.op("sp", lambda e, w1d=w1d: e.dma_start(out=w1st[:], in_=w1d[:, :, :]), writes=["w1st"])
        P.op("pool", lambda e: e.tensor_copy(out=w1b[:], in_=w1st[:]), reads=["w1st"], writes=["w1b"])
        P.op("pool", lambda e, w2s=w2s: e.tensor_copy(out=w2b[:], in_=w2s[:]), reads=[w2n], writes=["w2b"])
        ps, pk = ps_c_next()
        for l in range(32):
            tl, tk = tmpl[l % 4], f"tmpl{l % 4}"
            P.op("dve", lambda e, tl=tl, l=l, src_s=src_s, pos_s=pos_s: e.tensor_scalar(
                out=tl[:], in0=src_s[:, l:l + 16 * (NC - 1) + 1:16], scalar1=pos_s[:, l:l + 1], scalar2=None, op0=ALU.add),
                reads=[srck, posn], writes=[tk])
            P.op("pe", lambda e, ps=ps, tl=tl, l=l: e.matmul(ps[:, 0:NC], w1b[:, l, :], tl[:], start=(l == 0), stop=(l == 31)),
                 reads=["w1b", tk], writes=[pk])
        P.op("act", lambda e, ps=ps: e.copy(out=h1[:], in_=ps[:, 0:NC]), reads=[pk], writes=["h1"])
        P.op("act", lambda e: e.activation(out=h2[:], in_=h1[:], func=AF.Square), reads=["h1"], writes=["h2"])
        P.op("dve", lambda e: e.tensor_scalar(out=h2[:], in0=h2[:], scalar1=0.044715, scalar2=1.0, op0=ALU.mult, op1=ALU.add), writes=["h2"])
        P.op("dve", lambda e: e.tensor_tensor(out=h2[:], in0=h2[:], in1=h1[:], op=ALU.mult), reads=["h1"], writes=["h2"])
        P.op("act", lambda e: e.activation(out=h2[:], in_=h2[:], func=AF.Sigmoid, scale=1.5957691216057308), writes=["h2"])
        P.op("dve", lambda e: e.tensor_tensor(out=hg[:], in0=h2[:], in1=h1[:], op=ALU.mult), reads=["h1", "h2"], writes=["hg"])
        if side == 0:
            ps, pk = ps_c_next()
            P.op("pe", lambda e, ps=ps: e.matmul(ps[:, 0:NC], w2b[:], hg[:], start=True, stop=True), reads=["w2b", "hg"], writes=[pk])
            P.op("act", lambda e, ps=ps: e.copy(out=kcmpT[:], in_=ps[:, 0:NC]), reads=[pk], writes=["kcmpT"])
        else:
            for t2, (n0, nn) in enumerate([(0, 128), (128, NC1)]):
                ps, pk = ps_c_next()
                P.op("pe", lambda e, ps=ps, n0=n0, nn=nn: e.matmul(ps[0:nn, 0:128], hg[:, n0:n0 + nn], w2b[:], start=True, stop=True),
                     reads=["w2b", "hg"], writes=[pk])
                P.op("act", lambda e, ps=ps, nn=nn, t2=t2: e.copy(out=vcmp[0:nn, t2, :], in_=ps[0:nn, 0:128]), reads=[pk], writes=["vcmp"])
    P.barrier()
    st0.close()

    qb_s = [kb.sb(f"qb{i}", [128, 4, 512], BF16) for i in range(2)]
    sz_s = [kb.sb(f"sz{i}", [128, 12, 512], BF16) for i in range(2)]
    gt_s = [kb.sb(f"gt{i}", [12, 512], F32) for i in range(2)]
    gbc = [kb.sb(f"gbc{i}", [128, 12, 512], F32) for i in range(2)]
    sc = [kb.sb(f"sc{i}", [128, NC], F32) for i in range(2)]
    pe_ = [kb.sb(f"pe_{i}", [128, NC], F32) for i in range(2)]
    pn = [kb.sb(f"pn{i}", [128, NC], F32) for i in range(2)]
    ppad = kb.sb("ppad", [128, 260], F32)
    rs = [kb.sb(f"rs{i}", [128, 1], F32) for i in range(2)]
    pnT = [kb.sb(f"pnT{i}", [128, 2, 128], BF16) for i in range(2)]
    ocmp = [kb.sb(f"ocmp{i}", [128, 4, 512], F32) for i in range(2)]
    imp = kb.sb("imp", [128, NSB], F32)
    imp2 = kb.sb("imp2", [128, NSB], F32)
    v8a = kb.sb("v8a", [128, 8], F32)
    v8b = kb.sb("v8b", [128, 8], F32)
    nsel = kb.sb("nsel", [128, NSB], F32)
    nsT = [kb.sb(f"nsT{i}", [64, 512], BF16) for i in range(2)]
    pT = [kb.sb(f"pT{i}", [128, 512], BF16) for i in range(3)]
    osb = kb.sb("osb", [128, 512], F32)
    rden = kb.sb("rden", [128, 512], F32)
    wgt = kb.sb("wgt", [128, 512], F32)
    tt1 = kb.sb("tt1", [128, 512], F32)
    acc = kb.sb("acc", [128, 4, 512], F32)
    yb = [kb.sb(f"ybo{i}", [128, 4, 512], BF16) for i in range(2)]
    P.op("pool", lambda e: e.memset(ppad[:], 0.0), writes=["ppad"])
    pti = [0]

    def emit_cmp(qb):
        par = qb % 2
        q0 = qb * 512
        qs, qk = qb_s[par], f"qb{par}"
        szs, szk = sz_s[par], f"sz{par}"
        gts, gk = gt_s[par], f"gt{par}"
        P.op("sp", lambda e: e.dma_start(out=qs[:], in_=qT[:, :, q0:q0 + 512]), writes=[qk])
        P.op("qa", lambda e: e.dma_start(out=szs[:], in_=szT[:, :, q0:q0 + 512]), writes=[szk])
        P.op("sp", lambda e: e.dma_start(out=gts[:], in_=gateT[:, q0:q0 + 512]), writes=[gk])
        for gi in range(12):
            ps, pk = ps_c_next()
            P.op("pe", lambda e, ps=ps, gi=gi: e.matmul(ps[:], onehot[:, gi, :], gts[:], start=True, stop=True), reads=["onehot_s", gk], writes=[pk])
            P.op("act", lambda e, ps=ps, gi=gi: e.copy(out=gbc[par][:, gi, :], in_=ps[:]), reads=[pk], writes=[f"gbc{par}"])
        for qt in range(4):
            it = qb * 4 + qt
            c0 = 248 - 8 * it
            for pair in ((0, 1), (2, 3)):
                bank = {h: (pss[PS_C[h % 2]], f"ps{PS_C[h % 2]}") for h in pair}
                for h in pair:
                    ps, pk = bank[h]
                    P.op("pe", lambda e, ps=ps, h=h, qt=qt: e.matmul(ps[:, 0:NC], qs[:, h, qt * 128:(qt + 1) * 128], kcmpT[:], start=True, stop=True),
                         reads=[qk, "kcmpT"], writes=[pk])
                for h in pair:
                    ps, pk = bank[h]
                    i2 = h % 2
                    P.op("dve", lambda e, ps=ps, c0=c0, i2=i2: e.tensor_tensor(out=sc[i2][:], in0=ps[:, 0:NC], in1=cmpw[:, c0:c0 + NC], op=ALU.add),
                         reads=[pk, "cmpw_s"], writes=[f"sc{i2}"])
                for h in pair:
                    i2 = h % 2
                    P.op("act", lambda e, i2=i2: e.activation(out=pe_[i2][:], in_=sc[i2][:], func=AF.Exp, accum_out=rs[i2][:]), reads=[f"sc{i2}"], writes=[f"pe_{i2}", f"rs{i2}"])
                for h in pair:
                    i2 = h % 2
                    P.op("dve", lambda e, i2=i2: e.tensor_scalar(out=rs[i2][:], in0=rs[i2][:], scalar1=1e-30, scalar2=None, op0=ALU.max), writes=[f"rs{i2}"])
                for h in pair:
                    i2 = h % 2
                    P.op("dve", lambda e, i2=i2: e.reciprocal(out=rs[i2][:], in_=rs[i2][:]), writes=[f"rs{i2}"])
                for h in pair:
                    i2 = h % 2
                    P.op("dve", lambda e, i2=i2: e.tensor_scalar(out=pn[i2][:], in0=pe_[i2][:], scalar1=rs[i2][:], scalar2=None, op0=ALU.mult),
                         reads=[f"pe_{i2}", f"rs{i2}"], writes=[f"pn{i2}"])
                for h in pair:
                    i2 = h % 2
                    if h == 0:
                        P.op("pool", lambda e, i2=i2: e.tensor_copy(out=ppad[:, 1:1 + NC], in_=pn[i2][:]), reads=[f"pn{i2}"], writes=["ppad"])
                    else:
                        P.op("pool", lambda e, i2=i2: e.tensor_tensor(out=ppad[:, 1:1 + NC], in0=ppad[:, 1:1 + NC], in1=pn[i2][:], op=ALU.add),
                             reads=[f"pn{i2}"], writes=["ppad"])
                for h in pair:
                    ps, pk = bank[h]
                    i2 = h % 2
                    P.op("pe", lambda e, ps=ps, i2=i2: e.transpose(ps[:, 0:128], pn[i2][:, 0:128], identf[:]), reads=[f"pn{i2}", "identf_s"], writes=[pk])
                    P.op("pe", lambda e, ps=ps, i2=i2: e.transpose(ps[0:NC1, 128:256], pn[i2][:, 128:NC], identf[:]), reads=[f"pn{i2}", "identf_s"], writes=[pk])
                for h in pair:
                    ps, pk = bank[h]
                    i2 = h % 2
                    P.op("act", lambda e, ps=ps, i2=i2: e.copy(out=pnT[i2][:, 0, :], in_=ps[:, 0:128]), reads=[pk], writes=[f"pnT{i2}"])
                    P.op("act", lambda e, ps=ps, i2=i2: e.copy(out=pnT[i2][0:NC1, 1, :], in_=ps[0:NC1, 128:256]), reads=[pk], writes=[f"pnT{i2}"])
                for h in pair:
                    ps, pk = bank[h]
                    i2 = h % 2
                    P.op("pe", lambda e, ps=ps, i2=i2: e.matmul(ps[:, 256:384], vcmp[:, 0, :], pnT[i2][:, 0, :], start=True, stop=False),
                         reads=["vcmp", f"pnT{i2}"], writes=[pk])
                    P.op("pe", lambda e, ps=ps, i2=i2: e.matmul(ps[:, 256:384], vcmp[0:NC1, 1, :], pnT[i2][0:NC1, 1, :], start=False, stop=True),
                         reads=["vcmp", f"pnT{i2}"], writes=[pk])
                for h in pair:
                    ps, pk = bank[h]
                    P.op("act", lambda e, ps=ps, h=h, qt=qt: e.copy(out=ocmp[par][:, h, qt * 128:(qt + 1) * 128], in_=ps[:, 256:384]),
                         reads=[pk], writes=[f"ocmp{par}"])
            P.op("dve", lambda e: e.tensor_tensor(out=imp[:], in0=ppad[:, 0:256:4], in1=ppad[:, 1:257:4], op=ALU.add), reads=["ppad"], writes=["imp"])
            for m_ in (2, 3, 4):
                P.op("dve", lambda e, m_=m_: e.tensor_tensor(out=imp[:], in0=imp[:], in1=ppad[:, m_:m_ + 256:4], op=ALU.add), reads=["ppad"], writes=["imp"])
            w0 = 62 - 2 * it
            P.op("dve", lambda e, w0=w0: e.tensor_tensor(out=imp2[:], in0=imp[:], in1=keepw[:, w0:w0 + NSB], op=ALU.mult), reads=["imp", "keepw_s"], writes=["imp2"])
            P.op("dve", lambda e, w0=w0: e.tensor_tensor(out=imp2[:], in0=imp2[:], in1=addw[:, w0:w0 + NSB], op=ALU.add), reads=["addw_s"], writes=["imp2"])
            P.op("dve", lambda e: e.memset(imp2[:, 0:1], 1e6), writes=["imp2"])
            P.op("dve", lambda e: e.max(out=v8a[:], in_=imp2[:]), reads=["imp2"], writes=["v8a"])
            P.op("dve", lambda e: e.match_replace(out=imp[:], in_to_replace=v8a[:], in_values=imp2[:], imm_value=-3.0e38), reads=["imp2", "v8a"], writes=["imp"])
            P.op("dve", lambda e: e.max(out=v8b[:], in_=imp[:]), reads=["imp"], writes=["v8b"])
            P.op("dve", lambda e: e.tensor_scalar(out=nsel[:], in0=imp2[:], scalar1=v8b[:, 7:8], scalar2=None, op0=ALU.is_lt), reads=["imp2", "v8b"], writes=["nsel"])
            ps4, pk4 = ps_c_next()
            P.op("pe", lambda e, ps4=ps4: e.transpose(ps4[0:NSB, 0:128], nsel[:], identf[:]), reads=["nsel", "identf_s"], writes=[pk4])
            P.op("act", lambda e, ps4=ps4, qt=qt: e.copy(out=nsT[par][:, qt * 128:(qt + 1) * 128], in_=ps4[0:NSB, 0:128]), reads=[pk4], writes=[f"nsT{par}"])

    def emit_att(qb):
        par = qb % 2
        q0 = qb * 512
        qs, qk = qb_s[par], f"qb{par}"
        szs, szk = sz_s[par], f"sz{par}"
        nst, nsk = nsT[par], f"nsT{par}"
        jobs = []
        grp = 0
        for h in range(4):
            for br in (1, 2):
                kts = list(range(0, 4 * (qb + 1))) if br == 1 else list(range(max(0, 4 * qb - 4), 4 * qb + 4))
                if br == 2 and qb >= 1:
                    kts = [4 * qb - 1] + [k for k in kts if k != 4 * qb - 1]
                for ki, kt in enumerate(kts):
                    jobs.append((h, br, kt, ki, len(kts), grp))
                grp += 1
        state = {}

        def emit_S(job):
            h, br, kt, ki, nk, g = job
            ps, pk = ps_s_next()
            rel = kt - 4 * qb
            c0 = max(0, 128 * rel) if rel >= 0 else 0
            c1 = min(512, 128 * rel + 640) if br == 2 else 512
            if br == 1:
                diag = rel >= 0
                P.op("pe", lambda e: e.matmul(ps[:, c0:c1], ks_s[:, kt * 128:(kt + 1) * 128], qs[:, h, c0:c1], start=True, stop=False), reads=["ks_s", qk], writes=[pk])
                P.op("pe", lambda e: e.matmul(ps[:, c0:c1], ekt[:, kt, :], nst[:, c0:c1], start=False, stop=(not diag)), reads=["ekt_s", nsk], writes=[pk])
                if diag:
                    P.op("pe", lambda e: e.matmul(ps[:, c0:c1], identb[:], wmask[:, rel + 4, c0:c1], start=False, stop=True), reads=["identb_s", "wmask_s"], writes=[pk])
            else:
                P.op("pe", lambda e: e.matmul(ps[:, c0:c1], kw_s[:, kt * 128:(kt + 1) * 128], qs[:, h, c0:c1], start=True, stop=False), reads=["kw_s", qk], writes=[pk])
                P.op("pe", lambda e: e.matmul(ps[:, c0:c1], identb[:], wmask[:, rel + 4, c0:c1], start=False, stop=True), reads=["identb_s", "wmask_s"], writes=[pk])
            state[job] = (ps, pk, c0, c1)

        def emit_PV(job):
            h, br, kt, ki, nk, g = job
            ps, pk, c0, c1 = state.pop(job)
            pt_, ptk = pT[pti[0] % 3], f"pT{pti[0] % 3}"
            pti[0] += 1
            v_s, vname = (vs_s, "vs_s") if br == 1 else (vw_s, "vw_s")
            pso, pko = pss[PS_O[g % 2]], f"ps{PS_O[g % 2]}"
            pss_, pks = pss[PS_S[g % 2]], f"ps{PS_S[g % 2]}"
            P.op("act", lambda e: e.activation(out=pt_[:, c0:c1], in_=ps[:, c0:c1], func=AF.Exp), reads=[pk], writes=[ptk])
            P.op("pe", lambda e: e.matmul(pso[:, c0:c1], v_s[:, kt, :], pt_[:, c0:c1], start=(ki == 0), stop=(ki == nk - 1)), reads=[vname, ptk], writes=[pko])
            P.op("pe", lambda e: e.matmul(pss_[:, c0:c1], onesb[:], pt_[:, c0:c1], start=(ki == 0), stop=(ki == nk - 1)), reads=["onesb_s", ptk], writes=[pks])
            if ki == nk - 1:
                gi = br * 4 + h
                P.op("act", lambda e: e.copy(out=osb[:], in_=pso[:]), reads=[pko], writes=["osb"])
                P.op("dve", lambda e: e.reciprocal(out=rden[:], in_=pss_[:]), reads=[pks], writes=["rden"])
                P.op("dve", lambda e: e.tensor_tensor(out=wgt[:], in0=rden[:], in1=gbc[par][:, gi, :], op=ALU.mult), reads=["rden", f"gbc{par}"], writes=["wgt"])
                P.op("dve", lambda e: e.tensor_tensor(out=tt1[:], in0=osb[:], in1=wgt[:], op=ALU.mult), reads=["osb", "wgt"], writes=["tt1"])
                if br == 1:
                    P.op("dve", lambda e: e.tensor_tensor(out=acc[:, h, :], in0=tt1[:], in1=szs[:, gi, :], op=ALU.mult), reads=["tt1", szk], writes=["acc"])
                else:
                    P.op("dve", lambda e: e.tensor_tensor(out=tt1[:], in0=tt1[:], in1=szs[:, gi, :], op=ALU.mult), reads=[szk], writes=["tt1"])
                    P.op("pool", lambda e: e.tensor_tensor(out=acc[:, h, :], in0=acc[:, h, :], in1=tt1[:], op=ALU.add), reads=["tt1"], writes=["acc"])
                    ybo, ybk = yb[par], f"ybo{par}"
                    P.op("dve", lambda e: e.tensor_tensor(out=tt1[:], in0=ocmp[par][:, h, :], in1=gbc[par][:, h, :], op=ALU.mult), reads=[f"ocmp{par}", f"gbc{par}"], writes=["tt1"])
                    P.op("dve", lambda e: e.tensor_tensor(out=tt1[:], in0=tt1[:], in1=szs[:, h, :], op=ALU.mult), reads=[szk], writes=["tt1"])
                    P.op("pool", lambda e: e.tensor_tensor(out=ybo[:, h, :], in0=acc[:, h, :], in1=tt1[:], op=ALU.add), reads=["tt1", "acc"], writes=[ybk])

        emit_S(jobs[0])
        for i, job in enumerate(jobs):
            if i + 1 < len(jobs):
                emit_S(jobs[i + 1])
            emit_PV(job)
        ybo, ybk = yb[par], f"ybo{par}"
        P.op("sp", lambda e: e.dma_start(out=yT[:, :, q0:q0 + 512], in_=ybo[:]), reads=[ybk])

    emit_cmp(0)
    for qb in range(NQB):
        P.capture()
        emit_att(qb)
        sa = P.end_capture()
        sb_ = []
        if qb + 1 < NQB:
            P.capture()
            emit_cmp(qb + 1)
            sb_ = P.end_capture()
        P.replay(sa, sb_)
    return kb.finish()


def _c(a):
    return np.ascontiguousarray(a)


def scan_maps(inp, layer, uT):
    maps = []
    ident = np.eye(128, dtype=np.float32)
    for c in range(NCORES):
        g0 = 8 * c

        def pl(a):
            return _c(a.reshape(4, 2, 64).transpose(1, 2, 0).reshape(128, 4))

        lr = pl(inp["a_lam_re"][layer][g0:g0 + 8])
        li = pl(inp["a_lam_im"][layer][g0:g0 + 8])
        ldt = pl(np.repeat(inp["a_log_dt"][layer][g0:g0 + 8][:, None], 64, axis=1))
        br = _c(inp["a_b_re"][layer][g0:g0 + 8].reshape(4, 2, 64, 16).transpose(1, 2, 0, 3).reshape(128, 4, 16))
        bi = _c(inp["a_b_im"][layer][g0:g0 + 8].reshape(4, 2, 64, 16).transpose(1, 2, 0, 3).reshape(128, 4, 16))
        crT = _c(inp["a_c_re"][layer][g0:g0 + 8].reshape(4, 2, 16, 64).transpose(1, 3, 0, 2).reshape(128, 4, 16))
        ciT = _c(inp["a_c_im"][layer][g0:g0 + 8].reshape(4, 2, 16, 64).transpose(1, 3, 0, 2).reshape(128, 4, 16))
        dv = _c(inp["a_d"][layer][128 * c:128 * c + 128].reshape(128, 1))
        maps.append(dict(uT=_c(uT[128 * c:128 * c + 128]), lr=lr, li=li, ldt=ldt, br=br, bi=bi, crT=crT, ciT=ciT,
                         dvec=dv, ident=ident))
    return maps


def attn_maps(o16, o32, kvT, inp, consts):
    maps = []
    for c in range(NCORES):
        b, G = c // 4, c % 4
        ts = slice(b * SEQ, (b + 1) * SEQ)
        m = dict(consts)
        m["qT"] = _c(o16[512 * G:512 * G + 512, ts].reshape(4, 128, SEQ).transpose(1, 0, 2))
        szr = np.stack([o16[2048 + br * 2048 + 512 * G:2048 + br * 2048 + 512 * G + 512, ts].reshape(4, 128, SEQ)
                        for br in range(3)], 0)
        m["szT"] = _c(szr.transpose(2, 0, 1, 3).reshape(128, 12, SEQ))
        m["gateT"] = _c(np.stack([o32[br * 16 + 4 * G:br * 16 + 4 * G + 4, ts] for br in range(3)], 0).reshape(12, SEQ))

        def slot(s_):
            return kvT[s_ * 512 + G * 128:s_ * 512 + G * 128 + 128, ts]

        m["kcT"] = _c(slot(0))
        m["vcT"] = _c(slot(1))
        m["ksT"] = _c(slot(2))
        m["kwT"] = _c(slot(4))
        m["vs"] = _c(slot(3).T.reshape(SEQ // 128, 128, 128).transpose(1, 0, 2))
        m["vw"] = _c(slot(5).T.reshape(SEQ // 128, 128, 128).transpose(1, 0, 2))
        m["poskT"] = _c(inp["cmp_pos_k"].T)
        m["posvT"] = _c(inp["cmp_pos_v"].T)
        m["w1k"] = _c(inp["cmp_w1_k"].reshape(32, 128, 128).transpose(1, 0, 2))
        m["w1v"] = _c(inp["cmp_w1_v"].reshape(32, 128, 128).transpose(1, 0, 2))
        m["w2k"] = _c(inp["cmp_w2_k"])
        m["w2v"] = _c(inp["cmp_w2_v"])
        maps.append(m)
    return maps


def kernel(**inputs):
    inp = {k: np.asarray(v) for k, v in inputs.items()}
    T = BATCH * SEQ // NCORES
    x = _c(inp["x"].reshape(BATCH * SEQ, D_MODEL).astype(np.float32))

    def tok(a, c):
        return a[c * T:(c + 1) * T]

    def bc(v):
        return _c(np.broadcast_to(v, (128, D_MODEL)))

    tiles_a = [(128, "id", "o32")] * 8 + [(128, "silu", "o32")] * 8
    tiles_kv = [(128, "id", "o16")] * 24
    tiles_b = [(128, "scale", "o16")] * 16 + [(128, "silu", "o16")] * 48 + [(48, "sigmoid", "o32")]
    nc_proj_a = build_proj(16, T, tiles_a)
    nc_scan = build_scan()
    nc_tail_a = build_tail(8, T, True)
    for layer in range(2):
        wa = proj_host_w(inp["a_w_in"][layer], tiles_a)
        res = run(nc_proj_a, [dict(xT=host_fm(tok(x, c).T), w=wa) for c in range(NCORES)])
        o = np.concatenate([res[c]["o32"] for c in range(NCORES)], axis=1)
        uT, szT = o[:1024], o[1024:]
        res = run(nc_scan, scan_maps(inp, layer, uT))
        gT = np.concatenate([res[c]["gT"] for c in range(NCORES)], axis=0)
        bglu = _c(inp["a_b_glu"][layer].reshape(8, 128).T)
        res = run(nc_tail_a, [dict(x=_c(tok(x, c)), w=host_fm(inp["a_w_out"][layer]), lng=bc(inp["ln_g"][layer]), lnb=bc(inp["ln_b"][layer]),
                                   gT=host_fm(gT[:, c * T:(c + 1) * T]), szT=host_fm(szT[:, c * T:(c + 1) * T]),
                                   wglu=host_fm(inp["a_w_glu"][layer]), bglu=bglu) for c in range(NCORES)])
        x = np.concatenate([res[c]["xo"] for c in range(NCORES)], axis=0)
    tiles_b_kv = tiles_b + tiles_kv
    nc_proj_b_kv = build_proj(16, T, tiles_b_kv)
    nc_proj_b = build_proj(16, T, tiles_b)
    nc_attn = build_attn()
    nc_tail_b = build_tail(16, T, False, BF16)
    consts = attn_consts()
    for li in range(2):
        layer = 2 + li
        if li == 0:
            wbi = proj_host_w(np.concatenate([inp["b_w_in"][li], inp["kv_w"]], axis=1), tiles_b_kv)
            res = run(nc_proj_b_kv, [dict(xT=host_fm(tok(x, c).T), w=wbi) for c in range(NCORES)])
            o16 = np.concatenate([res[c]["o16"] for c in range(NCORES)], axis=1)
            kvT = o16[8192:]
            o16 = o16[:8192]
        else:
            wbi = proj_host_w(inp["b_w_in"][li], tiles_b)
            res = run(nc_proj_b, [dict(xT=host_fm(tok(x, c).T), w=wbi) for c in range(NCORES)])
            o16 = np.concatenate([res[c]["o16"] for c in range(NCORES)], axis=1)
        o32 = np.concatenate([res[c]["o32"] for c in range(NCORES)], axis=1)
        res = run(nc_attn, attn_maps(o16, o32, kvT, inp, consts))
        yT = np.concatenate([np.concatenate([res[b * 4 + G]["yT"].transpose(1, 0, 2).reshape(512, SEQ) for G in range(4)], axis=0)
                             for b in range(BATCH)], axis=1)
        res = run(nc_tail_b, [dict(x=_c(tok(x, c)), w=host_fm(inp["b_w_out"][li]), lng=bc(inp["ln_g"][layer]), lnb=bc(inp["ln_b"][layer]),
                                   yT=host_fm(yT[:, c * T:(c + 1) * T])) for c in range(NCORES)])
        x = np.concatenate([res[c]["xo"] for c in range(NCORES)], axis=0)
    return x.reshape(BATCH, SEQ, D_MODEL).astype(np.float32)
```

```python
import contextlib
import numpy as np
import ml_dtypes
import concourse.bass as bass
import concourse.mybir as mybir
from concourse.bass_utils import run_bass_kernel_spmd

F32 = mybir.dt.float32
BF16 = mybir.dt.bfloat16
AF = mybir.ActivationFunctionType
ALU = mybir.AluOpType
NPBF = ml_dtypes.bfloat16

D_MODEL = 2048
BATCH = 2
SEQ = 4096
NCORES = 8
LN_EPS = 1e-5
ALPHA = 8 ** 0.25
SCALE = 128 ** -0.5
NEG = -30000.0


class Prog:
    COMPUTE = ["pe", "act", "dve", "pool"]
    QUEUES = {"sp": "sp", "qa": "act", "qp": "pool"}
    STREAMS = ["pe", "act", "dve", "pool", "sp"]
    NSLOT = 6

    def __init__(self, nc, same_sync=True):
        self.nc = nc
        self.owners = list(self.COMPUTE) + [f"{q}{j}" for q in self.QUEUES for j in range(self.NSLOT)]
        self.stream = {e: e for e in self.COMPUTE}
        for q, st in self.QUEUES.items():
            for j in range(self.NSLOT):
                self.stream[f"{q}{j}"] = st
        self.inc = {e: (1 if e in self.COMPUTE else 16) for e in self.owners}
        self.q = {e: [] for e in self.STREAMS}
        self.cnt = {e: 0 for e in self.owners}
        self.seen = {st: {o: 0 for o in self.owners} for st in self.STREAMS}
        self.qn = {q: 0 for q in self.QUEUES}
        self.lastw = {}
        self.reads = {}
        self.same_sync = same_sync
        self.seq = 0
        self.cap = None

    def _need(self, eng, other, idx):
        st = self.stream[eng]
        if other == eng and eng == "pe":
            return
        if other == eng and not self.same_sync and eng in self.COMPUTE:
            return
        if self.seen[st][other] >= idx:
            return
        self.seen[st][other] = idx
        self.seq += 1
        self.q[st].append(("wait", other, idx, self.seq))

    def barrier(self):
        rep = {"pe": "pe", "act": "act", "dve": "dve", "pool": "pool", "sp": "sp0"}
        for st in self.STREAMS:
            for o in self.owners:
                if self.cnt[o] > 0:
                    self._need(rep[st], o, self.cnt[o])

    def capture(self):
        self.cap = []
        return self.cap

    def end_capture(self):
        c, self.cap = self.cap, None
        return c

    def replay(self, *streams):
        pos = [0] * len(streams)
        tot = [max(1, len(x)) for x in streams]
        n = sum(len(x) for x in streams)
        for _ in range(n):
            best, bi = None, None
            for i, x in enumerate(streams):
                if pos[i] < len(x):
                    f = pos[i] / tot[i]
                    if best is None or f < best:
                        best, bi = f, i
            a = streams[bi][pos[bi]]
            pos[bi] += 1
            self.op(*a)

    def op(self, eng, fn, reads=(), writes=()):
        if self.cap is not None:
            self.cap.append((eng, fn, tuple(reads), tuple(writes)))
            return None
        if eng in self.QUEUES:
            j = self.qn[eng] % self.NSLOT
            self.qn[eng] += 1
            eng = f"{eng}{j}"
            if self.cnt[eng] > 0:
                self._need(eng, eng, self.cnt[eng])
        for k in reads:
            w = self.lastw.get(k)
            if w is not None:
                self._need(eng, *w)
        for k in writes:
            w = self.lastw.get(k)
            if w is not None:
                self._need(eng, *w)
            for (o, i) in self.reads.get(k, {}).items():
                self._need(eng, o, i)
        self.cnt[eng] += 1
        idx = self.cnt[eng]
        self.seq += 1
        self.q[self.stream[eng]].append(("op", fn, idx, self.seq, eng))
        for k in reads:
            self.reads.setdefault(k, {})[eng] = idx
        for k in writes:
            self.lastw[k] = (eng, idx)
            self.reads[k] = {}
        return idx

    def emit(self):
        nc = self.nc
        with contextlib.ExitStack() as st:
            sems = {e: st.enter_context(nc.semaphore("sem_" + e)) for e in self.owners}
            block = st.enter_context(nc.Block())
            engobj = {"pe": block.tensor, "act": block.scalar, "dve": block.vector,
                      "pool": block.gpsimd, "sp": block.sync}
            for e in self.STREAMS:
                items = self.q[e]

                def body(engine, items=items, e=e):
                    for it in items:
                        if it[0] == "wait":
                            _, o, idx, _s = it
                            engine.wait_ge(sems[o], idx * self.inc[o])
                        else:
                            _, fn, idx, _s, owner = it
                            ins = fn(engine)
                            ins.then_inc(sems[owner], self.inc[owner])
                    if e == "sp":
                        for o in self.owners:
                            if self.cnt[o] > 0:
                                engine.wait_ge(sems[o], self.cnt[o] * self.inc[o])
                engobj[e](body)


class KB:
    def __init__(self, name="k"):
        self.nc = bass.Bass("TRN2", target_bir_lowering=False)
        self.st = contextlib.ExitStack()
        self.P = Prog(self.nc)
        self.nps = 0
        self.uid = 0

    def dram_in(self, name, shape, dt=F32):
        return self.nc.dram_tensor(name, list(shape), dt, kind="ExternalInput").ap()

    def dram_out(self, name, shape, dt=F32):
        return self.nc.dram_tensor(name, list(shape), dt, kind="ExternalOutput").ap()

    def sb(self, name, shape, dt=F32):
        return self.st.enter_context(self.nc.sbuf_tensor(name, list(shape), dt))

    def ps(self, name, shape, dt=F32):
        return self.st.enter_context(self.nc.psum_tensor(name, list(shape), dt))

    def finish(self):
        self.P.emit()
        self.st.close()
        return self.nc


def run(nc, in_maps):
    res = run_bass_kernel_spmd(nc, in_maps, core_ids=list(range(len(in_maps))))
    return res.results


def act_apply(e, out, in_, act):
    if act == "id":
        return e.copy(out=out, in_=in_)
    if act == "scale":
        return e.mul(out=out, in_=in_, mul=SCALE)
    if act == "silu":
        return e.activation(out=out, in_=in_, func=AF.Silu)
    if act == "sigmoid":
        return e.activation(out=out, in_=in_, func=AF.Sigmoid)
    raise ValueError(act)


def proj_blocks(tiles, WB=512):
    blocks, cur, curw = [], [], 0
    for t in tiles:
        if curw + t[0] > WB:
            blocks.append(cur)
            cur, curw = [], 0
        cur.append(t)
        curw += t[0]
    if cur:
        blocks.append(cur)
    return blocks


def proj_host_w(w, tiles, WB=512):
    K_, N = w.shape
    KC = K_ // 128
    blocks = proj_blocks(tiles, WB)
    out = np.zeros((len(blocks), 128, KC, WB), np.float32)
    c0 = 0
    for bi, tl in enumerate(blocks):
        bw = sum(t[0] for t in tl)
        out[bi, :, :, :bw] = w[:, c0:c0 + bw].reshape(KC, 128, bw).transpose(1, 0, 2)
        c0 += bw
    return out


def host_fm(aT):
    K_, T = aT.shape
    return np.ascontiguousarray(aT.reshape(K_ // 128, 128, T).transpose(1, 0, 2))


def build_proj(KC, T, tiles):
    kb = KB()
    P = kb.P
    WB = 512
    blocks = proj_blocks(tiles, WB)
    n32 = sum(t[0] for t in tiles if t[2] == "o32")
    n16 = sum(t[0] for t in tiles if t[2] == "o16")
    xT = kb.dram_in("xT", [128, KC, T])
    w = kb.dram_in("w", [len(blocks), 128, KC, WB])
    outs = {}
    if n32:
        outs["o32"] = kb.dram_out("o32", [n32, T], F32)
    if n16:
        outs["o16"] = kb.dram_out("o16", [n16, T], BF16)
    xb = kb.sb("xb", [128, KC, T], BF16)
    KG = 4
    LQ = ["sp", "qa", "qp"]
    xst = [kb.sb(f"xst{i}", [128, KG, T], F32) for i in range(2)]
    wst = [kb.sb(f"wst{i}", [128, KC, WB], F32) for i in range(2)]
    wb = [kb.sb(f"wb{i}", [128, KC, WB], BF16) for i in range(2)]
    o32s = [kb.sb(f"o32s{i}", [128, 512], F32) for i in range(2)]
    o16s = [kb.sb(f"o16s{i}", [128, 512], BF16) for i in range(2)]
    pss = [kb.ps(f"ps{i}", [128, 512], F32) for i in range(8)]
    H = KC // 2

    def load_w(bi):
        bw = sum(t[0] for t in blocks[bi])
        ws, wbb = wst[bi % 2], wb[bi % 2]
        P.op(LQ[(2 * bi) % 3], lambda e: e.dma_start(out=ws[:, 0:H, 0:bw], in_=w[bi, :, 0:H, 0:bw]), writes=[f"wst{bi % 2}a"])
        P.op(LQ[(2 * bi + 1) % 3], lambda e: e.dma_start(out=ws[:, H:KC, 0:bw], in_=w[bi, :, H:KC, 0:bw]), writes=[f"wst{bi % 2}b"])
        P.op("dve", lambda e: e.tensor_copy(out=wbb[:, 0:H, 0:bw], in_=ws[:, 0:H, 0:bw]), reads=[f"wst{bi % 2}a"], writes=[f"wb{bi % 2}a"])
        P.op("act", lambda e: e.copy(out=wbb[:, H:KC, 0:bw], in_=ws[:, H:KC, 0:bw]), reads=[f"wst{bi % 2}b"], writes=[f"wb{bi % 2}b"])

    for i in range(KC // KG):
        s_ = xst[i % 2]
        P.op(LQ[i % 3], lambda e, s_=s_, i=i: e.dma_start(out=s_[:], in_=xT[:, i * KG:(i + 1) * KG, :]), writes=[f"xst{i % 2}"])
        if i % 2 == 0:
            P.op("dve", lambda e, s_=s_, i=i: e.tensor_copy(out=xb[:, i * KG:(i + 1) * KG, :], in_=s_[:]),
                 reads=[f"xst{i % 2}"], writes=[f"xb{i}"])
        else:
            P.op("act", lambda e, s_=s_, i=i: e.copy(out=xb[:, i * KG:(i + 1) * KG, :], in_=s_[:]),
                 reads=[f"xst{i % 2}"], writes=[f"xb{i}"])
    load_w(0)
    rowpos = {"o32": 0, "o16": 0}
    psi = 0
    oi = {"o32": 0, "o16": 0}
    for bi, tl in enumerate(blocks):
        if bi + 1 < len(blocks):
            load_w(bi + 1)
        wbb = wb[bi % 2]
        off = 0
        for (nc_, act, oname) in tl:
            for n in range(T // 512):
                ps = pss[psi % 8]
                pk = f"ps{psi % 8}"
                psi += 1
                for k in range(KC):
                    P.op("pe", lambda e, ps=ps, wbb=wbb, off=off, nc_=nc_, k=k, n=n: e.matmul(
                        ps[0:nc_, :], wbb[:, k, off:off + nc_], xb[:, k, n * 512:(n + 1) * 512],
                        start=(k == 0), stop=(k == KC - 1)),
                        reads=[f"wb{bi % 2}" + ("a" if k < H else "b"), f"xb{k // KG}"], writes=[pk])
                j = oi[oname] % 2
                oi[oname] += 1
                osb = (o32s if oname == "o32" else o16s)[j]
                ok = f"{oname}s{j}"
                P.op("act", lambda e, osb=osb, ps=ps, nc_=nc_, act=act: act_apply(e, osb[0:nc_, :], ps[0:nc_, :], act),
                     reads=[pk], writes=[ok])
                r0 = rowpos[oname]
                od = outs[oname]
                P.op("sp", lambda e, od=od, osb=osb, r0=r0, nc_=nc_, n=n: e.dma_start(
                    out=od[r0:r0 + nc_, n * 512:(n + 1) * 512], in_=osb[0:nc_, :]),
                    reads=[ok])
            rowpos[oname] += nc_
            off += nc_
    return kb.finish()


def build_tail(KC, T, glu, ydt=F32):
    kb = KB()
    P = kb.P
    x = kb.dram_in("x", [T, D_MODEL])
    w_v = kb.dram_in("w", [128, KC, D_MODEL])
    lng = kb.dram_in("lng", [128, D_MODEL])
    lnb = kb.dram_in("lnb", [128, D_MODEL])
    xo = kb.dram_out("xo", [T, D_MODEL])
    wb = kb.sb("wb", [128, KC, D_MODEL], BF16)
    yb = kb.sb("yb", [128, KC, T], BF16)
    STF = KC * 256
    stg = [kb.sb(f"stg{i}", [128, STF], F32) for i in range(2)]
    lng_s = kb.sb("lng_s", [128, D_MODEL], F32)
    lnb_s = kb.sb("lnb_s", [128, D_MODEL], F32)
    xt = [kb.sb(f"xt{i}", [128, D_MODEL], F32) for i in range(2)]
    stats = kb.sb("stats", [128, 4, 6], F32)
    mv = kb.sb("mv", [128, 2], F32)
    rstd = kb.sb("rstd", [128, 1], F32)
    epsc = kb.sb("epsc", [128, 1], F32)
    pss = [kb.ps(f"ps{i}", [128, 512], F32) for i in range(8)]
    psi = [0]

    def nextps():
        i = psi[0] % 8
        psi[0] += 1
        return pss[i], f"ps{i}"

    P.op("sp", lambda e: e.dma_start(out=lng_s[:], in_=lng[:, :]), writes=["lng"])
    P.op("sp", lambda e: e.dma_start(out=lnb_s[:], in_=lnb[:, :]), writes=["lnb"])
    P.op("pool", lambda e: e.memset(epsc[:], LN_EPS), writes=["epsc"])
    si = [0]

    def load_cast(dst, dkey, src_v, ncols, engs=("dve", "act")):
        kc = src_v.shape[1]
        kg = max(1, STF // ncols)
        for k0 in range(0, kc, kg):
            kn = min(kg, kc - k0)
            j = si[0] % 2
            q = ("sp", "qa", "qp")[si[0] % 3]
            si[0] += 1
            sv = stg[j][:, 0:kn * ncols].rearrange("p (k n) -> p k n", k=kn)
            P.op(q, lambda e, sv=sv, k0=k0, kn=kn: e.dma_start(out=sv, in_=src_v[:, k0:k0 + kn, :]), writes=[f"stg{j}"])
            wkeys = [f"{dkey}{kk}" for kk in range(k0, k0 + kn)]
            if engs[j % len(engs)] == "act":
                P.op("act", lambda e, sv=sv, k0=k0, kn=kn: e.copy(out=dst[:, k0:k0 + kn, 0:ncols], in_=sv), reads=[f"stg{j}"], writes=wkeys)
            else:
                P.op(engs[j % len(engs)], lambda e, sv=sv, k0=k0, kn=kn: e.tensor_copy(out=dst[:, k0:k0 + kn, 0:ncols], in_=sv),
                     reads=[f"stg{j}"], writes=wkeys)

    if glu:
        gT_v = kb.dram_in("gT", [128, KC, T])
        szT_v = kb.dram_in("szT", [128, KC, T])
        wglu_v = kb.dram_in("wglu", [128, KC, KC * 128])
        bglu = kb.dram_in("bglu", [128, KC])
        wgb = kb.sb("wgb", [128, KC, KC * 128], BF16)
        bg = kb.sb("bg", [128, KC], F32)
        g32 = kb.sb("g32", [128, KC, 512], F32)
        sz32 = kb.sb("sz32", [128, KC, 512], F32)
        gb = kb.sb("gb", [128, KC, 512], BF16)
        sig = [kb.sb(f"sig{i}", [128, 512], F32) for i in range(2)]
        P.op("sp", lambda e: e.dma_start(out=bg[:], in_=bglu[:, :]), writes=["bg"])
        load_cast(wgb, "wgb", wglu_v, KC * 128)
        for n in range(T // 512):
            P.op("sp", lambda e, n=n: e.dma_start(out=g32[:], in_=gT_v[:, :, n * 512:(n + 1) * 512]), writes=["g32"])
            P.op("qa", lambda e, n=n: e.dma_start(out=sz32[:], in_=szT_v[:, :, n * 512:(n + 1) * 512]), writes=["sz32"])
            P.op("act", lambda e: e.copy(out=gb[:], in_=g32[:]), reads=["g32"], writes=["gb"])
            for m in range(KC):
                ps, pk = nextps()
                for k in range(KC):
                    P.op("pe", lambda e, ps=ps, k=k, m=m: e.matmul(ps[:], wgb[:, k, m * 128:(m + 1) * 128], gb[:, k, :],
                                                                   start=(k == 0), stop=(k == KC - 1)),
                         reads=[f"wgb{k}", "gb"], writes=[pk])
                sg = sig[m % 2]
                P.op("act", lambda e, sg=sg, ps=ps, m=m: e.activation(out=sg[:], in_=ps[:], func=AF.Sigmoid, bias=bg[:, m:m + 1]),
                     reads=[pk, "bg"], writes=[f"sig{m % 2}"])
                P.op("dve", lambda e, sg=sg, m=m: e.tensor_tensor(out=sg[:], in0=sg[:], in1=g32[:, m, :], op=ALU.mult),
                     reads=["g32"], writes=[f"sig{m % 2}"])
                P.op("dve", lambda e, sg=sg, m=m, n=n: e.tensor_tensor(out=yb[:, m, n * 512:(n + 1) * 512], in0=sg[:], in1=sz32[:, m, :], op=ALU.mult),
                     reads=["sz32", f"sig{m % 2}"], writes=["yb"])
    else:
        yT_v = kb.dram_in("yT", [128, KC, T], ydt)
        if ydt == BF16:
            P.op("sp", lambda e: e.dma_start(out=yb[:, 0:KC // 2, :], in_=yT_v[:, 0:KC // 2, :]), writes=["yb"])
            P.op("qa", lambda e: e.dma_start(out=yb[:, KC // 2:KC, :], in_=yT_v[:, KC // 2:KC, :]), writes=["yb"])
        else:
            load_cast(yb, "yb", yT_v, T)
    load_cast(wb, "wb", w_v, D_MODEL)
    x_v = x.rearrange("(n p) d -> n p d", p=128)
    xo_v = xo.rearrange("(n p) d -> n p d", p=128)
    for tt in range(T // 128):
        xs = xt[tt % 2]
        xk = f"xt{tt % 2}"
        P.op("qa" if tt % 2 else "sp", lambda e, xs=xs, tt=tt: e.dma_start(out=xs[:], in_=x_v[tt]), writes=[xk])
        for n in range(4):
            ps, pk = nextps()
            for k in range(KC):
                P.op("pe", lambda e, ps=ps, k=k, tt=tt, n=n: e.matmul(ps[:], yb[:, k, tt * 128:(tt + 1) * 128], wb[:, k, n * 512:(n + 1) * 512],
                                                                      start=(k == 0), stop=(k == KC - 1)),
                     reads=["yb", f"wb{k}"], writes=[pk])
            P.op("dve", lambda e, xs=xs, ps=ps, n=n: e.scalar_tensor_tensor(out=xs[:, n * 512:(n + 1) * 512], in0=xs[:, n * 512:(n + 1) * 512], scalar=ALPHA,
                                                                            in1=ps[:], op0=ALU.mult, op1=ALU.add),
                 reads=[pk], writes=[xk])
            P.op("dve", lambda e, xs=xs, n=n: e.bn_stats(out=stats[:, n, :], in_=xs[:, n * 512:(n + 1) * 512]),
                 reads=[xk], writes=["stats"])
        P.op("dve", lambda e: e.bn_aggr(out=mv[:], in_=stats[:]), reads=["stats"], writes=["mv"])
        P.op("act", lambda e: e.activation(out=rstd[:], in_=mv[:, 1:2], func=AF.Sqrt, bias=epsc[:]), reads=["mv", "epsc"], writes=["rstd"])
        P.op("dve", lambda e: e.reciprocal(out=rstd[:], in_=rstd[:]), reads=["rstd"], writes=["rstd"])
        P.op("dve", lambda e, xs=xs: e.tensor_scalar(out=xs[:], in0=xs[:], scalar1=mv[:, 0:1], scalar2=rstd[:], op0=ALU.subtract, op1=ALU.mult),
             reads=["mv", "rstd"], writes=[xk])
        P.op("pool", lambda e, xs=xs: e.tensor_tensor(out=xs[:], in0=xs[:], in1=lng_s[:], op=ALU.mult), reads=["lng"], writes=[xk])
        P.op("pool", lambda e, xs=xs: e.tensor_tensor(out=xs[:], in0=xs[:], in1=lnb_s[:], op=ALU.add), reads=["lnb"], writes=[xk])
        P.op("sp", lambda e, xs=xs, tt=tt: e.dma_start(out=xo_v[tt], in_=xs[:]), reads=[xk])
    return kb.finish()


def build_scan(L=4096, NB=2):
    kb = KB()
    P = kb.P
    nc = kb.nc
    NT = L * NB
    NLEV = int(np.log2(L))
    CH = 1024
    NCH = L // CH
    uT = kb.dram_in("uT", [128, NT])
    lr_d = kb.dram_in("lr", [128, 4])
    li_d = kb.dram_in("li", [128, 4])
    ldt_d = kb.dram_in("ldt", [128, 4])
    br_d = kb.dram_in("br", [128, 4, 16])
    bi_d = kb.dram_in("bi", [128, 4, 16])
    cr_d = kb.dram_in("crT", [128, 4, 16])
    ci_d = kb.dram_in("ciT", [128, 4, 16])
    d_d = kb.dram_in("dvec", [128, 1])
    id_d = kb.dram_in("ident", [128, 128])
    gT = kb.dram_out("gT", [128, NT])

    ub = kb.sb("ub", [128, NT], BF16)
    yd = kb.sb("yd", [128, NT], F32)
    ident = kb.sb("ident_s", [128, 128], F32)
    sm = {n: kb.sb("sm_" + n, [128, 4], F32) for n in ["lr", "li", "dt", "th", "mag", "c", "s", "t1", "t2", "t3", "ar", "ai",
                                                       "den", "am1", "cr", "ci", "nci"]}
    br = kb.sb("br_s", [128, 4, 16], F32)
    bi = kb.sb("bi_s", [128, 4, 16], F32)
    crT = kb.sb("crT_s", [128, 4, 16], F32)
    ciT = kb.sb("ciT_s", [128, 4, 16], F32)
    bbr = kb.sb("bbr", [128, 4, 16], F32)
    bbi = kb.sb("bbi", [128, 4, 16], F32)
    tmpb = kb.sb("tmpb", [128, 4, 16], F32)
    Bblk = {n: kb.sb(n, [128, 4, 32], F32) for n in ["Bblk_r", "Bblk_i"]}
    BT = {n: kb.sb(n, [128, 128], BF16) for n in ["BT_r", "BT_i"]}
    Cpad = {n: kb.sb(n, [128, 4, 128], F32) for n in ["Cpad_r", "Cpad_ni"]}
    pc = kb.sb("pc", [128, NLEV, 4], F32)
    psn = kb.sb("psn", [128, NLEV, 4], F32)
    pns = kb.sb("pns", [128, NLEV, 4], F32)
    dv = kb.sb("dv", [128, 1], F32)
    halfpi = kb.sb("halfpi", [128, 1], F32)
    mtop = kb.sb("mtop", [128, 1], F32)
    mbot = kb.sb("mbot", [128, 1], F32)
    nmtop = kb.sb("nmtop", [128, 1], F32)
    nmbot = kb.sb("nmbot", [128, 1], F32)
    gt = [kb.sb(f"gt{i}", [128, 512], F32) for i in range(2)]
    gs = [kb.sb(f"gs{i}", [128, 512], F32) for i in range(2)]
    st0 = contextlib.ExitStack()
    u32 = st0.enter_context(nc.sbuf_tensor("u32", [128, NT], F32))
    pss = [kb.ps(f"ps{i}", [128, 512], F32) for i in range(8)]
    psi = [0]

    def nextps():
        i = psi[0] % 8
        psi[0] += 1
        return pss[i], f"ps{i}"

    def dma_in(dst, src, key, q="sp"):
        P.op(q, lambda e: e.dma_start(out=dst, in_=src), writes=[key])

    for q in range(4):
        P.op(("sp", "qa")[q % 2], lambda e, q=q: e.dma_start(out=u32[:, q * NT // 4:(q + 1) * NT // 4], in_=uT[:, q * NT // 4:(q + 1) * NT // 4]), writes=[f"u32_{q}"])
        P.op("dve", lambda e, q=q: e.tensor_copy(out=ub[:, q * NT // 4:(q + 1) * NT // 4], in_=u32[:, q * NT // 4:(q + 1) * NT // 4]), reads=[f"u32_{q}"], writes=[f"ub_{q}"])
    UBK = [f"ub_{q}" for q in range(4)]
    dma_in(sm["lr"][:], lr_d[:, :], "lr")
    dma_in(sm["li"][:], li_d[:, :], "li")
    dma_in(sm["dt"][:], ldt_d[:, :], "dt")
    dma_in(br[:], br_d[:, :, :], "br")
    dma_in(bi[:], bi_d[:, :, :], "bi")
    dma_in(crT[:], cr_d[:, :, :], "crT")
    dma_in(ciT[:], ci_d[:, :, :], "ciT")
    dma_in(dv[:], d_d[:, :], "dv")
    dma_in(ident[:], id_d[:, :], "ident")
    P.op("pool", lambda e: e.memset(halfpi[:], float(np.pi / 2)), writes=["halfpi"])
    for (t_, a, b) in [(mtop, 1.0, 0.0), (mbot, 0.0, 1.0), (nmtop, -1.0, 0.0), (nmbot, 0.0, -1.0)]:
        P.op("pool", lambda e, t_=t_, a=a: e.memset(t_[0:64, :], a), writes=["masks"])
        P.op("pool", lambda e, t_=t_, b=b: e.memset(t_[64:128, :], b), writes=["masks"])
    for n in ["Bblk_r", "Bblk_i"]:
        P.op("pool", lambda e, n=n: e.memset(Bblk[n][:], 0.0), writes=[n])
    for n in ["Cpad_r", "Cpad_ni"]:
        P.op("pool", lambda e, n=n: e.memset(Cpad[n][:], 0.0), writes=[n])

    def tt(out, a, b, op, rk, wk, eng="dve"):
        P.op(eng, lambda e: e.tensor_tensor(out=out, in0=a, in1=b, op=op), reads=rk, writes=wk)

    def S(n):
        return sm[n][:]

    def renorm(c_ap, s_ap, ck, sk):
        tt(S("t1"), c_ap, c_ap, ALU.mult, [ck], ["t1"])
        tt(S("t2"), s_ap, s_ap, ALU.mult, [sk], ["t2"])
        tt(S("t1"), S("t1"), S("t2"), ALU.add, ["t2"], ["t1"])
        P.op("dve", lambda e: e.tensor_scalar(out=S("t1"), in0=S("t1"), scalar1=-0.5, scalar2=1.5, op0=ALU.mult, op1=ALU.add), writes=["t1"])
        tt(c_ap, c_ap, S("t1"), ALU.mult, ["t1"], [ck])
        tt(s_ap, s_ap, S("t1"), ALU.mult, ["t1"], [sk])

    P.op("act", lambda e: e.activation(out=S("dt"), in_=S("dt"), func=AF.Exp), reads=["dt"], writes=["dt"])
    tt(S("th"), S("li"), S("dt"), ALU.mult, ["li", "dt"], ["th"])
    tt(S("t1"), S("lr"), S("dt"), ALU.mult, ["lr", "dt"], ["t1"])
    P.op("act", lambda e: e.activation(out=S("mag"), in_=S("t1"), func=AF.Exp), reads=["t1"], writes=["mag"])
    P.op("act", lambda e: e.activation(out=S("s"), in_=S("th"), func=AF.Sin, scale=1.0 / 16), reads=["th"], writes=["s"])
    P.op("act", lambda e: e.activation(out=S("c"), in_=S("th"), func=AF.Sin, scale=1.0 / 16, bias=halfpi[:]), reads=["th", "halfpi"], writes=["c"])
    for _ in range(4):
        tt(S("t1"), S("c"), S("c"), ALU.mult, ["c"], ["t1"])
        tt(S("t2"), S("s"), S("s"), ALU.mult, ["s"], ["t2"])
        tt(S("t3"), S("c"), S("s"), ALU.mult, ["c", "s"], ["t3"])
        tt(S("c"), S("t1"), S("t2"), ALU.subtract, ["t1", "t2"], ["c"])
        tt(S("s"), S("t3"), S("t3"), ALU.add, ["t3"], ["s"])
    renorm(S("c"), S("s"), "c", "s")
    tt(S("ar"), S("mag"), S("c"), ALU.mult, ["mag", "c"], ["ar"])
    tt(S("ai"), S("mag"), S("s"), ALU.mult, ["mag", "s"], ["ai"])
    P.op("dve", lambda e: e.tensor_copy(out=pc[:, 0, :], in_=S("c")), reads=["c"], writes=["pc"])
    P.op("dve", lambda e: e.tensor_copy(out=psn[:, 0, :], in_=S("s")), reads=["s"], writes=["psn"])
    for lev in range(1, NLEV):
        tt(S("t1"), pc[:, lev - 1, :], pc[:, lev - 1, :], ALU.mult, ["pc"], ["t1"])
        tt(S("t2"), psn[:, lev - 1, :], psn[:, lev - 1, :], ALU.mult, ["psn"], ["t2"])
        tt(S("t3"), pc[:, lev - 1, :], psn[:, lev - 1, :], ALU.mult, ["pc", "psn"], ["t3"])
        tt(pc[:, lev, :], S("t1"), S("t2"), ALU.subtract, ["t1", "t2"], ["pc"])
        tt(psn[:, lev, :], S("t3"), S("t3"), ALU.add, ["t3"], ["psn"])
        renorm(pc[:, lev, :], psn[:, lev, :], "pc", "psn")
    P.op("dve", lambda e: e.tensor_scalar(out=pns[:], in0=psn[:], scalar1=-1.0, scalar2=None, op0=ALU.mult), reads=["psn"], writes=["pns"])
    tt(S("t1"), S("lr"), S("lr"), ALU.mult, ["lr"], ["t1"])
    tt(S("t2"), S("li"), S("li"), ALU.mult, ["li"], ["t2"])
    tt(S("den"), S("t1"), S("t2"), ALU.add, ["t1", "t2"], ["den"])
    P.op("dve", lambda e: e.reciprocal(out=S("den"), in_=S("den")), reads=["den"], writes=["den"])
    P.op("dve", lambda e: e.tensor_scalar(out=S("am1"), in0=S("ar"), scalar1=-1.0, scalar2=None, op0=ALU.add), reads=["ar"], writes=["am1"])
    tt(S("t1"), S("am1"), S("lr"), ALU.mult, ["am1", "lr"], ["t1"])
    tt(S("t2"), S("ai"), S("li"), ALU.mult, ["ai", "li"], ["t2"])
    tt(S("t3"), S("t1"), S("t2"), ALU.add, ["t1", "t2"], ["t3"])
    tt(S("cr"), S("t3"), S("den"), ALU.mult, ["t3", "den"], ["cr"])
    tt(S("t1"), S("ai"), S("lr"), ALU.mult, ["ai", "lr"], ["t1"])
    tt(S("t2"), S("am1"), S("li"), ALU.mult, ["am1", "li"], ["t2"])
    tt(S("t3"), S("t1"), S("t2"), ALU.subtract, ["t1", "t2"], ["t3"])
    tt(S("ci"), S("t3"), S("den"), ALU.mult, ["t3", "den"], ["ci"])
    P.op("dve", lambda e: e.tensor_scalar(out=S("nci"), in0=S("ci"), scalar1=-1.0, scalar2=None, op0=ALU.mult), reads=["ci"], writes=["nci"])
    for j in range(4):
        P.op("dve", lambda e, j=j: e.tensor_scalar(out=tmpb[:, j, :], in0=bi[:, j, :], scalar1=sm["nci"][:, j:j + 1], scalar2=None, op0=ALU.mult),
             reads=["bi", "nci"], writes=["tmpb"])
        P.op("dve", lambda e, j=j: e.scalar_tensor_tensor(out=bbr[:, j, :], in0=br[:, j, :], scalar=sm["cr"][:, j:j + 1], in1=tmpb[:, j, :], op0=ALU.mult, op1=ALU.add),
             reads=["br", "cr", "tmpb"], writes=["bbr"])
        P.op("dve", lambda e, j=j: e.tensor_scalar(out=tmpb[:, j, :], in0=br[:, j, :], scalar1=sm["ci"][:, j:j + 1], scalar2=None, op0=ALU.mult),
             reads=["br", "ci"], writes=["tmpb"])
        P.op("dve", lambda e, j=j: e.scalar_tensor_tensor(out=bbi[:, j, :], in0=bi[:, j, :], scalar=sm["cr"][:, j:j + 1], in1=tmpb[:, j, :], op0=ALU.mult, op1=ALU.add),
             reads=["bi", "cr", "tmpb"], writes=["bbi"])
    for (src, n) in [(bbr, "Bblk_r"), (bbi, "Bblk_i")]:
        sk = "bbr" if n == "Bblk_r" else "bbi"
        P.op("dve", lambda e, src=src, n=n: e.tensor_scalar(out=Bblk[n][:, :, 0:16], in0=src[:], scalar1=mtop[:], scalar2=None, op0=ALU.mult),
             reads=[sk, "masks"], writes=[n])
        P.op("dve", lambda e, src=src, n=n: e.tensor_scalar(out=Bblk[n][:, :, 16:32], in0=src[:], scalar1=mbot[:], scalar2=None, op0=ALU.mult),
             reads=[sk, "masks"], writes=[n])
    for (n, bn) in [("Bblk_r", "BT_r"), ("Bblk_i", "BT_i")]:
        ps, pk = nextps()
        P.op("pe", lambda e, ps=ps, n=n: e.transpose(ps[:, 0:128], Bblk[n][:].rearrange("p j c -> p (j c)"), ident[:]), reads=[n, "ident"], writes=[pk])
        P.op("act", lambda e, ps=ps, bn=bn: e.copy(out=BT[bn][:], in_=ps[:, 0:128]), reads=[pk], writes=[bn])
    for j in range(4):
        for (cn, src, ma, mb, sk) in [("Cpad_r", crT, mtop, mbot, "crT"), ("Cpad_ni", ciT, nmtop, nmbot, "ciT")]:
            P.op("dve", lambda e, j=j, cn=cn, src=src, ma=ma: e.tensor_scalar(out=Cpad[cn][:, j, 32 * j:32 * j + 16], in0=src[:, j, :], scalar1=ma[:], scalar2=None, op0=ALU.mult),
                 reads=[sk, "masks"], writes=[cn])
            P.op("dve", lambda e, j=j, cn=cn, src=src, mb=mb: e.tensor_scalar(out=Cpad[cn][:, j, 32 * j + 16:32 * j + 32], in0=src[:, j, :], scalar1=mb[:], scalar2=None, op0=ALU.mult),
                 reads=[sk, "masks"], writes=[cn])
    NBLK = L // 512
    for q in range(4):
        P.op("act", lambda e, q=q: e.mul(out=yd[:, q * NT // 4:(q + 1) * NT // 4], in_=u32[:, q * NT // 4:(q + 1) * NT // 4], mul=dv[:]),
             reads=[f"u32_{q}", "dv"], writes=[f"yd_{q}"])
    P.barrier()
    st0.close()

    def ydk(tok):
        return f"yd_{tok // (NT // 4)}"

    Ec = kb.sb("Ec", [128, L], F32)
    Es = kb.sb("Es", [128, L], F32)
    magf = kb.sb("magf", [128, L], F32)
    Xr = kb.sb("Xr", [128, L], F32)
    Xi = kb.sb("Xi", [128, L], F32)
    T1 = [kb.sb(f"T1_{i}", [128, CH], F32) for i in range(2)]
    T2 = [kb.sb(f"T2_{i}", [128, CH], F32) for i in range(2)]
    XRK = [f"Xr{c}" for c in range(NCH)]
    XIK = [f"Xi{c}" for c in range(NCH)]
    tci = [0]

    def rotate(sign):
        for c in range(NCH):
            sl = slice(c * CH, (c + 1) * CH)
            k = tci[0] % 2
            tci[0] += 1
            t1, t2 = T1[k], T2[k]
            P.op("pool", lambda e, sl=sl, t1=t1: e.tensor_tensor(out=t1[:], in0=Xi[:, sl], in1=Es[:, sl], op=ALU.mult), reads=[XIK[c], "E"], writes=[f"T1_{k}"])
            P.op("pool", lambda e, sl=sl, t2=t2: e.tensor_tensor(out=t2[:], in0=Xr[:, sl], in1=Es[:, sl], op=ALU.mult), reads=[XRK[c], "E"], writes=[f"T2_{k}"])
            P.op("dve", lambda e, sl=sl: e.tensor_tensor(out=Xr[:, sl], in0=Xr[:, sl], in1=Ec[:, sl], op=ALU.mult), reads=["E"], writes=[XRK[c]])
            P.op("dve", lambda e, sl=sl: e.tensor_tensor(out=Xi[:, sl], in0=Xi[:, sl], in1=Ec[:, sl], op=ALU.mult), reads=["E"], writes=[XIK[c]])
            P.op("dve", lambda e, sl=sl, t1=t1: e.tensor_tensor(out=Xr[:, sl], in0=Xr[:, sl], in1=t1[:], op=(ALU.add if sign < 0 else ALU.subtract)), reads=[f"T1_{k}"], writes=[XRK[c]])
            P.op("dve", lambda e, sl=sl, t2=t2: e.tensor_tensor(out=Xi[:, sl], in0=Xi[:, sl], in1=t2[:], op=(ALU.subtract if sign < 0 else ALU.add)), reads=[f"T2_{k}"], writes=[XIK[c]])

    for j in range(4):
        P.op("dve", lambda e: e.memset(Ec[:, 0:1], 1.0), writes=["E"])
        P.op("dve", lambda e: e.memset(Es[:, 0:1], 0.0), writes=["E"])
        for lev in range(NLEV):
            n = 1 << lev
            C = pc[:, lev, j:j + 1]
            Sn = psn[:, lev, j:j + 1]
            Sm = pns[:, lev, j:j + 1]
            t1 = T1[0] if n <= CH else None
            for o in range(0, n, CH):
                w_ = min(CH, n - o)
                a, b = T1[0], T2[0]
                P.op("dve", lambda e, o=o, w_=w_, a=a, Sm=Sm: e.tensor_scalar(out=a[:, 0:w_], in0=Es[:, o:o + w_], scalar1=Sm, scalar2=None, op0=ALU.mult), reads=["E", "pns"], writes=["T1_0"])
                P.op("dve", lambda e, o=o, w_=w_, b=b, Sn=Sn: e.tensor_scalar(out=b[:, 0:w_], in0=Ec[:, o:o + w_], scalar1=Sn, scalar2=None, op0=ALU.mult), reads=["E", "psn"], writes=["T2_0"])
                P.op("dve", lambda e, o=o, w_=w_, a=a, C=C, n=n: e.scalar_tensor_tensor(out=Ec[:, n + o:n + o + w_], in0=Ec[:, o:o + w_], scalar=C, in1=a[:, 0:w_], op0=ALU.mult, op1=ALU.add),
                     reads=["T1_0", "pc"], writes=["E"])
                P.op("dve", lambda e, o=o, w_=w_, b=b, C=C, n=n: e.scalar_tensor_tensor(out=Es[:, n + o:n + o + w_], in0=Es[:, o:o + w_], scalar=C, in1=b[:, 0:w_], op0=ALU.mult, op1=ALU.add),
                     reads=["T2_0", "pc"], writes=["E"])
        P.op("act", lambda e, j=j: e.activation(out=magf[:], in_=Ec[:], func=AF.Identity, scale=0.0, bias=sm["mag"][:, j:j + 1]), reads=["E", "mag"], writes=["magf"])
        for b in range(NB):
            for blk in range(NBLK):
                t0 = b * L + blk * 512
                c = (blk * 512) // CH
                for (bn, X_, xk) in [("BT_r", Xr, XRK[c]), ("BT_i", Xi, XIK[c])]:
                    ps, pk = nextps()
                    P.op("pe", lambda e, ps=ps, bn=bn, t0=t0, j=j: e.matmul(ps[:], BT[bn][32 * j:32 * j + 32, :], ub[32 * j:32 * j + 32, t0:t0 + 512],
                                                                            start=True, stop=True, tile_position=(32 * j, 0)),
                         reads=[bn] + UBK, writes=[pk])
                    P.op("act", lambda e, ps=ps, X_=X_, blk=blk: e.copy(out=X_[:, blk * 512:(blk + 1) * 512], in_=ps[:]), reads=[pk], writes=[xk])
            rotate(-1)
            P.op("dve", lambda e: e.tensor_tensor_scan(out=Xr[:], data0=magf[:], data1=Xr[:], initial=0.0, op0=ALU.mult, op1=ALU.add), reads=["magf"], writes=XRK)
            P.op("dve", lambda e: e.tensor_tensor_scan(out=Xi[:], data0=magf[:], data1=Xi[:], initial=0.0, op0=ALU.mult, op1=ALU.add), reads=["magf"], writes=XIK)
            rotate(+1)
            for blk in range(NBLK):
                t0 = b * L + blk * 512
                c = (blk * 512) // CH
                ps, pk = nextps()
                P.op("pe", lambda e, ps=ps, blk=blk, j=j: e.matmul(ps[:], Cpad["Cpad_r"][:, j, :], Xr[:, blk * 512:(blk + 1) * 512], start=True, stop=False),
                     reads=["Cpad_r", XRK[c]], writes=[pk])
                P.op("pe", lambda e, ps=ps, blk=blk, j=j: e.matmul(ps[:], Cpad["Cpad_ni"][:, j, :], Xi[:, blk * 512:(blk + 1) * 512], start=False, stop=True),
                     reads=["Cpad_ni", XIK[c]], writes=[pk])
                P.op("dve", lambda e, ps=ps, t0=t0: e.tensor_tensor(out=yd[:, t0:t0 + 512], in0=yd[:, t0:t0 + 512], in1=ps[:], op=ALU.add),
                     reads=[pk], writes=[ydk(t0)])
    for i in range(NT // 512):
        t0 = i * 512
        a, ak = gt[i % 2], f"gt{i % 2}"
        sg, sk = gs[i % 2], f"gs{i % 2}"
        P.op("act", lambda e, a=a, t0=t0: e.activation(out=a[:], in_=yd[:, t0:t0 + 512], func=AF.Square), reads=[ydk(t0)], writes=[ak])
        P.op("dve", lambda e, a=a: e.tensor_scalar(out=a[:], in0=a[:], scalar1=0.044715, scalar2=1.0, op0=ALU.mult, op1=ALU.add), reads=[], writes=[ak])
        P.op("dve", lambda e, a=a, t0=t0: e.tensor_tensor(out=a[:], in0=a[:], in1=yd[:, t0:t0 + 512], op=ALU.mult), reads=[ydk(t0)], writes=[ak])
        P.op("act", lambda e, a=a, sg=sg: e.activation(out=sg[:], in_=a[:], func=AF.Sigmoid, scale=1.5957691216057308), reads=[ak], writes=[sk])
        P.op("dve", lambda e, sg=sg, t0=t0: e.tensor_tensor(out=sg[:], in0=sg[:], in1=yd[:, t0:t0 + 512], op=ALU.mult), reads=[ydk(t0)], writes=[sk])
        P.op(("sp", "qa")[i % 2], lambda e, sg=sg, t0=t0: e.dma_start(out=gT[:, t0:t0 + 512], in_=sg[:]), reads=[sk])
    return kb.finish()


def attn_consts():
    m = np.arange(128)[:, None]
    n = np.arange(512)[None, :]
    wm = np.zeros((8, 128, 512), np.float32)
    for rel in range(-4, 0):
        wm[rel + 4] = np.where((128 * rel + m) <= n - 512, NEG, 0.0)
    for rel in range(0, 4):
        wm[rel + 4] = np.where((128 * rel + m) > n, NEG, 0.0)
    wmask = np.ascontiguousarray(wm.transpose(1, 0, 2)).astype(NPBF)
    j = np.arange(64)[:, None, None]
    kt = np.arange(32)[None, :, None]
    mm = np.arange(128)[None, None, :]
    ekt = np.where(j == 2 * kt + mm // 64, NEG, 0.0).astype(NPBF)
    r = np.arange(128)[:, None]
    jj = np.arange(503)[None, :]
    cmpw = np.where(16 * (jj - 248) + 31 <= r, 0.0, NEG).astype(np.float32)
    jj = np.arange(126)[None, :]
    rel = (jj - 62) - (r >= 64)
    keepw = np.where((rel > 0) | (rel == 0) | (rel == -1), 0.0, 1.0).astype(np.float32)
    addw = np.where(rel > 0, -1e30, np.where((rel == 0) | (rel == -1), 1e6, 0.0)).astype(np.float32)
    onehot = np.zeros((12, 12, 128), np.float32)
    for g in range(12):
        onehot[g, g, :] = 1.0
    return dict(wmask=wmask, ekt=ekt, cmpw=cmpw, keepw=keepw, addw=addw, onehot=onehot,
                identf=np.eye(128, dtype=np.float32), identb=np.eye(128).astype(NPBF),
                onesb=np.ones((128, 128)).astype(NPBF))


def build_attn(L=4096):
    kb = KB()
    P = kb.P
    nc = kb.nc
    NQB = L // 512
    NKT = L // 128
    NC = (L - 32) // 16 + 1
    NC1 = NC - 128
    NSB = L // 64
    qT = kb.dram_in("qT", [128, 4, L], BF16)
    szT = kb.dram_in("szT", [128, 12, L], BF16)
    gateT = kb.dram_in("gateT", [12, L], F32)
    kcT = kb.dram_in("kcT", [128, L], BF16)
    vcT = kb.dram_in("vcT", [128, L], BF16)
    ksT = kb.dram_in("ksT", [128, L], BF16)
    kwT = kb.dram_in("kwT", [128, L], BF16)
    vs_d = kb.dram_in("vs", [128, NKT, 128], BF16)
    vw_d = kb.dram_in("vw", [128, NKT, 128], BF16)
    posk = kb.dram_in("poskT", [128, 32], F32)
    posv = kb.dram_in("posvT", [128, 32], F32)
    w1k = kb.dram_in("w1k", [128, 32, 128], F32)
    w1v = kb.dram_in("w1v", [128, 32, 128], F32)
    w2k = kb.dram_in("w2k", [128, 128], F32)
    w2v = kb.dram_in("w2v", [128, 128], F32)
    c_wmask = kb.dram_in("wmask", [128, 8, 512], BF16)
    c_ekt = kb.dram_in("ekt", [64, 32, 128], BF16)
    c_cmpw = kb.dram_in("cmpw", [128, 503], F32)
    c_keepw = kb.dram_in("keepw", [128, 126], F32)
    c_addw = kb.dram_in("addw", [128, 126], F32)
    c_onehot = kb.dram_in("onehot", [12, 12, 128], F32)
    c_identf = kb.dram_in("identf", [128, 128], F32)
    c_identb = kb.dram_in("identb", [128, 128], BF16)
    c_onesb = kb.dram_in("onesb", [128, 128], BF16)
    yT = kb.dram_out("yT", [128, 4, L], BF16)
    lq = [0]

    def load(name, src, shape, dt, st=None):
        t = (st or kb.st).enter_context(nc.sbuf_tensor(name, list(shape), dt))
        q = ("sp", "qa")[lq[0] % 2]
        lq[0] += 1
        P.op(q, lambda e: e.dma_start(out=t[:], in_=src), writes=[name])
        return t

    st0 = contextlib.ExitStack()

    def sb0(name, shape, dt):
        return st0.enter_context(nc.sbuf_tensor(name, list(shape), dt))

    ks_s = load("ks_s", ksT[:, :], [128, L], BF16)
    kw_s = load("kw_s", kwT[:, :], [128, L], BF16)
    vs_s = load("vs_s", vs_d[:, :, :], [128, NKT, 128], BF16)
    vw_s = load("vw_s", vw_d[:, :, :], [128, NKT, 128], BF16)
    wmask = load("wmask_s", c_wmask[:, :, :], [128, 8, 512], BF16)
    ekt = load("ekt_s", c_ekt[:, :, :], [64, 32, 128], BF16)
    cmpw = load("cmpw_s", c_cmpw[:, :], [128, 503], F32)
    keepw = load("keepw_s", c_keepw[:, :], [128, 126], F32)
    addw = load("addw_s", c_addw[:, :], [128, 126], F32)
    onehot = load("onehot_s", c_onehot[:, :, :], [12, 12, 128], F32)
    identf = load("identf_s", c_identf[:, :], [128, 128], F32)
    identb = load("identb_s", c_identb[:, :], [128, 128], BF16)
    onesb = load("onesb_s", c_onesb[:, :], [128, 128], BF16)
    kcmpT = kb.sb("kcmpT", [128, NC], BF16)
    vcmp = kb.sb("vcmp", [128, 2, 128], BF16)
    kc_s = load("kc_s", kcT[:, :], [128, L], BF16, st0)
    vc_s = load("vc_s", vcT[:, :], [128, L], BF16, st0)
    posk_s = load("posk_s", posk[:, :], [128, 32], F32, st0)
    posv_s = load("posv_s", posv[:, :], [128, 32], F32, st0)
    w2k_s = load("w2k_s", w2k[:, :], [128, 128], F32, st0)
    w2v_s = load("w2v_s", w2v[:, :], [128, 128], F32, st0)
    w1st = sb0("w1st", [128, 32, 128], F32)
    w1b = sb0("w1b", [128, 32, 128], BF16)
    w2b = sb0("w2b", [128, 128], BF16)
    tmpl = [sb0(f"tmpl{i}", [128, NC], BF16) for i in range(4)]
    h1 = sb0("h1", [128, NC], F32)
    h2 = sb0("h2", [128, NC], F32)
    hg = sb0("hg", [128, NC], BF16)

    pss = [kb.ps(f"ps{i}", [128, 512], F32) for i in range(8)]
    SROT = [0, 1]
    PS_O = [2, 3]
    PS_S = [4, 5]
    PS_C = [6, 7]
    srot = [0]
    crot = [0]

    def ps_s_next():
        i = SROT[srot[0] % 2]
        srot[0] += 1
        return pss[i], f"ps{i}"

    def ps_c_next():
        i = PS_C[crot[0] % 2]
        crot[0] += 1
        return pss[i], f"ps{i}"

    for side, (src_s, pos_s, w1d, w2s) in enumerate([(kc_s, posk_s, w1k, w2k_s), (vc_s, posv_s, w1v, w2v_s)]):
        srck = "kc_s" if side == 0 else "vc_s"
        posn = "posk_s" if side == 0 else "posv_s"
        w2n = "w2k_s" if side == 0 else "w2v_s"
        P.op("sp", lambda e, w1d=w1d: e.dma_start(out=w1st[:], in_=w1d[:, :, :]), writes=["w1st"])
        P.op("pool", lambda e: e.tensor_copy(out=w1b[:], in_=w1st[:]), reads=["w1st"], writes=["w1b"])
        P.op("pool", lambda e, w2s=w2s: e.tensor_copy(out=w2b[:], in_=w2s[:]), reads=[w2n], writes=["w2b"])
        ps, pk = ps_c_next()
        for l in range(32):
            tl, tk = tmpl[l % 4], f"tmpl{l % 4}"
            P.op("dve", lambda e, tl=tl, l=l, src_s=src_s, pos_s=pos_s: e.tensor_scalar(
                out=tl[:], in0=src_s[:, l:l + 16 * (NC - 1) + 1:16], scalar1=pos_s[:, l:l + 1], scalar2=None, op0=ALU.add),
                reads=[srck, posn], writes=[tk])
            P.op("pe", lambda e, ps=ps, tl=tl, l=l: e.matmul(ps[:, 0:NC], w1b[:, l, :], tl[:], start=(l == 0), stop=(l == 31)),
                 reads=["w1b", tk], writes=[pk])
        P.op("act", lambda e, ps=ps: e.copy(out=h1[:], in_=ps[:, 0:NC]), reads=[pk], writes=["h1"])
        P.op("act", lambda e: e.activation(out=h2[:], in_=h1[:], func=AF.Square), reads=["h1"], writes=["h2"])
        P.op("dve", lambda e: e.tensor_scalar(out=h2[:], in0=h2[:], scalar1=0.044715, scalar2=1.0, op0=ALU.mult, op1=ALU.add), writes=["h2"])
        P.op("dve", lambda e: e.tensor_tensor(out=h2[:], in0=h2[:], in1=h1[:], op=ALU.mult), reads=["h1"], writes=["h2"])
        P.op("act", lambda e: e.activation(out=h2[:], in_=h2[:], func=AF.Sigmoid, scale=1.5957691216057308), writes=["h2"])
        P.op("dve", lambda e: e.tensor_tensor(out=hg[:], in0=h2[:], in1=h1[:], op=ALU.mult), reads=["h1", "h2"], writes=["hg"])
        if side == 0:
            ps, pk = ps_c_next()
            P.op("pe", lambda e, ps=ps: e.matmul(ps[:, 0:NC], w2b[:], hg[:], start=True, stop=True), reads=["w2b", "hg"], writes=[pk])
            P.op("act", lambda e, ps=ps: e.copy(out=kcmpT[:], in_=ps[:, 0:NC]), reads=[pk], writes=["kcmpT"])
        else:
            for t2, (n0, nn) in enumerate([(0, 128), (128, NC1)]):
                ps, pk = ps_c_next()
                P.op("pe", lambda e, ps=ps, n0=n0, nn=nn: e.matmul(ps[0:nn, 0:128], hg[:, n0:n0 + nn], w2b[:], start=True, stop=True),
                     reads=["w2b", "hg"], writes=[pk])
                P.op("act", lambda e, ps=ps, nn=nn, t2=t2: e.copy(out=vcmp[0:nn, t2, :], in_=ps[0:nn, 0:128]), reads=[pk], writes=["vcmp"])
    P.barrier()
    st0.close()

    qb_s = [kb.sb(f"qb{i}", [128, 4, 512], BF16) for i in range(2)]
    sz_s = [kb.sb(f"sz{i}", [128, 12, 512], BF16) for i in range(2)]
    gt_s = [kb.sb(f"gt{i}", [12, 512], F32) for i in range(2)]
    gbc = [kb.sb(f"gbc{i}", [128, 12, 512], F32) for i in range(2)]
    sc = [kb.sb(f"sc{i}", [128, NC], F32) for i in range(2)]
    pe_ = [kb.sb(f"pe_{i}", [128, NC], F32) for i in range(2)]
    pn = [kb.sb(f"pn{i}", [128, NC], F32) for i in range(2)]
    ppad = kb.sb("ppad", [128, 260], F32)
    rs = [kb.sb(f"rs{i}", [128, 1], F32) for i in range(2)]
    pnT = [kb.sb(f"pnT{i}", [128, 2, 128], BF16) for i in range(2)]
    ocmp = [kb.sb(f"ocmp{i}", [128, 4, 512], F32) for i in range(2)]
    imp = kb.sb("imp", [128, NSB], F32)
    imp2 = kb.sb("imp2", [128, NSB], F32)
    v8a = kb.sb("v8a", [128, 8], F32)
    v8b = kb.sb("v8b", [128, 8], F32)
    nsel = kb.sb("nsel", [128, NSB], F32)
    nsT = [kb.sb(f"nsT{i}", [64, 512], BF16) for i in range(2)]
    pT = [kb.sb(f"pT{i}", [128, 512], BF16) for i in range(3)]
    osb = kb.sb("osb", [128, 512], F32)
    rden = kb.sb("rden", [128, 512], F32)
    wgt = kb.sb("wgt", [128, 512], F32)
    tt1 = kb.sb("tt1", [128, 512], F32)
    acc = kb.sb("acc", [128, 4, 512], F32)
    yb = [kb.sb(f"ybo{i}", [128, 4, 512], BF16) for i in range(2)]
    P.op("pool", lambda e: e.memset(ppad[:], 0.0), writes=["ppad"])
    pti = [0]

    def emit_cmp(qb):
        par = qb % 2
        q0 = qb * 512
        qs, qk = qb_s[par], f"qb{par}"
        szs, szk = sz_s[par], f"sz{par}"
        gts, gk = gt_s[par], f"gt{par}"
        P.op("sp", lambda e: e.dma_start(out=qs[:], in_=qT[:, :, q0:q0 + 512]), writes=[qk])
        P.op("qa", lambda e: e.dma_start(out=szs[:], in_=szT[:, :, q0:q0 + 512]), writes=[szk])
        P.op("sp", lambda e: e.dma_start(out=gts[:], in_=gateT[:, q0:q0 + 512]), writes=[gk])
        for gi in range(12):
            ps, pk = ps_c_next()
            P.op("pe", lambda e, ps=ps, gi=gi: e.matmul(ps[:], onehot[:, gi, :], gts[:], start=True, stop=True), reads=["onehot_s", gk], writes=[pk])
            P.op("act", lambda e, ps=ps, gi=gi: e.copy(out=gbc[par][:, gi, :], in_=ps[:]), reads=[pk], writes=[f"gbc{par}"])
        for qt in range(4):
            it = qb * 4 + qt
            c0 = 248 - 8 * it
            for pair in ((0, 1), (2, 3)):
                bank = {h: (pss[PS_C[h % 2]], f"ps{PS_C[h % 2]}") for h in pair}
                for h in pair:
                    ps, pk = bank[h]
                    P.op("pe", lambda e, ps=ps, h=h, qt=qt: e.matmul(ps[:, 0:NC], qs[:, h, qt * 128:(qt + 1) * 128], kcmpT[:], start=True, stop=True),
                         reads=[qk, "kcmpT"], writes=[pk])
                for h in pair:
                    ps, pk = bank[h]
                    i2 = h % 2
                    P.op("dve", lambda e, ps=ps, c0=c0, i2=i2: e.tensor_tensor(out=sc[i2][:], in0=ps[:, 0:NC], in1=cmpw[:, c0:c0 + NC], op=ALU.add),
                         reads=[pk, "cmpw_s"], writes=[f"sc{i2}"])
                for h in pair:
                    i2 = h % 2
                    P.op("act", lambda e, i2=i2: e.activation(out=pe_[i2][:], in_=sc[i2][:], func=AF.Exp, accum_out=rs[i2][:]), reads=[f"sc{i2}"], writes=[f"pe_{i2}", f"rs{i2}"])
                for h in pair:
                    i2 = h % 2
                    P.op("dve", lambda e, i2=i2: e.tensor_scalar(out=rs[i2][:], in0=rs[i2][:], scalar1=1e-30, scalar2=None, op0=ALU.max), writes=[f"rs{i2}"])
                for h in pair:
                    i2 = h % 2
                    P.op("dve", lambda e, i2=i2: e.reciprocal(out=rs[i2][:], in_=rs[i2][:]), writes=[f"rs{i2}"])
                for h in pair:
                    i2 = h % 2
                    P.op("dve", lambda e, i2=i2: e.tensor_scalar(out=pn[i2][:], in0=pe_[i2][:], scalar1=rs[i2][:], scalar2=None, op0=ALU.mult),
                         reads=[f"pe_{i2}", f"rs{i2}"], writes=[f"pn{i2}"])
                for h in pair:
                    i2 = h % 2
                    if h == 0:
                        P.op("pool", lambda e, i2=i2: e.tensor_copy(out=ppad[:, 1:1 + NC], in_=pn[i2][:]), reads=[f"pn{i2}"], writes=["ppad"])
                    else:
                        P.op("pool", lambda e, i2=i2: e.tensor_tensor(out=ppad[:, 1:1 + NC], in0=ppad[:, 1:1 + NC], in1=pn[i2][:], op=ALU.add),
                             reads=[f"pn{i2}"], writes=["ppad"])
                for h in pair:
                    ps, pk = bank[h]
                    i2 = h % 2
                    P.op("pe", lambda e, ps=ps, i2=i2: e.transpose(ps[:, 0:128], pn[i2][:, 0:128], identf[:]), reads=[f"pn{i2}", "identf_s"], writes=[pk])
                    P.op("pe", lambda e, ps=ps, i2=i2: e.transpose(ps[0:NC1, 128:256], pn[i2][:, 128:NC], identf[:]), reads=[f"pn{i2}", "identf_s"], writes=[pk])
                for h in pair:
                    ps, pk = bank[h]
                    i2 = h % 2
                    P.op("act", lambda e, ps=ps, i2=i2: e.copy(out=pnT[i2][:, 0, :], in_=ps[:, 0:128]), reads=[pk], writes=[f"pnT{i2}"])
                    P.op("act", lambda e, ps=ps, i2=i2: e.copy(out=pnT[i2][0:NC1, 1, :], in_=ps[0:NC1, 128:256]), reads=[pk], writes=[f"pnT{i2}"])
                for h in pair:
                    ps, pk = bank[h]
                    i2 = h % 2
                    P.op("pe", lambda e, ps=ps, i2=i2: e.matmul(ps[:, 256:384], vcmp[:, 0, :], pnT[i2][:, 0, :], start=True, stop=False),
                         reads=["vcmp", f"pnT{i2}"], writes=[pk])
                    P.op("pe", lambda e, ps=ps, i2=i2: e.matmul(ps[:, 256:384], vcmp[0:NC1, 1, :], pnT[i2][0:NC1, 1, :], start=False, stop=True),
                         reads=["vcmp", f"pnT{i2}"], writes=[pk])
                for h in pair:
                    ps, pk = bank[h]
                    P.op("act", lambda e, ps=ps, h=h, qt=qt: e.copy(out=ocmp[par][:, h, qt * 128:(qt + 1) * 128], in_=ps[:, 256:384]),
                         reads=[pk], writes=[f"ocmp{par}"])
            P.op("dve", lambda e: e.tensor_tensor(out=imp[:], in0=ppad[:, 0:256:4], in1=ppad[:, 1:257:4], op=ALU.add), reads=["ppad"], writes=["imp"])
            for m_ in (2, 3, 4):
                P.op("dve", lambda e, m_=m_: e.tensor_tensor(out=imp[:], in0=imp[:], in1=ppad[:, m_:m_ + 256:4], op=ALU.add), reads=["ppad"], writes=["imp"])
            w0 = 62 - 2 * it
            P.op("dve", lambda e, w0=w0: e.tensor_tensor(out=imp2[:], in0=imp[:], in1=keepw[:, w0:w0 + NSB], op=ALU.mult), reads=["imp", "keepw_s"], writes=["imp2"])
            P.op("dve", lambda e, w0=w0: e.tensor_tensor(out=imp2[:], in0=imp2[:], in1=addw[:, w0:w0 + NSB], op=ALU.add), reads=["addw_s"], writes=["imp2"])
            P.op("dve", lambda e: e.memset(imp2[:, 0:1], 1e6), writes=["imp2"])
            P.op("dve", lambda e: e.max(out=v8a[:], in_=imp2[:]), reads=["imp2"], writes=["v8a"])
            P.op("dve", lambda e: e.match_replace(out=imp[:], in_to_replace=v8a[:], in_values=imp2[:], imm_value=-3.0e38), reads=["imp2", "v8a"], writes=["imp"])
            P.op("dve", lambda e: e.max(out=v8b[:], in_=imp[:]), reads=["imp"], writes=["v8b"])
            P.op("dve", lambda e: e.tensor_scalar(out=nsel[:], in0=imp2[:], scalar1=v8b[:, 7:8], scalar2=None, op0=ALU.is_lt), reads=["imp2", "v8b"], writes=["nsel"])
            ps4, pk4 = ps_c_next()
            P.op("pe", lambda e, ps4=ps4: e.transpose(ps4[0:NSB, 0:128], nsel[:], identf[:]), reads=["nsel", "identf_s"], writes=[pk4])
            P.op("act", lambda e, ps4=ps4, qt=qt: e.copy(out=nsT[par][:, qt * 128:(qt + 1) * 128], in_=ps4[0:NSB, 0:128]), reads=[pk4], writes=[f"nsT{par}"])

    def emit_att(qb):
        par = qb % 2
        q0 = qb * 512
        qs, qk = qb_s[par], f"qb{par}"
        szs, szk = sz_s[par], f"sz{par}"
        nst, nsk = nsT[par], f"nsT{par}"
        jobs = []
        grp = 0
        for h in range(4):
            for br in (1, 2):
                kts = list(range(0, 4 * (qb + 1))) if br == 1 else list(range(max(0, 4 * qb - 4), 4 * qb + 4))
                if br == 2 and qb >= 1:
                    kts = [4 * qb - 1] + [k for k in kts if k != 4 * qb - 1]
                for ki, kt in enumerate(kts):
                    jobs.append((h, br, kt, ki, len(kts), grp))
                grp += 1
        state = {}

        def emit_S(job):
            h, br, kt, ki, nk, g = job
            ps, pk = ps_s_next()
            rel = kt - 4 * qb
            c0 = max(0, 128 * rel) if rel >= 0 else 0
            c1 = min(512, 128 * rel + 640) if br == 2 else 512
            if br == 1:
                diag = rel >= 0
                P.op("pe", lambda e: e.matmul(ps[:, c0:c1], ks_s[:, kt * 128:(kt + 1) * 128], qs[:, h, c0:c1], start=True, stop=False), reads=["ks_s", qk], writes=[pk])
                P.op("pe", lambda e: e.matmul(ps[:, c0:c1], ekt[:, kt, :], nst[:, c0:c1], start=False, stop=(not diag)), reads=["ekt_s", nsk], writes=[pk])
                if diag:
                    P.op("pe", lambda e: e.matmul(ps[:, c0:c1], identb[:], wmask[:, rel + 4, c0:c1], start=False, stop=True), reads=["identb_s", "wmask_s"], writes=[pk])
            else:
                P.op("pe", lambda e: e.matmul(ps[:, c0:c1], kw_s[:, kt * 128:(kt + 1) * 128], qs[:, h, c0:c1], start=True, stop=False), reads=["kw_s", qk], writes=[pk])
                P.op("pe", lambda e: e.matmul(ps[:, c0:c1], identb[:], wmask[:, rel + 4, c0:c1], start=False, stop=True), reads=["identb_s", "wmask_s"], writes=[pk])
            state[job] = (ps, pk, c0, c1)

        def emit_PV(job):
            h, br, kt, ki, nk, g = job
            ps, pk, c0, c1 = state.pop(job)
            pt_, ptk = pT[pti[0] % 3], f"pT{pti[0] % 3}"
            pti[0] += 1
            v_s, vname = (vs_s, "vs_s") if br == 1 else (vw_s, "vw_s")
            pso, pko = pss[PS_O[g % 2]], f"ps{PS_O[g % 2]}"
            pss_, pks = pss[PS_S[g % 2]], f"ps{PS_S[g % 2]}"
            P.op("act", lambda e: e.activation(out=pt_[:, c0:c1], in_=ps[:, c0:c1], func=AF.Exp), reads=[pk], writes=[ptk])
            P.op("pe", lambda e: e.matmul(pso[:, c0:c1], v_s[:, kt, :], pt_[:, c0:c1], start=(ki == 0), stop=(ki == nk - 1)), reads=[vname, ptk], writes=[pko])
            P.op("pe", lambda e: e.matmul(pss_[:, c0:c1], onesb[:], pt_[:, c0:c1], start=(ki == 0), stop=(ki == nk - 1)), reads=["onesb_s", ptk], writes=[pks])
            if ki == nk - 1:
                gi = br * 4 + h
                P.op("act", lambda e: e.copy(out=osb[:], in_=pso[:]), reads=[pko], writes=["osb"])
                P.op("dve", lambda e: e.reciprocal(out=rden[:], in_=pss_[:]), reads=[pks], writes=["rden"])
                P.op("dve", lambda e: e.tensor_tensor(out=wgt[:], in0=rden[:], in1=gbc[par][:, gi, :], op=ALU.mult), reads=["rden", f"gbc{par}"], writes=["wgt"])
                P.op("dve", lambda e: e.tensor_tensor(out=tt1[:], in0=osb[:], in1=wgt[:], op=ALU.mult), reads=["osb", "wgt"], writes=["tt1"])
                if br == 1:
                    P.op("dve", lambda e: e.tensor_tensor(out=acc[:, h, :], in0=tt1[:], in1=szs[:, gi, :], op=ALU.mult), reads=["tt1", szk], writes=["acc"])
                else:
                    P.op("dve", lambda e: e.tensor_tensor(out=tt1[:], in0=tt1[:], in1=szs[:, gi, :], op=ALU.mult), reads=[szk], writes=["tt1"])
                    P.op("pool", lambda e: e.tensor_tensor(out=acc[:, h, :], in0=acc[:, h, :], in1=tt1[:], op=ALU.add), reads=["tt1"], writes=["acc"])
                    ybo, ybk = yb[par], f"ybo{par}"
                    P.op("dve", lambda e: e.tensor_tensor(out=tt1[:], in0=ocmp[par][:, h, :], in1=gbc[par][:, h, :], op=ALU.mult), reads=[f"ocmp{par}", f"gbc{par}"], writes=["tt1"])
                    P.op("dve", lambda e: e.tensor_tensor(out=tt1[:], in0=tt1[:], in1=szs[:, h, :], op=ALU.mult), reads=[szk], writes=["tt1"])
                    P.op("pool", lambda e: e.tensor_tensor(out=ybo[:, h, :], in0=acc[:, h, :], in1=tt1[:], op=ALU.add), reads=["tt1", "acc"], writes=[ybk])

        emit_S(jobs[0])
        for i, job in enumerate(jobs):
            if i + 1 < len(jobs):
                emit_S(jobs[i + 1])
            emit_PV(job)
        ybo, ybk = yb[par], f"ybo{par}"
        P.op("sp", lambda e: e.dma_start(out=yT[:, :, q0:q0 + 512], in_=ybo[:]), reads=[ybk])

    emit_cmp(0)
    for qb in range(NQB):
        P.capture()
        emit_att(qb)
        sa = P.end_capture()
        sb_ = []
        if qb + 1 < NQB:
            P.capture()
            emit_cmp(qb + 1)
            sb_ = P.end_capture()
        P.replay(sa, sb_)
    return kb.finish()


def _c(a):
    return np.ascontiguousarray(a)


def scan_maps(inp, layer, uT):
    maps = []
    ident = np.eye(128, dtype=np.float32)
    for c in range(NCORES):
        g0 = 8 * c

        def pl(a):
            return _c(a.reshape(4, 2, 64).transpose(1, 2, 0).reshape(128, 4))

        lr = pl(inp["a_lam_re"][layer][g0:g0 + 8])
        li = pl(inp["a_lam_im"][layer][g0:g0 + 8])
        ldt = pl(np.repeat(inp["a_log_dt"][layer][g0:g0 + 8][:, None], 64, axis=1))
        br = _c(inp["a_b_re"][layer][g0:g0 + 8].reshape(4, 2, 64, 16).transpose(1, 2, 0, 3).reshape(128, 4, 16))
        bi = _c(inp["a_b_im"][layer][g0:g0 + 8].reshape(4, 2, 64, 16).transpose(1, 2, 0, 3).reshape(128, 4, 16))
        crT = _c(inp["a_c_re"][layer][g0:g0 + 8].reshape(4, 2, 16, 64).transpose(1, 3, 0, 2).reshape(128, 4, 16))
        ciT = _c(inp["a_c_im"][layer][g0:g0 + 8].reshape(4, 2, 16, 64).transpose(1, 3, 0, 2).reshape(128, 4, 16))
        dv = _c(inp["a_d"][layer][128 * c:128 * c + 128].reshape(128, 1))
        maps.append(dict(uT=_c(uT[128 * c:128 * c + 128]), lr=lr, li=li, ldt=ldt, br=br, bi=bi, crT=crT, ciT=ciT,
                         dvec=dv, ident=ident))
    return maps


def attn_maps(o16, o32, kvT, inp, consts):
    maps = []
    for c in range(NCORES):
        b, G = c // 4, c % 4
        ts = slice(b * SEQ, (b + 1) * SEQ)
        m = dict(consts)
        m["qT"] = _c(o16[512 * G:512 * G + 512, ts].reshape(4, 128, SEQ).transpose(1, 0, 2))
        szr = np.stack([o16[2048 + br * 2048 + 512 * G:2048 + br * 2048 + 512 * G + 512, ts].reshape(4, 128, SEQ)
                        for br in range(3)], 0)
        m["szT"] = _c(szr.transpose(2, 0, 1, 3).reshape(128, 12, SEQ))
        m["gateT"] = _c(np.stack([o32[br * 16 + 4 * G:br * 16 + 4 * G + 4, ts] for br in range(3)], 0).reshape(12, SEQ))

        def slot(s_):
            return kvT[s_ * 512 + G * 128:s_ * 512 + G * 128 + 128, ts]

        m["kcT"] = _c(slot(0))
        m["vcT"] = _c(slot(1))
        m["ksT"] = _c(slot(2))
        m["kwT"] = _c(slot(4))
        m["vs"] = _c(slot(3).T.reshape(SEQ // 128, 128, 128).transpose(1, 0, 2))
        m["vw"] = _c(slot(5).T.reshape(SEQ // 128, 128, 128).transpose(1, 0, 2))
        m["poskT"] = _c(inp["cmp_pos_k"].T)
        m["posvT"] = _c(inp["cmp_pos_v"].T)
        m["w1k"] = _c(inp["cmp_w1_k"].reshape(32, 128, 128).transpose(1, 0, 2))
        m["w1v"] = _c(inp["cmp_w1_v"].reshape(32, 128, 128).transpose(1, 0, 2))
        m["w2k"] = _c(inp["cmp_w2_k"])
        m["w2v"] = _c(inp["cmp_w2_v"])
        maps.append(m)
    return maps


def kernel(**inputs):
    inp = {k: np.asarray(v) for k, v in inputs.items()}
    T = BATCH * SEQ // NCORES
    x = _c(inp["x"].reshape(BATCH * SEQ, D_MODEL).astype(np.float32))

    def tok(a, c):
        return a[c * T:(c + 1) * T]

    def bc(v):
        return _c(np.broadcast_to(v, (128, D_MODEL)))

    tiles_a = [(128, "id", "o32")] * 8 + [(128, "silu", "o32")] * 8
    tiles_kv = [(128, "id", "o16")] * 24
    tiles_b = [(128, "scale", "o16")] * 16 + [(128, "silu", "o16")] * 48 + [(48, "sigmoid", "o32")]
    nc_proj_a = build_proj(16, T, tiles_a)
    nc_scan = build_scan()
    nc_tail_a = build_tail(8, T, True)
    for layer in range(2):
        wa = proj_host_w(inp["a_w_in"][layer], tiles_a)
        res = run(nc_proj_a, [dict(xT=host_fm(tok(x, c).T), w=wa) for c in range(NCORES)])
        o = np.concatenate([res[c]["o32"] for c in range(NCORES)], axis=1)
        uT, szT = o[:1024], o[1024:]
        res = run(nc_scan, scan_maps(inp, layer, uT))
        gT = np.concatenate([res[c]["gT"] for c in range(NCORES)], axis=0)
        bglu = _c(inp["a_b_glu"][layer].reshape(8, 128).T)
        res = run(nc_tail_a, [dict(x=_c(tok(x, c)), w=host_fm(inp["a_w_out"][layer]), lng=bc(inp["ln_g"][layer]), lnb=bc(inp["ln_b"][layer]),
                                   gT=host_fm(gT[:, c * T:(c + 1) * T]), szT=host_fm(szT[:, c * T:(c + 1) * T]),
                                   wglu=host_fm(inp["a_w_glu"][layer]), bglu=bglu) for c in range(NCORES)])
        x = np.concatenate([res[c]["xo"] for c in range(NCORES)], axis=0)
    tiles_b_kv = tiles_b + tiles_kv
    nc_proj_b_kv = build_proj(16, T, tiles_b_kv)
    nc_proj_b = build_proj(16, T, tiles_b)
    nc_attn = build_attn()
    nc_tail_b = build_tail(16, T, False, BF16)
    consts = attn_consts()
    for li in range(2):
        layer = 2 + li
        if li == 0:
            wbi = proj_host_w(np.concatenate([inp["b_w_in"][li], inp["kv_w"]], axis=1), tiles_b_kv)
            res = run(nc_proj_b_kv, [dict(xT=host_fm(tok(x, c).T), w=wbi) for c in range(NCORES)])
            o16 = np.concatenate([res[c]["o16"] for c in range(NCORES)], axis=1)
            kvT = o16[8192:]
            o16 = o16[:8192]
        else:
            wbi = proj_host_w(inp["b_w_in"][li], tiles_b)
            res = run(nc_proj_b, [dict(xT=host_fm(tok(x, c).T), w=wbi) for c in range(NCORES)])
            o16 = np.concatenate([res[c]["o16"] for c in range(NCORES)], axis=1)
        o32 = np.concatenate([res[c]["o32"] for c in range(NCORES)], axis=1)
        res = run(nc_attn, attn_maps(o16, o32, kvT, inp, consts))
        yT = np.concatenate([np.concatenate([res[b * 4 + G]["yT"].transpose(1, 0, 2).reshape(512, SEQ) for G in range(4)], axis=0)
                             for b in range(BATCH)], axis=1)
        res = run(nc_tail_b, [dict(x=_c(tok(x, c)), w=host_fm(inp["b_w_out"][li]), lng=bc(inp["ln_g"][layer]), lnb=bc(inp["ln_b"][layer]),
                                   yT=host_fm(yT[:, c * T:(c + 1) * T])) for c in range(NCORES)])
        x = np.concatenate([res[c]["xo"] for c in range(NCORES)], axis=0)
    return x.reshape(BATCH, SEQ, D_MODEL).astype(np.float32)
```
